# Optimizing a Trainium2 kernel written in Bass

```python
import math
import jax
import jax.numpy as jnp
from jax import lax
import numpy as np

D_MODEL = 1024
BATCH = 32
SEQ = 2048
DEPTH = 2

GRID_W = 64
CTX_LEN = 256
N_EVEN = (DEPTH + 1) // 2
N_ODD = DEPTH // 2
ADA_CHUNKS = 6
NORM_EPS = 1e-6
ROPE_THETA = 10000.0
ROPE_DIM = 64
ROPE_FREQS = ROPE_DIM // 4
Q_BLOCK = 128

A_HEADS = 4
A_DH = 64
A_DV = 2 * A_DH
LAMBDA_INIT_BASE = 0.8
LAMBDA_INIT_SPAN = 0.6
LAMBDA_INIT_RATE = 0.3
B_HEADS = 8
B_DK = 64
B_DV = 64
B_CHUNK = 32
C_HEADS = 4
C_DK = 128
C_DV = 128
C_CHUNK = 64
C_CONV = 5
D_HEADS = 8
D_KV_HEADS = 2
D_DH = 64
D_WINDOW = 128
P_HEADS = 8
P_NKEYS = 128
P_NEXPERTS = P_NKEYS * P_NKEYS
P_DQ = 256
P_TOPK = 16
P_TOKEN_BLOCK = 128

EVEN_SPLITS = (A_HEADS * 2 * A_DH, A_HEADS * 2 * A_DH, A_HEADS * A_DV,
               B_HEADS * B_DK, B_HEADS * B_DV, B_HEADS * B_DK, B_HEADS * B_DK, B_HEADS * B_DV)
EVEN_IN = sum(EVEN_SPLITS)
EVEN_OUT = A_HEADS * A_DV + B_HEADS * B_DV
C_QKV = C_HEADS * (2 * C_DK + C_DV)
ODD_SPLITS = (C_QKV, C_HEADS * C_DV, 4 * C_HEADS, D_HEADS * D_DH, 2 * D_KV_HEADS * D_DH)
ODD_IN = sum(ODD_SPLITS)
ODD_OUT = C_HEADS * C_DV + D_HEADS * D_DH

kernel_name = 'hybrid_diffusion_trunk_diffattn_hgrn2_gdn_swa_peer'


def split_cols(p, sizes):
    idx = [int(s) for s in np.cumsum(sizes)[:-1]]
    return jnp.split(p, idx, axis=-1)


def rmsnorm(x, w):
    xf = x.astype(jnp.float32)
    xf = xf * lax.rsqrt(jnp.mean(xf * xf, axis=-1, keepdims=True) + NORM_EPS)
    return (xf * w.astype(jnp.float32)).astype(x.dtype)


def l2norm(x):
    xf = x.astype(jnp.float32)
    return xf * lax.rsqrt(jnp.sum(xf * xf, axis=-1, keepdims=True) + NORM_EPS)


def modulate(x, w, shift, scale):
    return rmsnorm(x, w) * (1.0 + scale) + shift


def ada_split(cvec, w, b):
    return jnp.split(jax.nn.silu(cvec) @ w + b, ADA_CHUNKS, axis=-1)


def axial_rope_tables(rows):
    row = jnp.repeat(jnp.arange(rows, dtype=jnp.float32), GRID_W)
    col = jnp.tile(jnp.arange(GRID_W, dtype=jnp.float32), rows)
    inv = ROPE_THETA ** (-jnp.arange(ROPE_FREQS, dtype=jnp.float32) / ROPE_FREQS)
    ar = row[:, None] * inv[None, :]
    ac = col[:, None] * inv[None, :]
    ang = jnp.concatenate([ar, ar, ac, ac], axis=-1)
    return jnp.cos(ang), jnp.sin(ang)


def apply_rope(x, cos, sin):
    x1, x2, x3, x4 = jnp.split(x, 4, axis=-1)
    rot = jnp.concatenate([-x2, x1, -x4, x3], axis=-1)
    return (x * cos[:, None, :] + rot * sin[:, None, :]).astype(x.dtype)


def centred_depthwise_conv(x, w):
    pad = C_CONV // 2
    return lax.conv_general_dilated(x, w[:, None, :].astype(x.dtype), window_strides=(1,),
                                    padding=[(pad, pad)], dimension_numbers=('NWC', 'WIO', 'NWC'),
                                    feature_group_count=x.shape[-1])


def to_chunks(a, chunk):
    B, n = a.shape[:2]
    a = a.reshape((B, n // chunk, chunk) + a.shape[2:])
    return jnp.moveaxis(jnp.moveaxis(a, 1, 0), 2, 3)


def from_chunks(o, n):
    o = jnp.moveaxis(jnp.moveaxis(o, 3, 2), 0, 1)
    return o.reshape((o.shape[0], n) + o.shape[3:])


def gla_chunked(q, k, v, log_f, s0):
    n = q.shape[1]
    lower = jnp.tril(jnp.ones((B_CHUNK, B_CHUNK), dtype=bool))
    mid = B_CHUNK // 2

    def step(S, inp):
        qi, ki, vi, gi = inp
        b = jnp.cumsum(gi, axis=-2)
        ref = b[..., mid:mid + 1, :]
        att = jnp.einsum('bhqd,bhkd->bhqk', qi * jnp.exp(b - ref), ki * jnp.exp(ref - b))
        att = jnp.where(lower, att, 0.0)
        o = (jnp.einsum('bhqk,bhke->bhqe', att, vi)
             + jnp.einsum('bhqd,bhde->bhqe', qi * jnp.exp(b), S))
        b_last = b[..., -1:, :]
        S = (S * jnp.exp(b_last)[..., 0, :, None]
             + jnp.einsum('bhkd,bhke->bhde', ki * jnp.exp(b_last - b), vi))
        return S, o

    S, o = lax.scan(step, s0, tuple(to_chunks(a, B_CHUNK) for a in (q, k, v, log_f)))
    return from_chunks(o, n), S


def gated_delta_chunked(q, k, v, log_alpha, beta, s0):
    n = q.shape[1]
    dv = v.shape[-1]
    incl = jnp.tril(jnp.ones((C_CHUNK, C_CHUNK), dtype=bool))
    strict = jnp.tril(jnp.ones((C_CHUNK, C_CHUNK), dtype=bool), k=-1)
    eye = jnp.eye(C_CHUNK, dtype=jnp.float32)

    def step(S, inp):
        qi, ki, vi, gi, bi = inp
        g = jnp.cumsum(gi, axis=-1)
        decay = jnp.exp(jnp.where(incl, g[..., :, None] - g[..., None, :], -jnp.inf))
        kb = ki * bi[..., None]
        a_mat = jnp.where(strict, jnp.einsum('bhid,bhjd->bhij', kb, ki) * decay, 0.0) + eye
        rhs = jnp.concatenate([vi * bi[..., None], kb * jnp.exp(g)[..., None]], axis=-1)
        sol = lax.linalg.triangular_solve(a_mat, rhs, left_side=True, lower=True, unit_diagonal=True)
        u, w = sol[..., :dv], sol[..., dv:]
        v_new = u - jnp.einsum('bhcd,bhde->bhce', w, S)
        att = jnp.einsum('bhid,bhjd->bhij', qi, ki) * decay
        o = (jnp.einsum('bhcd,bhde->bhce', qi * jnp.exp(g)[..., None], S)
             + jnp.einsum('bhij,bhje->bhie', att, v_new))
        g_last = g[..., -1:]
        S = (S * jnp.exp(g_last)[..., None]
             + jnp.einsum('bhcd,bhce->bhde', ki * jnp.exp(g_last - g)[..., None], v_new))
        return S, o

    S, o = lax.scan(step, s0, tuple(to_chunks(a, C_CHUNK) for a in (q, k, v, log_alpha, beta)))
    return from_chunks(o, n), S


def bidirectional_scan(scan_fn, lat_fwd, lat_bwd, ctx_fwd, ctx_bwd, s0):
    o_lat, o_ctx = [], []
    for lat, ctxs, rev in ((lat_fwd, ctx_fwd, False), (lat_bwd, ctx_bwd, True)):
        if rev:
            lat = tuple(jnp.flip(a, axis=1) for a in lat)
            ctxs = tuple(jnp.flip(a, axis=1) for a in ctxs)
        oc, s_ctx = scan_fn(*ctxs, s0)
        ol, _ = scan_fn(*lat, s_ctx)
        if rev:
            oc, ol = jnp.flip(oc, axis=1), jnp.flip(ol, axis=1)
        o_lat.append(ol)
        o_ctx.append(oc)
    return o_lat[0] + o_lat[1], o_ctx[0] + o_ctx[1]


def diff_attn_probs(q, k, lam):
    s = jnp.einsum('bqhmd,bkhmd->bhmqk', q, k).astype(jnp.float32) * (A_DH ** -0.5)
    p = jax.nn.softmax(s, axis=-1)
    return p[:, :, 0] - lam * p[:, :, 1]


def diff_attention_latent(q, k, v, kc, vc, lam):
    B, T, H = q.shape[:3]
    k_all = jnp.concatenate([kc, k], axis=1)
    v_all = jnp.concatenate([vc, v], axis=1)
    nb = T // Q_BLOCK
    qb = jnp.moveaxis(q.reshape(B, nb, Q_BLOCK, H, 2, A_DH), 1, 0)

    def block(qi):
        a = diff_attn_probs(qi, k_all, lam)
        return jnp.einsum('bhqk,bkhe->bqhe', a.astype(v_all.dtype), v_all)

    o = lax.map(block, qb)
    return jnp.moveaxis(o, 0, 1).reshape(B, T, H, A_DV)


def diff_attention_ctx(qc, kc, vc, lam):
    a = diff_attn_probs(qc, kc, lam)
    return jnp.einsum('bhqk,bkhe->bqhe', a.astype(vc.dtype), vc)


def window_attention_latent(q, k, v, kc, vc, sink):
    B, T, H, d = q.shape
    G = H // D_KV_HEADS
    L = kc.shape[1]
    nb = T // Q_BLOCK
    band = Q_BLOCK + 2 * D_WINDOW
    pad = ((0, 0), (D_WINDOW, D_WINDOW), (0, 0), (0, 0))
    kp, vp = jnp.pad(k, pad), jnp.pad(v, pad)
    qb = jnp.moveaxis(q.reshape(B, nb, Q_BLOCK, D_KV_HEADS, G, d), 1, 0)
    rel = jnp.arange(band)[None, :] - D_WINDOW - jnp.arange(Q_BLOCK)[:, None]
    in_window = jnp.abs(rel) <= D_WINDOW
    sink_l = sink.astype(jnp.float32).reshape(1, D_KV_HEADS, G, 1, 1)
    scale = d ** -0.5

    def block(args):
        i, qi = args
        start = i * Q_BLOCK
        kb = lax.dynamic_slice_in_dim(kp, start, band, axis=1)
        vb = lax.dynamic_slice_in_dim(vp, start, band, axis=1)
        key_pos = start - D_WINDOW + jnp.arange(band)
        mask = in_window & ((key_pos >= 0) & (key_pos < T))[None, :]
        s_band = jnp.einsum('bqkgd,bnkd->bkgqn', qi, kb).astype(jnp.float32) * scale
        s_band = jnp.where(mask, s_band, -jnp.inf)
        s_ctx = jnp.einsum('bqkgd,bnkd->bkgqn', qi, kc).astype(jnp.float32) * scale
        logits = jnp.concatenate([jnp.broadcast_to(sink_l, s_ctx.shape[:-1] + (1,)), s_ctx, s_band], axis=-1)
        p = jax.nn.softmax(logits, axis=-1).astype(v.dtype)
        return (jnp.einsum('bkgqn,bnkd->bqkgd', p[..., 1:1 + L], vc)
                + jnp.einsum('bkgqn,bnkd->bqkgd', p[..., 1 + L:], vb))

    o = lax.map(block, (jnp.arange(nb), qb))
    return jnp.moveaxis(o, 0, 1).reshape(B, T, H, d)


def window_attention_ctx(qc, kc, vc, sink):
    B, L, H, d = qc.shape
    G = H // D_KV_HEADS
    s = jnp.einsum('bqkgd,bnkd->bkgqn', qc.reshape(B, L, D_KV_HEADS, G, d), kc).astype(jnp.float32) * d ** -0.5
    sink_l = jnp.broadcast_to(sink.astype(jnp.float32).reshape(1, D_KV_HEADS, G, 1, 1), s.shape[:-1] + (1,))
    p = jax.nn.softmax(jnp.concatenate([sink_l, s], axis=-1), axis=-1)[..., 1:]
    return jnp.einsum('bkgqn,bnkd->bqkgd', p.astype(vc.dtype), vc).reshape(B, L, H, d)


def peer_ffn(h, w_q, keys, u_tab, v_tab):
    B, T, D = h.shape
    hb = h.reshape(B * T // P_TOKEN_BLOCK, P_TOKEN_BLOCK, D)

    def block(xb):
        q = (xb @ w_q).reshape(P_TOKEN_BLOCK, P_HEADS, 2, P_DQ // 2)
        s = jnp.einsum('nhpd,hpkd->nhpk', q, keys).astype(jnp.float32)
        s_top, i_top = lax.top_k(s, P_TOPK)
        cand = (s_top[:, :, 0, :, None] + s_top[:, :, 1, None, :]).reshape(P_TOKEN_BLOCK, P_HEADS, P_TOPK * P_TOPK)
        cand_idx = (i_top[:, :, 0, :, None] * P_NKEYS + i_top[:, :, 1, None, :]).reshape(P_TOKEN_BLOCK, P_HEADS, P_TOPK * P_TOPK)
        best, pos = lax.top_k(cand, P_TOPK)
        idx = jnp.take_along_axis(cand_idx, pos, axis=-1)
        gate = jax.nn.softmax(best, axis=-1)
        u = u_tab[idx]
        v = v_tab[idx]
        act = jax.nn.gelu(jnp.einsum('nd,nhkd->nhk', xb, u).astype(jnp.float32), approximate=False)
        return jnp.einsum('nhk,nhkd->nd', (gate * act).astype(v.dtype), v)

    return lax.map(block, hb).reshape(B, T, D)


def even_mixer(layer, h, hc, w_in, w_out, diff_lambda, subln_w, hgrn_lb, hgrn_norm_w, cos, sin, need_ctx):
    B, T, _ = h.shape
    L = hc.shape[1]
    q_a, k_a, v_a, q_b, i_b, ff_b, fb_b, g_b = split_cols(h @ w_in, EVEN_SPLITS)
    qc_a, kc_a, vc_a, qc_b, ic_b, ffc_b, fbc_b, gc_b = split_cols(hc @ w_in, EVEN_SPLITS)

    lambda_init = LAMBDA_INIT_BASE - LAMBDA_INIT_SPAN * math.exp(-LAMBDA_INIT_RATE * layer)
    lam_f = diff_lambda.astype(jnp.float32)
    lam = jnp.exp(jnp.sum(lam_f[0] * lam_f[1])) - jnp.exp(jnp.sum(lam_f[2] * lam_f[3])) + lambda_init

    def qk_latent(t):
        return apply_rope(t.reshape(B, T, 2 * A_HEADS, A_DH), cos, sin).reshape(B, T, A_HEADS, 2, A_DH)

    kc = kc_a.reshape(B, L, A_HEADS, 2, A_DH)
    vc = vc_a.reshape(B, L, A_HEADS, A_DV)
    o_a = diff_attention_latent(qk_latent(q_a), qk_latent(k_a), v_a.reshape(B, T, A_HEADS, A_DV), kc, vc, lam)
    o_a = (rmsnorm(o_a, subln_w) * (1.0 - lambda_init)).astype(h.dtype)

    lb = jnp.cumsum(jax.nn.softmax(hgrn_lb.astype(jnp.float32), axis=0), axis=0)[layer]

    def hgrn_dirs(q, i, ff, fb, n):
        q = (jax.nn.silu(q.astype(jnp.float32)) * B_DK ** -0.5).reshape(B, n, B_HEADS, B_DK)
        i = i.astype(jnp.float32).reshape(B, n, B_HEADS, B_DV)
        out = []
        for f_logit in (ff, fb):
            f = (lb + (1.0 - lb) * jax.nn.sigmoid(f_logit.astype(jnp.float32))).reshape(B, n, B_HEADS, B_DK)
            out.append((q, 1.0 - f, i, jnp.log(f)))
        return out

    lat_f, lat_b = hgrn_dirs(q_b, i_b, ff_b, fb_b, T)
    ctx_f, ctx_b = hgrn_dirs(qc_b, ic_b, ffc_b, fbc_b, L)
    s0 = jnp.zeros((B, B_HEADS, B_DK, B_DV), jnp.float32)
    o_b, oc_b = bidirectional_scan(gla_chunked, lat_f, lat_b, ctx_f, ctx_b, s0)

    def hgrn_out(o, g, n):
        gate = jax.nn.silu(g.astype(jnp.float32).reshape(B, n, B_HEADS, B_DV))
        return (rmsnorm(o, hgrn_norm_w) * gate).reshape(B, n, B_HEADS * B_DV)

    y = jnp.concatenate([o_a.reshape(B, T, A_HEADS * A_DV), hgrn_out(o_b, g_b, T).astype(h.dtype)], axis=-1) @ w_out
    if not need_ctx:
        return y, None
    oc_a = rmsnorm(diff_attention_ctx(qc_a.reshape(B, L, A_HEADS, 2, A_DH), kc, vc, lam), subln_w) * (1.0 - lambda_init)
    yc = jnp.concatenate([oc_a.reshape(B, L, A_HEADS * A_DV).astype(hc.dtype),
                          hgrn_out(oc_b, gc_b, L).astype(hc.dtype)], axis=-1) @ w_out
    return y, yc


def odd_mixer(h, hc, w_in, w_out, conv_w, a_log, dt_bias, gdn_norm_w, sink, cos, sin, need_ctx):
    B, T, _ = h.shape
    L = hc.shape[1]
    qkv, z, gates, q_d, kv_d = split_cols(h @ w_in, ODD_SPLITS)
    qkvc, zc, gatesc, qc_d, kvc_d = split_cols(hc @ w_in, ODD_SPLITS)

    a_log_f = a_log.astype(jnp.float32)
    dt_f = dt_bias.astype(jnp.float32)

    def gdn_dirs(qkv_t, gates_t, n):
        qkv_t = jax.nn.silu(centred_depthwise_conv(qkv_t, conv_w))
        q, k, v = split_cols(qkv_t, (C_HEADS * C_DK, C_HEADS * C_DK, C_HEADS * C_DV))
        q = l2norm(q.reshape(B, n, C_HEADS, C_DK)) * C_DK ** -0.5
        k = l2norm(k.reshape(B, n, C_HEADS, C_DK))
        v = v.astype(jnp.float32).reshape(B, n, C_HEADS, C_DV)
        a_f, a_b, b_f, b_b = jnp.split(gates_t.astype(jnp.float32), 4, axis=-1)
        out = []
        for d, (a_t, b_t) in enumerate(((a_f, b_f), (a_b, b_b))):
            log_alpha = -jnp.exp(a_log_f[d]) * jax.nn.softplus(a_t + dt_f[d])
            out.append((q, k, v, log_alpha, jax.nn.sigmoid(b_t)))
        return out

    lat_f, lat_b = gdn_dirs(qkv, gates, T)
    ctx_f, ctx_b = gdn_dirs(qkvc, gatesc, L)
    s0 = jnp.zeros((B, C_HEADS, C_DK, C_DV), jnp.float32)
    o_c, oc_c = bidirectional_scan(gated_delta_chunked, lat_f, lat_b, ctx_f, ctx_b, s0)

    def gdn_out(o, zt, n):
        gate = jax.nn.silu(zt.astype(jnp.float32).reshape(B, n, C_HEADS, C_DV))
        return (rmsnorm(o, gdn_norm_w) * gate).reshape(B, n, C_HEADS * C_DV)

    k_d, v_d = jnp.split(kv_d, 2, axis=-1)
    kc_d, vc_d = jnp.split(kvc_d, 2, axis=-1)
    q_lat = apply_rope(q_d.reshape(B, T, D_HEADS, D_DH), cos, sin)
    k_lat = apply_rope(k_d.reshape(B, T, D_KV_HEADS, D_DH), cos, sin)
    kc = kc_d.reshape(B, L, D_KV_HEADS, D_DH)
    vc = vc_d.reshape(B, L, D_KV_HEADS, D_DH)
    o_d = window_attention_latent(q_lat, k_lat, v_d.reshape(B, T, D_KV_HEADS, D_DH), kc, vc, sink)

    y = jnp.concatenate([gdn_out(o_c, z, T).astype(h.dtype), o_d.reshape(B, T, D_HEADS * D_DH)], axis=-1) @ w_out
    if not need_ctx:
        return y, None
    oc_d = window_attention_ctx(qc_d.reshape(B, L, D_HEADS, D_DH), kc, vc, sink)
    yc = jnp.concatenate([gdn_out(oc_c, zc, L).astype(hc.dtype), oc_d.reshape(B, L, D_HEADS * D_DH)], axis=-1) @ w_out
    return y, yc


def setup_inputs(seed: int = 0) -> dict:
    key = jax.random.key(seed)
    ks = jax.random.split(key, 26)
    f32 = jnp.float32

    def nrm(k, shape, scale):
        return jax.random.normal(k, shape, f32) * scale

    dt = jnp.exp(jax.random.uniform(ks[18], (N_ODD, 2, C_HEADS), f32, math.log(1e-3), math.log(1e-1)))
    return {
        'x': nrm(ks[0], (BATCH, SEQ, D_MODEL), 1.0),
        'c': nrm(ks[1], (BATCH, D_MODEL), 1.0),
        'ctx': nrm(ks[2], (BATCH, CTX_LEN, D_MODEL), 1.0),
        'c_ctx': nrm(ks[3], (D_MODEL,), 1.0),
        'ada_w': nrm(ks[4], (DEPTH, D_MODEL, ADA_CHUNKS * D_MODEL), 0.5 * D_MODEL ** -0.5),
        'ada_b': nrm(ks[5], (DEPTH, ADA_CHUNKS * D_MODEL), 0.02),
        'norm_w': 1.0 + nrm(ks[6], (DEPTH, 2, D_MODEL), 0.02),
        'final_norm_w': 1.0 + nrm(ks[7], (D_MODEL,), 0.02),
        'even_w_in': nrm(ks[8], (N_EVEN, D_MODEL, EVEN_IN), D_MODEL ** -0.5),
        'even_w_out': nrm(ks[9], (N_EVEN, EVEN_OUT, D_MODEL), EVEN_OUT ** -0.5),
        'diff_lambda': nrm(ks[10], (N_EVEN, 4, A_DH), 0.1),
        'diff_subln_w': 1.0 + nrm(ks[11], (N_EVEN, A_DV), 0.02),
        'hgrn_lb': nrm(ks[12], (DEPTH + 1, B_HEADS * B_DK), 0.1),
        'hgrn_norm_w': 1.0 + nrm(ks[13], (N_EVEN, B_DV), 0.02),
        'odd_w_in': nrm(ks[14], (N_ODD, D_MODEL, ODD_IN), D_MODEL ** -0.5),
        'odd_w_out': nrm(ks[15], (N_ODD, ODD_OUT, D_MODEL), ODD_OUT ** -0.5),
        'gdn_conv_w': nrm(ks[16], (N_ODD, C_CONV, C_QKV), C_CONV ** -0.5),
        'gdn_a_log': jnp.log(jax.random.uniform(ks[17], (N_ODD, 2, C_HEADS), f32, 1.0, 16.0)),
        'gdn_dt_bias': dt + jnp.log(-jnp.expm1(-dt)),
        'gdn_norm_w': 1.0 + nrm(ks[19], (N_ODD, C_DV), 0.02),
        'swa_sink': nrm(ks[20], (N_ODD, D_HEADS), 0.5),
        'peer_w_q': nrm(ks[21], (DEPTH, D_MODEL, P_HEADS * P_DQ), D_MODEL ** -0.5),
        'peer_keys': nrm(ks[22], (DEPTH, P_HEADS, 2, P_NKEYS, P_DQ // 2), (P_DQ // 2) ** -0.5),
        'peer_u': nrm(ks[23], (DEPTH, P_NEXPERTS, D_MODEL), D_MODEL ** -0.5),
        'peer_v': nrm(ks[24], (DEPTH, P_NEXPERTS, D_MODEL), 0.25),
    }


def reference(x, c, ctx, c_ctx, ada_w, ada_b, norm_w, final_norm_w, even_w_in, even_w_out, diff_lambda,
              diff_subln_w, hgrn_lb, hgrn_norm_w, odd_w_in, odd_w_out, gdn_conv_w, gdn_a_log, gdn_dt_bias,
              gdn_norm_w, swa_sink, peer_w_q, peer_keys, peer_u, peer_v):
    T = x.shape[1]
    ROWS = T // GRID_W
    cos, sin = axial_rope_tables(ROWS)
    xc = ctx
    for layer in range(DEPTH):
        need_ctx = layer < DEPTH - 1
        w_ada, b_ada = ada_w[layer], ada_b[layer]
        sh1, sc1, g1, sh2, sc2, g2 = [m[:, None, :] for m in ada_split(c, w_ada, b_ada)]
        sh1c, sc1c, g1c, sh2c, sc2c, g2c = ada_split(c_ctx, w_ada, b_ada)
        h = modulate(x, norm_w[layer, 0], sh1, sc1)
        hc = modulate(xc, norm_w[layer, 0], sh1c, sc1c)
        if layer % 2 == 0:
            e = layer // 2
            y, yc = even_mixer(layer, h, hc, even_w_in[e], even_w_out[e], diff_lambda[e], diff_subln_w[e],
                               hgrn_lb, hgrn_norm_w[e], cos, sin, need_ctx)
        else:
            o = layer // 2
            y, yc = odd_mixer(h, hc, odd_w_in[o], odd_w_out[o], gdn_conv_w[o], gdn_a_log[o], gdn_dt_bias[o],
                              gdn_norm_w[o], swa_sink[o], cos, sin, need_ctx)
        x = x + g1 * y
        x = x + g2 * peer_ffn(modulate(x, norm_w[layer, 1], sh2, sc2),
                              peer_w_q[layer], peer_keys[layer], peer_u[layer], peer_v[layer])
        if need_ctx:
            xc = xc + g1c * yc
            xc = xc + g2c * peer_ffn(modulate(xc, norm_w[layer, 1], sh2c, sc2c),
                                     peer_w_q[layer], peer_keys[layer], peer_u[layer], peer_v[layer])
    return rmsnorm(x, final_norm_w)
```

```python
import math
import numpy as np
import concourse.bass as bass
import concourse.mybir as mybir
from concourse.bass_utils import run_bass_kernel_spmd
from contextlib import ExitStack

F32 = mybir.dt.float32
BF16 = mybir.dt.bfloat16
I32 = mybir.dt.int32
U32 = mybir.dt.uint32
AF = mybir.ActivationFunctionType
ALU = mybir.AluOpType
AX = mybir.AxisListType

ENG = ['pe', 'dve', 'act', 'pool', 'sp']
CH = 30000
NEXP = 16384
D = 1024
NT = 2304
LCTX = 256
TLAT = 2048
EPS = 1e-6
NEG = -30000.0
NB = 4


class Buf:
    __slots__ = ('w', 'r', 'x')

    def __init__(self):
        self.w = None
        self.r = []
        self.x = False


class _Rec:
    def __init__(self):
        self.call = None

    def __getattr__(self, name):
        def f(*a, **k):
            self.call = (name, a, k)
            return self
        return f


def _record(fn):
    r = _Rec()
    fn(r)
    assert r.call is not None
    return r.call


class Prog:
    def __init__(self, nc, stack, n_dma_slots=32, n_chunks=8):
        self.nc = nc
        self.stack = stack
        self.ops = {e: [] for e in ENG}
        self.count = {e: 0 for e in ENG}
        self.sems = {e: [stack.enter_context(nc.semaphore(f"s_{e}_{i}")) for i in range(n_chunks)] for e in ENG}
        self.dsem = [stack.enter_context(nc.semaphore(f"d{i}")) for i in range(n_dma_slots)]
        self.dval = [0] * n_dma_slots
        self.dnext = 0
        self.waited = {e: {} for e in ENG}
        self.same_sync = {'pe': False, 'dve': True, 'act': True, 'pool': True, 'sp': False}
        self.uid = 0

    def sb(self, shape, dt, name=None):
        self.uid += 1
        return self.stack.enter_context(self.nc.sbuf_tensor(f"{name or 't'}_{self.uid}", shape, dt))

    def ps(self, shape, dt, name=None):
        self.uid += 1
        return self.stack.enter_context(self.nc.psum_tensor(f"{name or 'p'}_{self.uid}", shape, dt))

    def _deps(self, reads, writes, e=None):
        deps = []
        for b in reads:
            if b.w is not None:
                deps.append(b.w)
            if b.x:
                deps.extend(r for r in b.r if r[1] != e)
        for b in writes:
            if b.w is not None:
                deps.append(b.w)
            deps.extend(b.r)
        return deps

    def _emit_waits(self, e, deps):
        for d in deps:
            if d[0] == 'e':
                if d[1] == e and not self.same_sync[e]:
                    continue
                key = ('e', d[1])
                val = d[2]
                if self.waited[e].get(key, 0) >= val:
                    continue
                self.waited[e][key] = val
                k, v = (val - 1) // CH, (val - 1) % CH + 1
                self.ops[e].append(('wait', self.sems[d[1]][k], v))
            else:
                key = ('d', d[1])
                val = d[2]
                if self.waited[e].get(key, 0) >= val:
                    continue
                self.waited[e][key] = val
                self.ops[e].append(('wait', self.dsem[d[1]], val))

    def op(self, e, fn, reads=(), writes=()):
        self._emit_waits(e, self._deps(reads, writes, e))
        self.count[e] += 1
        n = self.count[e]
        k = (n - 1) // CH
        self.ops[e].append(('op', _record(fn), self.sems[e][k], 1))
        tok = ('e', e, n)
        for b in reads:
            b.r.append(tok)
        for b in writes:
            b.w = tok
            b.r = []

    def dma(self, q, fn, reads=(), writes=()):
        self._emit_waits(q, self._deps(reads, writes, q))
        s = self.dnext
        self.dnext = (self.dnext + 1) % len(self.dsem)
        prev = self.dval[s]
        if prev > 0:
            self._emit_waits(q, [('d', s, prev)])
        self.dval[s] = prev + 16
        self.ops[q].append(('op', _record(fn), self.dsem[s], 16))
        tok = ('d', s, prev + 16)
        for b in reads:
            b.r.append(tok)
        for b in writes:
            b.w = tok
            b.r = []

    def barrier(self):
        deps = [('e', e2, self.count[e2]) for e2 in ENG if self.count[e2] > 0]
        deps += [('d', s, v) for s, v in enumerate(self.dval) if v > 0]
        for e in ENG:
            saved = self.same_sync[e]
            self.same_sync[e] = True
            self._emit_waits(e, deps)
            self.same_sync[e] = saved

    def emit(self):
        nc = self.nc
        with nc.Block() as block:
            def run(e):
                def body(eng):
                    for item in self.ops[e]:
                        if item[0] == 'wait':
                            eng.wait_ge(item[1], item[2])
                        else:
                            name, a, k = item[1]
                            getattr(eng, name)(*a, **k).then_inc(item[2], item[3])
                return body
            block.tensor(run('pe'))
            block.vector(run('dve'))
            block.scalar(run('act'))
            block.gpsimd(run('pool'))
            block.sync(run('sp'))


class T:
    def __init__(self, t):
        self.t = t
        self.b = Buf()

    def __getitem__(self, k):
        return self.t[k]


def make_consts():
    c = {}
    c['ident'] = np.eye(128, dtype=np.float32)
    t = np.arange(TLAT)
    row = (t // 64).astype(np.float32)
    col = (t % 64).astype(np.float32)
    inv = (np.float32(10000.0) ** (-np.arange(16, dtype=np.float32) / np.float32(16))).astype(np.float32)
    ar = row[:, None] * inv[None, :]
    ac = col[:, None] * inv[None, :]
    ang = np.concatenate([ar, ar, ac, ac], axis=-1).astype(np.float32)
    cos = np.cos(ang).astype(np.float32).T
    sin = np.sin(ang).astype(np.float32).T
    sign = np.ones((64, 1), np.float32)
    sign[0:16] = -1.0
    sign[32:48] = -1.0
    ssin = sin * sign
    c['rope'] = np.ascontiguousarray(np.stack([np.concatenate([cos, cos], 0), np.concatenate([ssin, ssin], 0)], axis=1))
    s = np.arange(128)[:, None]
    u = np.arange(128)[None, :]
    cs, cu = s // 32, u // 32
    same = (cs == cu)
    hg = np.zeros((128, 2, 4, 128), np.float32)
    tri_f = same & (s <= u)
    tri_b = same & (s >= u)
    mid_f = same & (s <= 32 * cu + 16)
    mid_b = same & (s >= 32 * cu + 15)
    hg[:, 0, 0] = tri_f.astype(np.float32) - mid_f
    hg[:, 1, 0] = tri_b.astype(np.float32) - mid_b
    hg[:, 0, 1] = tri_f
    hg[:, 1, 1] = tri_b
    hg[:, 0, 2] = same.astype(np.float32) - tri_f
    hg[:, 1, 2] = same.astype(np.float32) - tri_b
    hg[:, 0, 3] = tri_f
    hg[:, 1, 3] = tri_b
    c['hgc'] = hg
    c['hcsel'] = (np.arange(128)[:, None] // 32 == np.arange(4)[None, :]).astype(np.float32)
    cs, cu = s // 64, u // 64
    same = (cs == cu)
    gd = np.zeros((128, 2, 4, 128), np.float32)
    t_f = same & (s <= u)
    t_b = same & (s >= u)
    gd[:, 0, 0] = t_f
    gd[:, 1, 0] = t_b
    gd[:, 0, 1] = same.astype(np.float32) - t_f
    gd[:, 1, 1] = same.astype(np.float32) - t_b
    gd[:, 0, 2] = np.where(same & (s > u), 0.0, NEG)
    gd[:, 1, 2] = np.where(same & (s < u), 0.0, NEG)
    gd[:, 0, 3] = np.where(same & (s <= u), 0.0, NEG)
    gd[:, 1, 3] = np.where(same & (s >= u), 0.0, NEG)
    c['gdc'] = gd
    c['gcsel'] = (np.arange(128)[:, None] // 64 == np.arange(2)[None, :]).astype(np.float32)
    sw = np.zeros((128, 2, 128), np.float32)
    sw[:, 0] = (u <= s)
    sw[:, 1] = (s <= u)
    c['swm'] = sw
    return c


CONST_SHAPES = {'ident': [128, 128], 'rope': [128, 2, TLAT], 'hgc': [128, 2, 4, 128], 'hcsel': [128, 4],
                'gdc': [128, 2, 4, 128], 'gcsel': [128, 2], 'swm': [128, 2, 128]}

IN_SHAPES = {
    'x': [NB, TLAT, D], 'ctx': [NB, LCTX, D], 'c5T': [128, 8, 5],
    'ada_w': [2, D, 6 * D], 'ada_b': [2, 6 * D], 'norm_w': [2, 2, D], 'final_norm_w': [D],
    'even_w_in': [1, D, 4096], 'even_w_out': [1, D, D], 'diff_lambda': [1, 4, 64], 'diff_subln_w': [1, 128],
    'hgrn_lb': [3, 512], 'hgrn_norm_w': [1, 64], 'odd_w_in': [1, D, 2832], 'odd_w_out': [1, D, D],
    'gdn_conv_w': [1, 5, 1536], 'gdn_a_log': [1, 2, 4], 'gdn_dt_bias': [1, 2, 4], 'gdn_norm_w': [1, 128],
    'swa_sink': [1, 8], 'peer_w_q': [2, D, 2048], 'peer_keys': [2, 8, 2, 128, 128],
    'peer_u': [2, NEXP, D], 'peer_v': [2, NEXP, D],
}


class Kern:
    def __init__(self, cfg):
        self.cfg = cfg
        nc = bass.Bass("TRN2", target_bir_lowering=False)
        self.nc = nc
        self.I = {}
        for k, shp in list(IN_SHAPES.items()) + list(CONST_SHAPES.items()):
            self.I[k] = nc.dram_tensor(k, shp, F32, kind="ExternalInput").ap()
        self.out = nc.dram_tensor("out", [NB, TLAT, D], F32, kind="ExternalOutput").ap()
        dbg = cfg.get('dbg', False)
        kind = "ExternalOutput" if dbg else "Internal"
        self.XS = nc.dram_tensor("XS", [NB, NT, D], F32, kind=kind).ap()
        self.MOD = nc.dram_tensor("MOD", [2, 5, 6 * D], F32, kind=kind).ap()
        self.OF = nc.dram_tensor("OF", [NT, 512], F32, kind="Internal").ap()
        self.b_XS = [[Buf() for _ in range(18)] for _ in range(NB)]
        self.b_MOD = Buf()
        self.b_OF = [Buf() for _ in range(18)]
        self.ZS = nc.dram_tensor("ZS", [NT, 512], F32, kind="Internal").ap()
        self.b_ZS = [Buf() for _ in range(18)]
        self.b_out = Buf()
        if dbg:
            self.DBG = nc.dram_tensor("DBG", [NT, D], F32, kind="ExternalOutput").ap()
            self.b_DBG = Buf()

    def tile(self, shape, dt, name=None):
        return T(self.P.sb(shape, dt, name))

    def load_bcast(self, dst, src1d, q='sp', reads=()):
        self.P.dma(q, lambda e: e.dma_start(out=dst.t[:], in_=src1d.partition_broadcast(128)), reads=list(reads), writes=[dst.b])

    def load_w(self, dst, src2d, c0, n, d0=0):
        self.P.dma('pool', lambda e: e.dma_start(out=dst.t[:, :, d0:d0 + n], in_=src2d[:, c0:c0 + n].rearrange("(k p) c -> p k c", p=128)),
                   writes=[dst.b])

    def rstd_from_ss(self, ss, n, out):
        P = self.P
        P.op('dve', lambda e: e.tensor_scalar(out=out.t[:], in0=ss.t[:], scalar1=1.0 / n, scalar2=EPS, op0=ALU.mult, op1=ALU.add),
             reads=[ss.b], writes=[out.b])
        P.op('act', lambda e: e.activation(out=out.t[:], in_=out.t[:], func=AF.Sqrt), reads=[out.b], writes=[out.b])
        P.op('dve', lambda e: e.reciprocal(out=out.t[:], in_=out.t[:]), reads=[out.b], writes=[out.b])

    def x_src(self, l, b, t):
        if l == 0:
            if t < 2:
                return self.I['ctx'][b, t * 128:(t + 1) * 128, :], None
            return self.I['x'][b, (t - 2) * 128:(t - 1) * 128, :], None
        return self.XS[b, t * 128:(t + 1) * 128, :], self.b_XS[b][t]

    def transpose_to(self, src_bf, dst_ap_fn, dst_buf, nblk=8):
        P = self.P
        for k in range(nblk):
            P.op('pe', lambda e, k=k: e.transpose(out=self.psb.t[:, k * 128:(k + 1) * 128], in_=src_bf.t[:, k * 128:(k + 1) * 128], identity=self.identb.t[:]),
                 reads=[src_bf.b, self.identb.b], writes=[self.psb.b])
        P.op('act', lambda e: e.activation(out=dst_ap_fn(), in_=self.psb.t[:, 0:nblk * 128].rearrange("p (a b) -> p a b", a=nblk), func=AF.Copy),
             reads=[self.psb.b], writes=[dst_buf])

    def build(self):
        nc = self.nc
        cfg = self.cfg
        with ExitStack() as st:
            P = Prog(nc, st)
            self.P = P
            self.ps = [T(P.ps([128, 512], F32)) for _ in range(7)]
            self.psb = T(P.ps([128, 1024], BF16))
            for p_ in self.ps + [self.psb]:
                p_.b.x = True
            self.setup()
            for l in cfg.get('layers', [0, 1]):
                for b in cfg.get('batches', list(range(NB))):
                    self.layer_batch(l, b)
            if cfg.get('final', True):
                self.final_norm()
            if cfg.get('dbg', False):
                pass
            P.barrier()
            P.emit()
            self.counts = {e: (P.count[e], len(P.ops[e])) for e in ENG}
        return nc

    def setup(self):
        P = self.P
        I = self.I
        self.ident = self.tile([128, 128], F32)
        self.identb = self.tile([128, 128], BF16)
        self.ones = self.tile([128, 128], F32)
        self.negones = self.tile([128, 128], F32)
        self.keysT = self.tile([128, 2, 16, 128], BF16)
        self.modbig = self.tile([128, 6 * D], F32, "modbig")
        self.mod = [T(self.modbig.t[:, i * D:(i + 1) * D]) for i in range(6)]
        self.xin = [self.tile([128, D], F32, "xin") for _ in range(2)]
        self.wf = [self.tile([128, D], F32, "wf") for _ in range(2)]
        self.wb = [self.tile([128, D], BF16, "wb") for _ in range(2)]
        self.ss = self.tile([128, 8], F32)
        self.rs = self.tile([128, 8], F32)
        self.xi = 0
        P.dma('sp', lambda e: e.dma_start(out=self.ident.t[:], in_=I['ident']), writes=[self.ident.b])
        P.op('dve', lambda e: e.tensor_copy(out=self.identb.t[:], in_=self.ident.t[:]), reads=[self.ident.b], writes=[self.identb.b])
        P.op('dve', lambda e: e.memset(self.ones.t[:], 1.0), writes=[self.ones.b])
        P.op('dve', lambda e: e.memset(self.negones.t[:], -1.0), writes=[self.negones.b])
        with ExitStack() as ph:
            P.stack, saved = ph, P.stack
            kf = self.tile([128, 16, 128], F32)
            for l in range(2):
                P.dma('sp', lambda e, l=l: e.dma_start(out=kf.t[:], in_=I['peer_keys'][l].rearrange("h t k d -> k (h t) d")), writes=[kf.b])
                for g in range(4):
                    pt = self.ps[g]
                    for j in range(4):
                        hp = g * 4 + j
                        P.op('pe', lambda e, pt=pt, j=j, hp=hp: e.transpose(out=pt.t[:, j * 128:(j + 1) * 128], in_=kf.t[:, hp, :], identity=self.ident.t[:]),
                             reads=[kf.b, self.ident.b], writes=[pt.b])
                    P.op('act', lambda e, pt=pt, g=g, l=l: e.activation(out=self.keysT.t[:, l, g * 4:(g + 1) * 4, :], in_=pt.t[:].rearrange("p (a b) -> p a b", a=4), func=AF.Copy),
                         reads=[pt.b], writes=[self.keysT.b])
            c5 = self.tile([128, 8, 5], F32)
            P.dma('sp', lambda e: e.dma_start(out=c5.t[:], in_=I['c5T']), writes=[c5.b])
            P.op('act', lambda e: e.activation(out=c5.t[:], in_=c5.t[:], func=AF.Silu), reads=[c5.b], writes=[c5.b])
            wst = [self.tile([128, 3072], F32, "adaw") for _ in range(2)]
            bias = self.tile([5, 3072], F32)
            msb = self.tile([5, 3072], F32)
            wi = 0
            for l in range(2):
                for half in range(2):
                    c0 = half * 3072
                    P.dma('sp', lambda e, l=l, c0=c0: e.dma_start(out=bias.t[:], in_=I['ada_b'][l, c0:c0 + 3072].partition_broadcast(5)), writes=[bias.b])
                    for k in range(8):
                        w = wst[wi % 2]
                        wi += 1
                        P.dma('sp', lambda e, w=w, l=l, k=k, c0=c0: e.dma_start(out=w.t[:], in_=I['ada_w'][l, k * 128:(k + 1) * 128, c0:c0 + 3072]), writes=[w.b])
                        for cg in range(6):
                            pt = self.ps[cg]
                            P.op('pe', lambda e, pt=pt, w=w, k=k, cg=cg: e.matmul(pt.t[0:5, :], lhsT=c5.t[:, k, :], rhs=w.t[:, cg * 512:(cg + 1) * 512],
                                                                                 start=(k == 0), stop=(k == 7)), reads=[c5.b, w.b], writes=[pt.b])
                    for cg in range(6):
                        pt = self.ps[cg]
                        P.op('dve', lambda e, pt=pt, cg=cg: e.tensor_tensor(out=msb.t[:, cg * 512:(cg + 1) * 512], in0=pt.t[0:5, :], in1=bias.t[:, cg * 512:(cg + 1) * 512], op=ALU.add),
                             reads=[pt.b, bias.b], writes=[msb.b])
                    P.dma('sp', lambda e, l=l, c0=c0: e.dma_start(out=self.MOD[l, :, c0:c0 + 3072], in_=msb.t[:]), reads=[msb.b], writes=[self.b_MOD])
            P.barrier()
            P.stack = saved

    def load_mod(self, l, r, which, dstA, dstS, tmp_sc, tmp_nw):
        P = self.P
        j_sh, j_sc = (0, 1) if which == 0 else (3, 4)
        self.load_bcast(dstS, self.MOD[l, r, j_sh * D:(j_sh + 1) * D], reads=[self.b_MOD])
        self.P.dma('sp', lambda e: e.dma_start(out=tmp_sc.t[:], in_=self.MOD[l, r, j_sc * D:(j_sc + 1) * D].partition_broadcast(128)),
                   reads=[self.b_MOD], writes=[tmp_sc.b])
        self.load_bcast(tmp_nw, self.I['norm_w'][l, which, :])
        P.op('dve', lambda e: e.scalar_tensor_tensor(out=dstA.t[:], in0=tmp_sc.t[:], scalar=1.0, in1=tmp_nw.t[:], op0=ALU.add, op1=ALU.mult),
             reads=[tmp_sc.b, tmp_nw.b], writes=[dstA.b])

    def load_gate(self, l, r, which, dst):
        j = 2 if which == 0 else 5
        self.P.dma('sp', lambda e: e.dma_start(out=dst.t[:], in_=self.MOD[l, r, j * D:(j + 1) * D].partition_broadcast(128)),
                   reads=[self.b_MOD], writes=[dst.b])

    def norm_mod(self, xt, A, S, out_f=None, out_b=None):
        P = self.P
        junk = self.wb[0]
        P.op('act', lambda e: e.activation(out=junk.t[:], in_=xt.t[:], func=AF.Square, accum_out=self.ss.t[:, 0:1]),
             reads=[xt.b], writes=[junk.b, self.ss.b])
        P.op('dve', lambda e: e.tensor_scalar(out=self.rs.t[:, 0:1], in0=self.ss.t[:, 0:1], scalar1=1.0 / D, scalar2=EPS, op0=ALU.mult, op1=ALU.add),
             reads=[self.ss.b], writes=[self.rs.b])
        P.op('act', lambda e: e.activation(out=self.rs.t[:, 0:1], in_=self.rs.t[:, 0:1], func=AF.Sqrt), reads=[self.rs.b], writes=[self.rs.b])
        P.op('dve', lambda e: e.reciprocal(out=self.rs.t[:, 0:1], in_=self.rs.t[:, 0:1]), reads=[self.rs.b], writes=[self.rs.b])
        tmp = self.wf[0]
        P.op('dve', lambda e: e.scalar_tensor_tensor(out=tmp.t[:], in0=xt.t[:], scalar=self.rs.t[:, 0:1], in1=A.t[:], op0=ALU.mult, op1=ALU.mult),
             reads=[xt.b, self.rs.b, A.b], writes=[tmp.b])
        if out_f is not None:
            P.op('pool', lambda e: e.tensor_tensor(out=out_f.t[:], in0=tmp.t[:], in1=S.t[:], op=ALU.add), reads=[tmp.b, S.b], writes=[out_f.b])
            if out_b is not None:
                P.op('act', lambda e: e.activation(out=out_b.t[:], in_=out_f.t[:], func=AF.Copy), reads=[out_f.b], writes=[out_b.b])
        else:
            P.op('pool', lambda e: e.tensor_tensor(out=out_b.t[:], in0=tmp.t[:], in1=S.t[:], op=ALU.add), reads=[tmp.b, S.b], writes=[out_b.b])

    def layer_batch(self, l, b):
        P = self.P
        cfg = self.cfg
        tiles = list(range(18)) if l == 0 else list(range(2, 18))
        with ExitStack() as ph:
            P.stack, saved = ph, P.stack
            self.hT = self.tile([128, 8, NT], BF16, "hT")
            self.Osb = [self.tile([128, D], BF16, "Osb") if (l == 0 or t_ >= 2) else None for t_ in range(18)]
            Al, Sl, Ac, Sc = self.mod[0], self.mod[1], self.mod[2], self.mod[3]
            self.load_mod(l, b, 0, Al, Sl, self.mod[4], self.mod[5])
            self.load_mod(l, 4, 0, Ac, Sc, self.mod[4], self.mod[5])
            for t in range(18):
                src, sbuf = self.x_src(l, b, t)
                xt = self.xin[self.xi % 2]
                self.xi += 1
                P.dma('sp', lambda e, xt=xt, src=src: e.dma_start(out=xt.t[:], in_=src), reads=[sbuf] if sbuf else [], writes=[xt.b])
                hb = self.wb[1]
                self.norm_mod(xt, Ac if t < 2 else Al, Sc if t < 2 else Sl, out_b=hb)
                self.transpose_to(hb, lambda t=t: self.hT.t[:, :, t * 128:(t + 1) * 128], self.hT.b)
            if cfg.get('mixer', True):
                with ExitStack() as mx:
                    P.stack = mx
                    if l == 0:
                        self.mixer_even(b)
                    else:
                        self.mixer_odd(b)
                    P.barrier()
                P.stack = ph
            else:
                for t in tiles:
                    P.op('pool', lambda e, t=t: e.memset(self.Osb[t].t[:], 0.0), writes=[self.Osb[t].b])
            wout = self.tile([128, 8, D], BF16, "wout")
            self.load_w(wout, self.I['even_w_out' if l == 0 else 'odd_w_out'][0], 0, D)
            G1l, G1c = self.mod[0], self.mod[1]
            self.load_gate(l, b, 0, G1l)
            self.load_gate(l, 4, 0, G1c)
            oT = self.tile([128, 8, 128], BF16, "oT")
            for t in tiles:
                G = G1c if t < 2 else G1l
                self.transpose_to(self.Osb[t], lambda: oT.t[:], oT.b)
                src, sbuf = self.x_src(l, b, t)
                xt = self.xin[self.xi % 2]
                self.xi += 1
                P.dma('sp', lambda e, xt=xt, src=src: e.dma_start(out=xt.t[:], in_=src), reads=[sbuf] if sbuf else [], writes=[xt.b])
                for half in range(2):
                    pt = self.ps[half]
                    for k in range(8):
                        P.op('pe', lambda e, pt=pt, k=k, half=half: e.matmul(pt.t[:], lhsT=oT.t[:, k, :], rhs=wout.t[:, k, half * 512:(half + 1) * 512],
                                                                           start=(k == 0), stop=(k == 7)), reads=[oT.b, wout.b], writes=[pt.b])
                    tmp = self.wf[0]
                    P.op('dve', lambda e, pt=pt, half=half, G=G, tmp=tmp: e.tensor_tensor(out=tmp.t[:, half * 512:(half + 1) * 512], in0=pt.t[:], in1=G.t[:, half * 512:(half + 1) * 512], op=ALU.mult),
                         reads=[pt.b, G.b], writes=[tmp.b])
                P.op('pool', lambda e, xt=xt: e.tensor_tensor(out=xt.t[:], in0=xt.t[:], in1=self.wf[0].t[:], op=ALU.add), reads=[xt.b, self.wf[0].b], writes=[xt.b])
                P.dma('sp', lambda e, xt=xt, t=t: e.dma_start(out=self.XS[b, t * 128:(t + 1) * 128, :], in_=xt.t[:]), reads=[xt.b], writes=[self.b_XS[b][t]])
            P.barrier()
            P.stack = saved
        if cfg.get('peer', True):
            with ExitStack() as ph:
                P.stack, saved = ph, P.stack
                self.peer_phase(l, b, tiles)
                P.barrier()
                P.stack = saved

    def final_norm(self):
        P = self.P
        fw_ = self.mod[0]
        self.load_bcast(fw_, self.I['final_norm_w'])
        for b in self.cfg.get('batches', list(range(NB))):
            for t in range(2, 18):
                xt = self.xin[self.xi % 2]
                self.xi += 1
                P.dma('sp', lambda e, xt=xt, b=b, t=t: e.dma_start(out=xt.t[:], in_=self.XS[b, t * 128:(t + 1) * 128, :]), reads=[self.b_XS[b][t]], writes=[xt.b])
                junk = self.wb[0]
                P.op('act', lambda e, xt=xt: e.activation(out=junk.t[:], in_=xt.t[:], func=AF.Square, accum_out=self.ss.t[:, 0:1]),
                     reads=[xt.b], writes=[junk.b, self.ss.b])
                P.op('dve', lambda e: e.tensor_scalar(out=self.rs.t[:, 0:1], in0=self.ss.t[:, 0:1], scalar1=1.0 / D, scalar2=EPS, op0=ALU.mult, op1=ALU.add),
                     reads=[self.ss.b], writes=[self.rs.b])
                P.op('act', lambda e: e.activation(out=self.rs.t[:, 0:1], in_=self.rs.t[:, 0:1], func=AF.Sqrt), reads=[self.rs.b], writes=[self.rs.b])
                P.op('dve', lambda e: e.reciprocal(out=self.rs.t[:, 0:1], in_=self.rs.t[:, 0:1]), reads=[self.rs.b], writes=[self.rs.b])
                o = self.wf[1]
                P.op('dve', lambda e, xt=xt: e.scalar_tensor_tensor(out=o.t[:], in0=xt.t[:], scalar=self.rs.t[:, 0:1], in1=fw_.t[:], op0=ALU.mult, op1=ALU.mult),
                     reads=[xt.b, self.rs.b, fw_.b], writes=[o.b])
                P.dma('sp', lambda e, b=b, t=t: e.dma_start(out=self.out[b, (t - 2) * 128:(t - 1) * 128, :], in_=o.t[:]), reads=[o.b], writes=[self.b_out])

    def peer_phase(self, l, b, tiles):
        P = self.P
        I = self.I
        wq = self.tile([128, 8, 2048], BF16, "wq")
        self.load_w(wq, I['peer_w_q'][l], 0, 2048)
        A2l, S2l, G2l, A2c, S2c, G2c = self.mod
        tmp_sc, tmp_nw = self.wf[0], self.wf[1]
        self.load_mod(l, b, 1, A2l, S2l, tmp_sc, tmp_nw)
        self.load_gate(l, b, 1, G2l)
        if l == 0:
            self.load_mod(l, 4, 1, A2c, S2c, tmp_sc, tmp_nw)
            self.load_gate(l, 4, 1, G2c)
        C = PeerCtx(self)
        h2f = self.tile([128, D], F32, "h2f")
        h2b = self.wb[1]
        h2T = self.tile([128, 8, 128], BF16, "h2T")
        acc = self.tile([128, D], F32, "acc")
        u_tab, v_tab = I['peer_u'].rearrange("l e d -> (l e) d"), I['peer_v'].rearrange("l e d -> (l e) d")
        for t in tiles:
            A, S, G = (A2c, S2c, G2c) if t < 2 else (A2l, S2l, G2l)
            xt = self.xin[self.xi % 2]
            self.xi += 1
            P.dma('sp', lambda e, xt=xt, t=t: e.dma_start(out=xt.t[:], in_=self.XS[b, t * 128:(t + 1) * 128, :]), reads=[self.b_XS[b][t]], writes=[xt.b])
            self.norm_mod(xt, A, S, out_f=h2f, out_b=h2b)
            self.transpose_to(h2b, lambda: h2T.t[:], h2T.b)
            peer_tile(self, C, l, h2f, h2T, wq, u_tab, v_tab, acc)
            P.op('dve', lambda e, G=G: e.tensor_tensor(out=acc.t[:], in0=acc.t[:], in1=G.t[:], op=ALU.mult), reads=[acc.b, G.b], writes=[acc.b])
            P.op('pool', lambda e, xt=xt: e.tensor_tensor(out=xt.t[:], in0=xt.t[:], in1=acc.t[:], op=ALU.add), reads=[xt.b, acc.b], writes=[xt.b])
            P.dma('sp', lambda e, xt=xt, t=t: e.dma_start(out=self.XS[b, t * 128:(t + 1) * 128, :], in_=xt.t[:]), reads=[xt.b], writes=[self.b_XS[b][t]])

    def proj_tok(self, t, w, c0, n, pt):
        for k in range(8):
            self.P.op('pe', lambda e, k=k: e.matmul(pt.t[:, 0:n], lhsT=self.hT.t[:, k, t * 128:(t + 1) * 128], rhs=w.t[:, k, c0:c0 + n],
                                                  start=(k == 0), stop=(k == 7)), reads=[self.hT.b, w.b], writes=[pt.b])

    def proj_feat(self, w, c0, t0, n, pt):
        for k in range(8):
            self.P.op('pe', lambda e, k=k: e.matmul(pt.t[:, 0:n], lhsT=w.t[:, k, c0:c0 + 128], rhs=self.hT.t[:, k, t0:t0 + n],
                                                  start=(k == 0), stop=(k == 7)), reads=[self.hT.b, w.b], writes=[pt.b])

    def perm_rope_w(self, w, wp, ncols):
        src = w.t[:, :, 0:ncols].rearrange("p k (g h s) -> p k g h s", h=2, s=16)
        dst = wp.t[:, :, 0:ncols].rearrange("p k (g h s) -> p k g h s", h=2, s=16)
        for hf in range(2):
            self.P.op('pool', lambda e, hf=hf: e.tensor_copy(out=dst[:, :, :, hf, :], in_=src[:, :, :, 1 - hf, :]), reads=[w.b], writes=[wp.b])

    def rope_evac(self, pa, pb, rope, r0, n, dst_ap, dst_buf):
        P = self.P
        t1, t2 = self.wf[0], self.wf[1]
        P.op('dve', lambda e: e.tensor_tensor(out=t1.t[:, 0:n], in0=pa.t[:, 0:n], in1=rope.t[:, 0, r0:r0 + n], op=ALU.mult), reads=[pa.b, rope.b], writes=[t1.b])
        P.op('dve', lambda e: e.tensor_tensor(out=t2.t[:, 0:n], in0=pb.t[:, 0:n], in1=rope.t[:, 1, r0:r0 + n], op=ALU.mult), reads=[pb.b, rope.b], writes=[t2.b])
        P.op('pool', lambda e: e.tensor_tensor(out=dst_ap, in0=t1.t[:, 0:n], in1=t2.t[:, 0:n], op=ALU.add), reads=[t1.b, t2.b], writes=[dst_buf])

    def dump_osb(self):
        for t in range(18):
            if self.Osb[t] is None:
                continue
            self.P.dma('pool', lambda e, t=t: e.dma_start(out=self.DBG[t * 128:(t + 1) * 128, :], in_=self.Osb[t].t[:]), reads=[self.Osb[t].b], writes=[self.b_DBG])

    def mixer_even(self, b):
        P = self.P
        win = self.I['even_w_in'][0]
        parts = self.cfg.get('parts', 'ab')
        if 'a' in parts:
            with ExitStack() as sc:
                P.stack, saved = sc, P.stack
                self.attn_A(b, win)
                P.barrier()
            P.stack = saved
        else:
            for t in range(18):
                P.op('pool', lambda e, t=t: e.memset(self.Osb[t].t[:, 0:512], 0.0), writes=[self.Osb[t].b])
        if 'b' in parts:
            with ExitStack() as sc:
                P.stack, saved = sc, P.stack
                self.hgrn(b, win)
                P.barrier()
            P.stack = saved
        else:
            for t in range(18):
                P.op('pool', lambda e, t=t: e.memset(self.Osb[t].t[:, 512:1024], 0.0), writes=[self.Osb[t].b])
        if self.cfg.get('dump_osb', False):
            self.dump_osb()

    def attn_A(self, b, win):
        P = self.P
        I = self.I
        tl = self.tile
        lambda_init = 0.8 - 0.6 * math.exp(-0.3 * 0)
        rope = tl([128, 2, TLAT], F32, "rope")
        P.dma('sp', lambda e: e.dma_start(out=rope.t[:], in_=I['rope']), writes=[rope.b])
        sm = tl([128, 8], F32, "sm")
        dl = tl([128, 256], F32, "dl")
        self.load_bcast(dl, I['diff_lambda'][0].rearrange("a d -> (a d)"))
        junk = tl([128, 128], F32, "junkA")
        P.op('dve', lambda e: e.scalar_tensor_tensor(out=junk.t[:, 0:64], in0=dl.t[:, 0:64], scalar=1.0, in1=dl.t[:, 64:128], op0=ALU.mult, op1=ALU.mult, accum_out=sm.t[:, 0:1]),
             reads=[dl.b], writes=[junk.b, sm.b])
        P.op('dve', lambda e: e.scalar_tensor_tensor(out=junk.t[:, 0:64], in0=dl.t[:, 128:192], scalar=1.0, in1=dl.t[:, 192:256], op0=ALU.mult, op1=ALU.mult, accum_out=sm.t[:, 1:2]),
             reads=[dl.b], writes=[junk.b, sm.b])
        P.op('act', lambda e: e.activation(out=sm.t[:, 0:2], in_=sm.t[:, 0:2], func=AF.Exp), reads=[sm.b], writes=[sm.b])
        P.op('dve', lambda e: e.tensor_tensor(out=sm.t[:, 2:3], in0=sm.t[:, 1:2], in1=sm.t[:, 0:1], op=ALU.subtract), reads=[sm.b], writes=[sm.b])
        P.op('dve', lambda e: e.tensor_scalar(out=sm.t[:, 2:3], in0=sm.t[:, 2:3], scalar1=-lambda_init, scalar2=None, op0=ALU.add), reads=[sm.b], writes=[sm.b])
        subw = tl([128, 128], F32, "subw")
        self.load_bcast(subw, I['diff_subln_w'][0])
        P.op('dve', lambda e: e.tensor_scalar(out=subw.t[:], in0=subw.t[:], scalar1=1.0 - lambda_init, scalar2=None, op0=ALU.mult), reads=[subw.b], writes=[subw.b])
        V1 = tl([128, 18, 4, 132], BF16, "V1")
        P.op('pool', lambda e: e.memset(V1.t[:], 1.0), writes=[V1.b])
        with ExitStack() as sc:
            P.stack, saved = sc, P.stack
            wv = tl([128, 8, 512], BF16, "wv")
            self.load_w(wv, win, 1024, 512)
            for t in range(18):
                pt = self.ps[t % 2]
                self.proj_tok(t, wv, 0, 512, pt)
                P.op('act', lambda e, t=t, pt=pt: e.activation(out=V1.t[:, t, :, 0:128], in_=pt.t[:].rearrange("p (h d) -> p h d", h=4), func=AF.Copy),
                     reads=[pt.b], writes=[V1.b])
            P.barrier()
        P.stack = saved
        wqk = tl([128, 8, 256], BF16, "wqk")
        wqkp = tl([128, 8, 256], BF16, "wqkp")
        qT = tl([128, NT], BF16, "qTa")
        kT = tl([128, NT], BF16, "kTa")
        PT = [tl([128, 512], BF16, "PT") for _ in range(3)]
        A0 = tl([128, 4, 132], F32, "A0")
        o1 = tl([128, 128], F32, "o1")
        oo = tl([128, 128], F32, "oo")
        blocks = [(0, 256)] + [(256 + i * 512, 512) for i in range(4)]
        for hd in range(4):
            self.load_w(wqk, win, hd * 128, 128, d0=0)
            self.load_w(wqk, win, 512 + hd * 128, 128, d0=128)
            self.perm_rope_w(wqk, wqkp, 256)
            for dstT, c0 in ((qT, 0), (kT, 128)):
                for (t0, n) in blocks:
                    pa, pb = self.ps[0], self.ps[1]
                    self.proj_feat(wqk, c0, t0, n, pa)
                    if t0 == 0:
                        P.op('act', lambda e, dstT=dstT, pa=pa: e.activation(out=dstT.t[:, 0:256], in_=pa.t[:, 0:256], func=AF.Copy), reads=[pa.b], writes=[dstT.b])
                    else:
                        self.proj_feat(wqkp, c0, t0, n, pb)
                        self.rope_evac(pa, pb, rope, t0 - 256, n, dstT.t[:, t0:t0 + n], dstT.b)
            pti = 0
            for (q0, nq) in blocks:
                nqs = nq // 128
                kts = list(range(2)) if q0 == 0 else list(range(18))
                for sub in range(2):
                    p0 = sub * 64
                    for ki, kt in enumerate(kts):
                        pS = self.ps[ki % 2]
                        P.op('pe', lambda e, pS=pS, kt=kt, p0=p0, q0=q0, nq=nq: e.matmul(pS.t[:, 0:nq], lhsT=kT.t[p0:p0 + 64, kt * 128:(kt + 1) * 128],
                                                                                       rhs=qT.t[p0:p0 + 64, q0:q0 + nq], start=True, stop=True),
                             reads=[kT.b, qT.b], writes=[pS.b])
                        pt_ = PT[pti % 3]
                        pti += 1
                        P.op('act', lambda e, pS=pS, pt_=pt_, nq=nq: e.activation(out=pt_.t[:, 0:nq], in_=pS.t[:, 0:nq], func=AF.Exp, scale=0.125),
                             reads=[pS.b], writes=[pt_.b])
                        for qs in range(nqs):
                            po = self.ps[2 + qs]
                            P.op('pe', lambda e, po=po, pt_=pt_, qs=qs, kt=kt, hd=hd, ki=ki, nk=len(kts): e.matmul(
                                po.t[:, 0:129], lhsT=pt_.t[:, qs * 128:(qs + 1) * 128], rhs=V1.t[:, kt, hd, 0:129], start=(ki == 0), stop=(ki == nk - 1)),
                                reads=[pt_.b, V1.b], writes=[po.b])
                    for qs in range(nqs):
                        po = self.ps[2 + qs]
                        if sub == 0:
                            P.op('act', lambda e, po=po, qs=qs: e.activation(out=A0.t[:, qs, 0:129], in_=po.t[:, 0:129], func=AF.Copy), reads=[po.b], writes=[A0.b])
                        else:
                            tt = (q0 + qs * 128) // 128
                            P.op('dve', lambda e, qs=qs: e.reciprocal(out=sm.t[:, 3:4], in_=A0.t[:, qs, 128:129]), reads=[A0.b], writes=[sm.b])
                            P.op('dve', lambda e, po=po: e.reciprocal(out=sm.t[:, 4:5], in_=po.t[:, 128:129]), reads=[po.b], writes=[sm.b])
                            P.op('dve', lambda e: e.tensor_tensor(out=sm.t[:, 4:5], in0=sm.t[:, 4:5], in1=sm.t[:, 2:3], op=ALU.mult), reads=[sm.b], writes=[sm.b])
                            P.op('dve', lambda e, qs=qs: e.tensor_scalar(out=o1.t[:], in0=A0.t[:, qs, 0:128], scalar1=sm.t[:, 3:4], scalar2=None, op0=ALU.mult),
                                 reads=[A0.b, sm.b], writes=[o1.b])
                            P.op('dve', lambda e, po=po: e.scalar_tensor_tensor(out=oo.t[:], in0=po.t[:, 0:128], scalar=sm.t[:, 4:5], in1=o1.t[:], op0=ALU.mult, op1=ALU.add),
                                 reads=[po.b, sm.b, o1.b], writes=[oo.b])
                            P.op('act', lambda e: e.activation(out=junk.t[:], in_=oo.t[:], func=AF.Square, accum_out=sm.t[:, 5:6]), reads=[oo.b], writes=[junk.b, sm.b])
                            P.op('dve', lambda e: e.tensor_scalar(out=sm.t[:, 6:7], in0=sm.t[:, 5:6], scalar1=1.0 / 128, scalar2=EPS, op0=ALU.mult, op1=ALU.add), reads=[sm.b], writes=[sm.b])
                            P.op('act', lambda e: e.activation(out=sm.t[:, 6:7], in_=sm.t[:, 6:7], func=AF.Sqrt), reads=[sm.b], writes=[sm.b])
                            P.op('dve', lambda e: e.reciprocal(out=sm.t[:, 6:7], in_=sm.t[:, 6:7]), reads=[sm.b], writes=[sm.b])
                            P.op('dve', lambda e, tt=tt, hd=hd: e.scalar_tensor_tensor(out=self.Osb[tt].t[:, hd * 128:(hd + 1) * 128], in0=oo.t[:], scalar=sm.t[:, 6:7], in1=subw.t[:],
                                                                                    op0=ALU.mult, op1=ALU.mult), reads=[oo.b, sm.b, subw.b], writes=[self.Osb[tt].b])

    def mixer_odd(self, b):
        P = self.P
        win = self.I['odd_w_in'][0]
        parts = self.cfg.get('parts', 'cd')
        if 'd' in parts:
            with ExitStack() as sc:
                P.stack, saved = sc, P.stack
                self.swa(b, win)
                P.barrier()
            P.stack = saved
        else:
            for t in range(2, 18):
                P.op('pool', lambda e, t=t: e.memset(self.Osb[t].t[:, 512:1024], 0.0), writes=[self.Osb[t].b])
        if 'c' in parts:
            with ExitStack() as sc:
                P.stack, saved = sc, P.stack
                self.gdn(b, win)
                P.barrier()
            P.stack = saved
        else:
            for t in range(2, 18):
                P.op('pool', lambda e, t=t: e.memset(self.Osb[t].t[:, 0:512], 0.0), writes=[self.Osb[t].b])
        if self.cfg.get('dump_osb', False):
            self.dump_osb()

    def swa(self, b, win):
        P = self.P
        I = self.I
        tl = self.tile
        rope = tl([128, 2, TLAT], F32, "rope")
        P.dma('sp', lambda e: e.dma_start(out=rope.t[:], in_=I['rope']), writes=[rope.b])
        swm = tl([128, 2, 128], F32, "swm")
        P.dma('sp', lambda e: e.dma_start(out=swm.t[:], in_=I['swm']), writes=[swm.b])
        esink = tl([128, 8], F32, "esink")
        self.load_bcast(esink, I['swa_sink'][0])
        P.op('act', lambda e: e.activation(out=esink.t[:], in_=esink.t[:], func=AF.Exp), reads=[esink.b], writes=[esink.b])
        sm = tl([128, 8], F32, "smD")
        wq = tl([128, 8, 512], BF16, "wqd")
        wqp = tl([128, 8, 512], BF16, "wqdp")
        for g in range(4):
            for kv in range(2):
                self.load_w(wq, win, 2064 + (kv * 4 + g) * 64, 64, d0=g * 128 + kv * 64)
        self.perm_rope_w(wq, wqp, 512)
        wkv = tl([128, 8, 256], BF16, "wkvd")
        wkp = tl([128, 8, 256], BF16, "wkdp")
        self.load_w(wkv, win, 2576, 256)
        self.perm_rope_w(wkv, wkp, 128)
        V1 = tl([128, 18, 2, 66], BF16, "V1d")
        P.op('pool', lambda e: e.memset(V1.t[:], 1.0), writes=[V1.b])
        for t in range(18):
            pt = self.ps[t % 2]
            self.proj_tok(t, wkv, 128, 128, pt)
            P.op('act', lambda e, t=t, pt=pt: e.activation(out=V1.t[:, t, :, 0:64], in_=pt.t[:, 0:128].rearrange("p (h d) -> p h d", h=2), func=AF.Copy),
                 reads=[pt.b], writes=[V1.b])
        kT = tl([128, NT], BF16, "kTd")
        qT = tl([128, 4, TLAT], BF16, "qTd")
        blocks = [(0, 256)] + [(256 + i * 512, 512) for i in range(4)]
        for (t0, n) in blocks:
            pa, pb = self.ps[0], self.ps[1]
            self.proj_feat(wkv, 0, t0, n, pa)
            if t0 == 0:
                P.op('act', lambda e, pa=pa: e.activation(out=kT.t[:, 0:256], in_=pa.t[:, 0:256], func=AF.Copy), reads=[pa.b], writes=[kT.b])
            else:
                self.proj_feat(wkp, 0, t0, n, pb)
                self.rope_evac(pa, pb, rope, t0 - 256, n, kT.t[:, t0:t0 + n], kT.b)
        for g in range(4):
            for (t0, n) in blocks[1:]:
                pa, pb = self.ps[0], self.ps[1]
                self.proj_feat(wq, g * 128, t0, n, pa)
                self.proj_feat(wqp, g * 128, t0, n, pb)
                self.rope_evac(pa, pb, rope, t0 - 256, n, qT.t[:, g, t0 - 256:t0 - 256 + n], qT.b)
        PT = [tl([128, 512], BF16, "PTd") for _ in range(5)]
        zz = tl([128, 4], F32, "zz")
        for kv in range(2):
            p0 = kv * 64
            for i in range(16):
                keys = [(0, None), (1, None)]
                if i > 0:
                    keys.append((2 + i - 1, 0))
                keys.append((2 + i, None))
                if i < 15:
                    keys.append((2 + i + 1, 1))
                for n_, (kt, msk) in enumerate(keys):
                    pS = self.ps[n_ % 2]
                    P.op('pe', lambda e, pS=pS, kt=kt, p0=p0, i=i: e.matmul(pS.t[:, 0:512], lhsT=kT.t[p0:p0 + 64, kt * 128:(kt + 1) * 128],
                                                                         rhs=qT.t[p0:p0 + 64, :, i * 128:(i + 1) * 128], start=True, stop=True),
                         reads=[kT.b, qT.b], writes=[pS.b])
                    pt_ = PT[n_]
                    P.op('act', lambda e, pS=pS, pt_=pt_: e.activation(out=pt_.t[:], in_=pS.t[:], func=AF.Exp, scale=0.125), reads=[pS.b], writes=[pt_.b])
                    if msk is not None:
                        P.op('pool', lambda e, pt_=pt_, msk=msk: e.tensor_tensor(out=pt_.t[:].rearrange("p (g q) -> p g q", g=4), in0=pt_.t[:].rearrange("p (g q) -> p g q", g=4),
                                                                             in1=swm.t[:, msk, :].unsqueeze(1).to_broadcast([128, 4, 128]), op=ALU.mult),
                             reads=[pt_.b, swm.b], writes=[pt_.b])
                po = self.ps[2]
                for g in range(4):
                    for n_, (kt, msk) in enumerate(keys):
                        P.op('pe', lambda e, g=g, n_=n_, kt=kt, kv=kv, nk=len(keys): e.matmul(po.t[:, g * 66:g * 66 + 65], lhsT=PT[n_].t[:, g * 128:(g + 1) * 128],
                                                                                          rhs=V1.t[:, kt, kv, 0:65], start=(n_ == 0), stop=(n_ == nk - 1)),
                             reads=[PT[n_].b, V1.b], writes=[po.b])
                po3 = po.t[:, 0:264].rearrange("p (g c) -> p g c", c=66)
                P.op('dve', lambda e, kv=kv, po3=po3: e.tensor_tensor(out=zz.t[:], in0=po3[:, :, 64], in1=esink.t[:, kv * 4:(kv + 1) * 4], op=ALU.add),
                     reads=[po.b, esink.b], writes=[zz.b])
                P.op('dve', lambda e: e.reciprocal(out=zz.t[:], in_=zz.t[:]), reads=[zz.b], writes=[zz.b])
                ot = self.Osb[2 + i]
                P.op('dve', lambda e, ot=ot, kv=kv, po3=po3: e.tensor_tensor(out=ot.t[:, 512 + kv * 256:512 + (kv + 1) * 256].rearrange("p (g d) -> p g d", g=4), in0=po3[:, :, 0:64],
                                                                          in1=zz.t[:].unsqueeze(2).to_broadcast([128, 4, 64]), op=ALU.mult),
                     reads=[po.b, zz.b], writes=[ot.b])

    def hgrn(self, b, win):
        P = self.P
        I = self.I
        tl = self.tile
        wB = tl([128, 8, 2048], BF16, "wB")
        self.load_w(wB, win, 1536, 1024, d0=0)
        self.load_w(wB, win, 3584, 512, d0=1536)
        hgc = tl([128, 2, 4, 128], F32, "hgc")
        P.dma('sp', lambda e: e.dma_start(out=hgc.t[:], in_=I['hgc']), writes=[hgc.b])
        hcs = tl([128, 4], F32, "hcs")
        P.dma('sp', lambda e: e.dma_start(out=hcs.t[:], in_=I['hcsel']), writes=[hcs.b])
        lb = tl([128, 512], F32, "lb")
        oml = tl([128, 512], F32, "oml")
        e1_, e2_ = self.wf[0], self.wf[1]
        self.load_bcast(lb, I['hgrn_lb'][0])
        self.load_bcast(e1_, I['hgrn_lb'][1:3].rearrange("a d -> (a d)"))
        P.op('act', lambda e: e.activation(out=lb.t[:], in_=lb.t[:], func=AF.Exp), reads=[lb.b], writes=[lb.b])
        P.op('act', lambda e: e.activation(out=e1_.t[:], in_=e1_.t[:], func=AF.Exp), reads=[e1_.b], writes=[e1_.b])
        P.op('dve', lambda e: e.tensor_tensor(out=oml.t[:], in0=e1_.t[:, 0:512], in1=e1_.t[:, 512:1024], op=ALU.add), reads=[e1_.b], writes=[oml.b])
        P.op('dve', lambda e: e.tensor_tensor(out=oml.t[:], in0=oml.t[:], in1=lb.t[:], op=ALU.add), reads=[oml.b, lb.b], writes=[oml.b])
        P.op('dve', lambda e: e.reciprocal(out=oml.t[:], in_=oml.t[:]), reads=[oml.b], writes=[oml.b])
        P.op('dve', lambda e: e.tensor_tensor(out=lb.t[:], in0=lb.t[:], in1=oml.t[:], op=ALU.mult), reads=[oml.b, lb.b], writes=[lb.b])
        P.op('dve', lambda e: e.tensor_scalar(out=oml.t[:], in0=lb.t[:], scalar1=-1.0, scalar2=1.0, op0=ALU.mult, op1=ALU.add), reads=[lb.b], writes=[oml.b])
        hnw = tl([128, 64], F32, "hnw")
        self.load_bcast(hnw, I['hgrn_norm_w'][0])
        S = tl([128, 4, 64], F32, "S")
        Sb = tl([128, 4, 64], BF16, "Sb")
        ebl = tl([128, 4, 4], F32, "ebl")
        qf = tl([128, 512], F32, "qf")
        vb = tl([128, 512], BF16, "vb")
        ff = tl([128, 512], F32, "ff")
        lf = tl([128, 512], F32, "lf")
        kk = tl([128, 512], F32, "kk")
        ee = [tl([128, 512], F32, "ee") for _ in range(2)]
        qt = tl([128, 512], BF16, "qt")
        kt_ = tl([128, 512], BF16, "kt")
        qc = tl([128, 512], BF16, "qc")
        kh = tl([128, 512], BF16, "kh")
        qtT = tl([128, 4, 128], BF16, "qtT")
        ktT = tl([128, 4, 128], BF16, "ktT")
        qcT = tl([128, 4, 128], BF16, "qcT")
        att = [tl([128, 128], BF16, "att") for _ in range(2)]
        qcTz = tl([128, 4, 64], BF16, "qcTz")
        P.op('pool', lambda e: e.memset(qcTz.t[:], 0.0), writes=[qcTz.b])
        khz = tl([128, 512], BF16, "khz")
        rm3 = tl([128, 1], F32, "rm3")
        P.op('dve', lambda e: e.tensor_scalar(out=rm3.t[:], in0=hcs.t[:, 2:3], scalar1=-1.0, scalar2=1.0, op0=ALU.mult, op1=ALU.add), reads=[hcs.b], writes=[rm3.b])
        of = tl([128, 512], F32, "of")
        gw = tl([128, 512], F32, "gw")
        sq = tl([128, 512], F32, "sq")
        s8 = tl([128, 8], F32, "s8")
        ps = self.ps
        for dr in range(2):
            self.load_w(wB, win, 2560 + dr * 512, 512, d0=1024)
            P.op('dve', lambda e: e.memset(S.t[:], 0.0), writes=[S.b])
            P.op('dve', lambda e: e.memset(Sb.t[:], 0.0), writes=[Sb.b])
            order = [0, 1] + list(range(2, 18)) if dr == 0 else [1, 0] + list(range(17, 1, -1))
            chunks = [0, 1, 2, 3] if dr == 0 else [3, 2, 1, 0]
            for t in order:
                self.proj_tok(t, wB, 0, 512, ps[0])
                P.op('act', lambda e: e.activation(out=qf.t[:], in_=ps[0].t[:], func=AF.Silu), reads=[ps[0].b], writes=[qf.b])
                self.proj_tok(t, wB, 512, 512, ps[1])
                P.op('act', lambda e: e.activation(out=vb.t[:], in_=ps[1].t[:], func=AF.Copy), reads=[ps[1].b], writes=[vb.b])
                self.proj_tok(t, wB, 1024, 512, ps[2])
                P.op('act', lambda e: e.activation(out=ff.t[:], in_=ps[2].t[:], func=AF.Sigmoid), reads=[ps[2].b], writes=[ff.b])
                P.op('dve', lambda e: e.tensor_tensor(out=ff.t[:], in0=ff.t[:], in1=oml.t[:], op=ALU.mult), reads=[ff.b, oml.b], writes=[ff.b])
                P.op('dve', lambda e: e.tensor_tensor(out=ff.t[:], in0=ff.t[:], in1=lb.t[:], op=ALU.add), reads=[ff.b, lb.b], writes=[ff.b])
                P.op('act', lambda e: e.activation(out=lf.t[:], in_=ff.t[:], func=AF.Ln), reads=[ff.b], writes=[lf.b])
                P.op('pool', lambda e: e.tensor_scalar(out=kk.t[:], in0=ff.t[:], scalar1=-1.0, scalar2=1.0, op0=ALU.mult, op1=ALU.add), reads=[ff.b], writes=[kk.b])
                for j in range(3):
                    P.op('pe', lambda e, j=j, dr=dr: e.matmul(ps[3 + j].t[:], lhsT=hgc.t[:, dr, j, :], rhs=lf.t[:], start=True, stop=True),
                         reads=[hgc.b, lf.b], writes=[ps[3 + j].b])
                P.op('act', lambda e: e.activation(out=ee[0].t[:], in_=ps[3].t[:], func=AF.Exp), reads=[ps[3].b], writes=[ee[0].b])
                P.op('dve', lambda e: e.scalar_tensor_tensor(out=qt.t[:], in0=qf.t[:], scalar=0.125, in1=ee[0].t[:], op0=ALU.mult, op1=ALU.mult),
                     reads=[qf.b, ee[0].b], writes=[qt.b])
                P.op('act', lambda e: e.activation(out=ee[1].t[:], in_=ps[3].t[:], func=AF.Exp, scale=-1.0), reads=[ps[3].b], writes=[ee[1].b])
                P.op('pool', lambda e: e.tensor_tensor(out=kt_.t[:], in0=kk.t[:], in1=ee[1].t[:], op=ALU.mult), reads=[kk.b, ee[1].b], writes=[kt_.b])
                P.op('act', lambda e: e.activation(out=ee[0].t[:], in_=ps[4].t[:], func=AF.Exp), reads=[ps[4].b], writes=[ee[0].b])
                P.op('dve', lambda e: e.scalar_tensor_tensor(out=qc.t[:], in0=qf.t[:], scalar=0.125, in1=ee[0].t[:], op0=ALU.mult, op1=ALU.mult),
                     reads=[qf.b, ee[0].b], writes=[qc.b])
                P.op('act', lambda e: e.activation(out=ee[1].t[:], in_=ps[5].t[:], func=AF.Exp), reads=[ps[5].b], writes=[ee[1].b])
                P.op('pool', lambda e: e.tensor_tensor(out=kh.t[:], in0=kk.t[:], in1=ee[1].t[:], op=ALU.mult), reads=[kk.b, ee[1].b], writes=[kh.b])
                self.transpose_to(qt, lambda: qtT.t[:], qtT.b, nblk=4)
                self.transpose_to(kt_, lambda: ktT.t[:], ktT.b, nblk=4)
                self.transpose_to(qc, lambda: qcT.t[:], qcT.b, nblk=4)
                P.op('pool', lambda e: e.tensor_copy(out=qcTz.t[:, :, 32:64], in_=qcT.t[:, :, 96:128]), reads=[qcT.b], writes=[qcTz.b])
                P.op('pool', lambda e: e.tensor_scalar(out=khz.t[:], in0=kh.t[:], scalar1=rm3.t[:, 0:1], scalar2=None, op0=ALU.mult), reads=[kh.b, rm3.b], writes=[khz.b])
                for hp in range(4):
                    P.op('pe', lambda e, hp=hp: e.matmul(ps[6].t[:, 64 + hp * 4:64 + (hp + 1) * 4], lhsT=lf.t[:, hp * 128:(hp + 1) * 128], rhs=hcs.t[:], start=True, stop=True),
                         reads=[lf.b, hcs.b], writes=[ps[6].b])
                P.op('act', lambda e: e.activation(out=ebl.t[:], in_=ps[6].t[:, 64:80].rearrange("p (a c) -> p a c", a=4), func=AF.Exp), reads=[ps[6].b], writes=[ebl.b])
                for hp in range(4):
                    for half in range(2):
                        h = hp * 2 + half
                        p0 = half * 64
                        pa = ps[half]
                        P.op('pe', lambda e, pa=pa, p0=p0, hp=hp: e.matmul(pa.t[:, 0:128], lhsT=ktT.t[p0:p0 + 64, hp, :], rhs=qtT.t[p0:p0 + 64, hp, :], start=True, stop=True),
                             reads=[ktT.b, qtT.b], writes=[pa.b])
                        P.op('dve', lambda e, pa=pa, half=half, dr=dr: e.tensor_tensor(out=att[half].t[:], in0=pa.t[:, 0:128], in1=hgc.t[:, dr, 3, :], op=ALU.mult),
                             reads=[pa.b, hgc.b], writes=[att[half].b])
                        po = ps[2 + half]
                        P.op('pe', lambda e, po=po, half=half, hp=hp, h=h: e.matmul(po.t[:, hp * 64:(hp + 1) * 64], lhsT=att[half].t[:], rhs=vb.t[:, h * 64:(h + 1) * 64], start=True, stop=False),
                             reads=[att[half].b, vb.b], writes=[po.b])
                    for ci, c in enumerate(chunks):
                        for half in range(2):
                            p0 = half * 64
                            po = ps[2 + half]
                            if c < 3:
                                P.op('pe', lambda e, po=po, p0=p0, hp=hp, c=c, ci=ci: e.matmul(po.t[c * 32:(c + 1) * 32, hp * 64:(hp + 1) * 64], lhsT=qcT.t[p0:p0 + 64, hp, c * 32:(c + 1) * 32],
                                                                                         rhs=Sb.t[p0:p0 + 64, hp, :], start=False, stop=(ci == 3)),
                                     reads=[qcT.b, Sb.b], writes=[po.b])
                            else:
                                P.op('pe', lambda e, po=po, p0=p0, hp=hp, c=c, ci=ci: e.matmul(po.t[64:128, hp * 64:(hp + 1) * 64], lhsT=qcTz.t[p0:p0 + 64, hp, :],
                                                                                         rhs=Sb.t[p0:p0 + 64, hp, :], start=False, stop=(ci == 3)),
                                     reads=[qcTz.b, Sb.b], writes=[po.b])
                        for half in range(2):
                            h = hp * 2 + half
                            p0 = half * 64
                            if c < 3:
                                P.op('pe', lambda e, p0=p0, c=c, h=h: e.matmul(ps[6].t[p0:p0 + 64, 0:64], lhsT=kh.t[c * 32:(c + 1) * 32, h * 64:(h + 1) * 64],
                                                                            rhs=vb.t[c * 32:(c + 1) * 32, h * 64:(h + 1) * 64], start=True, stop=True),
                                     reads=[kh.b, vb.b], writes=[ps[6].b])
                            else:
                                P.op('pe', lambda e, p0=p0, c=c, h=h: e.matmul(ps[6].t[p0:p0 + 64, 0:64], lhsT=khz.t[64:128, h * 64:(h + 1) * 64],
                                                                            rhs=vb.t[64:128, h * 64:(h + 1) * 64], start=True, stop=True),
                                     reads=[khz.b, vb.b], writes=[ps[6].b])
                        P.op('dve', lambda e, hp=hp, c=c: e.scalar_tensor_tensor(out=S.t[:, hp, :], in0=S.t[:, hp, :], scalar=ebl.t[:, hp, c:c + 1], in1=ps[6].t[:, 0:64],
                                                                              op0=ALU.mult, op1=ALU.add), reads=[S.b, ebl.b, ps[6].b], writes=[S.b])
                        P.op('act', lambda e, hp=hp: e.activation(out=Sb.t[:, hp, :], in_=S.t[:, hp, :], func=AF.Copy), reads=[S.b], writes=[Sb.b])
                of4 = of.t[:].rearrange("p (a h d) -> p a h d", a=4, h=2)
                if dr == 0:
                    for half in range(2):
                        P.op('act', lambda e, half=half: e.activation(out=of4[:, :, half, :], in_=ps[2 + half].t[:, 0:256].rearrange("p (a d) -> p a d", a=4), func=AF.Copy),
                             reads=[ps[2 + half].b], writes=[of.b])
                    P.dma('sp', lambda e, t=t: e.dma_start(out=self.OF[t * 128:(t + 1) * 128, :], in_=of.t[:]), reads=[of.b], writes=[self.b_OF[t]])
                else:
                    P.dma('sp', lambda e, t=t: e.dma_start(out=of.t[:], in_=self.OF[t * 128:(t + 1) * 128, :]), reads=[self.b_OF[t]], writes=[of.b])
                    for half in range(2):
                        P.op('dve', lambda e, half=half: e.tensor_tensor(out=of4[:, :, half, :], in0=of4[:, :, half, :], in1=ps[2 + half].t[:, 0:256].rearrange("p (a d) -> p a d", a=4), op=ALU.add),
                             reads=[ps[2 + half].b, of.b], writes=[of.b])
                    self.proj_tok(t, wB, 1536, 512, ps[4])
                    P.op('act', lambda e: e.activation(out=gw.t[:], in_=ps[4].t[:], func=AF.Silu), reads=[ps[4].b], writes=[gw.b])
                    P.op('pool', lambda e: e.tensor_tensor(out=gw.t[:].rearrange("p (h d) -> p h d", h=8), in0=gw.t[:].rearrange("p (h d) -> p h d", h=8),
                                                          in1=hnw.t[:].unsqueeze(1).to_broadcast([128, 8, 64]), op=ALU.mult), reads=[gw.b, hnw.b], writes=[gw.b])
                    P.op('pool', lambda e: e.tensor_tensor(out=sq.t[:], in0=of.t[:], in1=of.t[:], op=ALU.mult), reads=[of.b], writes=[sq.b])
                    P.op('dve', lambda e: e.tensor_reduce(out=s8.t[:], in_=sq.t[:].rearrange("p (h d) -> p h d", h=8), axis=AX.X, op=ALU.add), reads=[sq.b], writes=[s8.b])
                    P.op('dve', lambda e: e.tensor_scalar(out=s8.t[:], in0=s8.t[:], scalar1=1.0 / 64, scalar2=EPS, op0=ALU.mult, op1=ALU.add), reads=[s8.b], writes=[s8.b])
                    P.op('act', lambda e: e.activation(out=s8.t[:], in_=s8.t[:], func=AF.Sqrt), reads=[s8.b], writes=[s8.b])
                    P.op('dve', lambda e: e.reciprocal(out=s8.t[:], in_=s8.t[:]), reads=[s8.b], writes=[s8.b])
                    P.op('dve', lambda e: e.tensor_tensor(out=sq.t[:].rearrange("p (h d) -> p h d", h=8), in0=of.t[:].rearrange("p (h d) -> p h d", h=8),
                                                         in1=s8.t[:].unsqueeze(2).to_broadcast([128, 8, 64]), op=ALU.mult), reads=[of.b, s8.b], writes=[sq.b])
                    P.op('dve', lambda e, t=t: e.tensor_tensor(out=self.Osb[t].t[:, 512:1024], in0=sq.t[:], in1=gw.t[:], op=ALU.mult), reads=[sq.b, gw.b], writes=[self.Osb[t].b])

    def gdn(self, b, win):
        P = self.P
        I = self.I
        tl = self.tile
        ps = self.ps
        ident, ones, negones = self.ident, self.ones, self.negones
        gdc = tl([128, 2, 4, 128], F32, "gdc")
        P.dma('sp', lambda e: e.dma_start(out=gdc.t[:], in_=I['gdc']), writes=[gdc.b])
        gcs = tl([128, 2], F32, "gcs")
        P.dma('sp', lambda e: e.dma_start(out=gcs.t[:], in_=I['gcsel']), writes=[gcs.b])
        nexpA = tl([128, 8], F32, "nexpA")
        self.load_bcast(nexpA, I['gdn_a_log'][0].rearrange("a h -> (a h)"))
        P.op('act', lambda e: e.activation(out=nexpA.t[:], in_=nexpA.t[:], func=AF.Exp), reads=[nexpA.b], writes=[nexpA.b])
        P.op('dve', lambda e: e.tensor_scalar(out=nexpA.t[:], in0=nexpA.t[:], scalar1=-1.0, scalar2=None, op0=ALU.mult), reads=[nexpA.b], writes=[nexpA.b])
        dtb = tl([128, 8], F32, "dtb")
        self.load_bcast(dtb, I['gdn_dt_bias'][0].rearrange("a h -> (a h)"))
        gnw = tl([128, 128], F32, "gnw")
        self.load_bcast(gnw, I['gdn_norm_w'][0])
        cw = tl([128, 5, 12], F32, "cw")
        for j in range(5):
            P.dma('sp', lambda e, j=j: e.dma_start(out=cw.t[:, j, :], in_=I['gdn_conv_w'][0, j, :].rearrange("(cb p) -> p cb", p=128), allow_slow_non_contiguous=True), writes=[cw.b])
        qT = [tl([128, NT], BF16, "qTc") for _ in range(4)]
        kT = [tl([128, NT], BF16, "kTc") for _ in range(4)]
        vtok = tl([128, 18, 4, 128], BF16, "vtok")
        blocks = [(0, 256)] + [(256 + i * 512, 512) for i in range(4)]
        segs = [(0, 256), (256, NT)]
        with ExitStack() as sc:
            P.stack, saved = sc, P.stack
            P.barrier()
            raw = T(self.modbig.t[:, 0:NT])
            acc = T(self.modbig.t[:, NT:2 * NT])
            vfm = T(self.modbig.t[:, 2 * NT:6 * D].bitcast(BF16)[:, 0:NT])
            wblk = [tl([128, 8, 128], BF16, "wblk") for _ in range(2)]
            for cb in range(12):
                w = wblk[cb % 2]
                self.load_w(w, win, cb * 128, 128)
                for bi, (t0, n) in enumerate(blocks):
                    pa = ps[bi % 2]
                    self.proj_feat(w, 0, t0, n, pa)
                    P.op('act', lambda e, pa=pa, t0=t0, n=n: e.activation(out=raw.t[:, t0:t0 + n], in_=pa.t[:, 0:n], func=AF.Copy), reads=[pa.b], writes=[raw.b])
                P.op('dve', lambda e, cb=cb: e.tensor_scalar(out=acc.t[:], in0=raw.t[:], scalar1=cw.t[:, 2, cb:cb + 1], scalar2=None, op0=ALU.mult), reads=[raw.b, cw.b], writes=[acc.b])
                for j in (0, 1, 3, 4):
                    sh = j - 2
                    for (s0, s1) in segs:
                        lo, hi = max(s0, s0 - sh), min(s1, s1 - sh)
                        P.op('dve', lambda e, cb=cb, j=j, lo=lo, hi=hi, sh=sh: e.scalar_tensor_tensor(out=acc.t[:, lo:hi], in0=raw.t[:, lo + sh:hi + sh], scalar=cw.t[:, j, cb:cb + 1],
                                                                                                   in1=acc.t[:, lo:hi], op0=ALU.mult, op1=ALU.add), reads=[raw.b, cw.b, acc.b], writes=[acc.b])
                P.op('act', lambda e: e.activation(out=raw.t[:], in_=acc.t[:], func=AF.Silu), reads=[acc.b], writes=[raw.b])
                if cb < 8:
                    dest = qT[cb] if cb < 4 else kT[cb - 4]
                    scale = 128.0 ** -0.5 if cb < 4 else 1.0
                    for (t0, n) in blocks:
                        sq, rn = self.wf[0], self.wf[1]
                        P.op('pool', lambda e, t0=t0, n=n: e.tensor_tensor(out=sq.t[:, 0:n], in0=raw.t[:, t0:t0 + n], in1=raw.t[:, t0:t0 + n], op=ALU.mult), reads=[raw.b], writes=[sq.b])
                        P.op('pe', lambda e, n=n: e.matmul(ps[2].t[:, 0:n], lhsT=ones.t[:], rhs=sq.t[:, 0:n], start=True, stop=True), reads=[ones.b, sq.b], writes=[ps[2].b])
                        P.op('dve', lambda e, n=n: e.tensor_scalar(out=rn.t[:, 0:n], in0=ps[2].t[:, 0:n], scalar1=EPS, scalar2=None, op0=ALU.add), reads=[ps[2].b], writes=[rn.b])
                        P.op('act', lambda e, n=n: e.activation(out=rn.t[:, 0:n], in_=rn.t[:, 0:n], func=AF.Sqrt), reads=[rn.b], writes=[rn.b])
                        P.op('dve', lambda e, n=n: e.reciprocal(out=rn.t[:, 0:n], in_=rn.t[:, 0:n]), reads=[rn.b], writes=[rn.b])
                        P.op('dve', lambda e, t0=t0, n=n, dest=dest, scale=scale: e.scalar_tensor_tensor(out=dest.t[:, t0:t0 + n], in0=raw.t[:, t0:t0 + n], scalar=scale, in1=rn.t[:, 0:n],
                                                                                                     op0=ALU.mult, op1=ALU.mult), reads=[raw.b, rn.b], writes=[dest.b])
                else:
                    h = cb - 8
                    P.op('act', lambda e: e.activation(out=vfm.t[:], in_=raw.t[:], func=AF.Copy), reads=[raw.b], writes=[vfm.b])
                    for tg in range(0, 18, 8):
                        nt_ = min(8, 18 - tg)
                        for k in range(nt_):
                            P.op('pe', lambda e, k=k, tg=tg: e.transpose(out=self.psb.t[:, k * 128:(k + 1) * 128], in_=vfm.t[:, (tg + k) * 128:(tg + k + 1) * 128], identity=self.identb.t[:]),
                                 reads=[vfm.b, self.identb.b], writes=[self.psb.b])
                        P.op('act', lambda e, tg=tg, nt_=nt_, h=h: e.activation(out=vtok.t[:, tg:tg + nt_, h, :], in_=self.psb.t[:, 0:nt_ * 128].rearrange("p (a b) -> p a b", a=nt_), func=AF.Copy),
                             reads=[self.psb.b], writes=[vtok.b])
            P.barrier()
        P.stack = saved
        stop = self.cfg.get('gdn_stop', 99)
        if stop == 1:
            for t in range(2, 18):
                P.op('pool', lambda e, t=t: e.memset(self.Osb[t].t[:, 0:512], 0.0), writes=[self.Osb[t].b])
                P.op('pool', lambda e, t=t: e.tensor_copy(out=self.Osb[t].t[:, 0:128], in_=vtok.t[:, t, 0, :]), reads=[vtok.b], writes=[self.Osb[t].b])
            return
        wg = tl([128, 8, 16], BF16, "wg")
        self.load_w(wg, win, 2048, 16)
        graw = tl([128, 18, 16], F32, "graw")
        for t in range(18):
            pg = ps[t % 2]
            self.proj_tok(t, wg, 0, 16, pg)
            P.op('act', lambda e, t=t, pg=pg: e.activation(out=graw.t[:, t, :], in_=pg.t[:, 0:16], func=AF.Copy), reads=[pg.b], writes=[graw.b])
        la = tl([128, 18, 8], F32, "la")
        beta = tl([128, 18, 8], F32, "beta")
        nbeta = tl([128, 18, 8], F32, "nbeta")
        P.op('dve', lambda e: e.tensor_tensor(out=la.t[:], in0=graw.t[:, :, 0:8], in1=dtb.t[:].unsqueeze(1).to_broadcast([128, 18, 8]), op=ALU.add), reads=[graw.b, dtb.b], writes=[la.b])
        P.op('act', lambda e: e.activation(out=la.t[:], in_=la.t[:], func=AF.Exp), reads=[la.b], writes=[la.b])
        P.op('act', lambda e: e.activation(out=la.t[:], in_=la.t[:], func=AF.Ln, bias=1.0, scale=1.0), reads=[la.b], writes=[la.b])
        P.op('dve', lambda e: e.tensor_tensor(out=la.t[:], in0=la.t[:], in1=nexpA.t[:].unsqueeze(1).to_broadcast([128, 18, 8]), op=ALU.mult), reads=[la.b, nexpA.b], writes=[la.b])
        P.op('act', lambda e: e.activation(out=beta.t[:], in_=graw.t[:, :, 8:16], func=AF.Sigmoid), reads=[graw.b], writes=[beta.b])
        P.op('dve', lambda e: e.tensor_scalar(out=nbeta.t[:], in0=beta.t[:], scalar1=-1.0, scalar2=None, op0=ALU.mult), reads=[beta.b], writes=[nbeta.b])
        with ExitStack() as sc:
            P.stack, saved = sc, P.stack
            wz = tl([128, 8, 512], BF16, "wz")
            self.load_w(wz, win, 1536, 512)
            zst = [self.wf[0], self.wf[1]]
            for t in range(2, 18):
                pz = ps[2 + t % 2]
                self.proj_tok(t, wz, 0, 512, pz)
                zt = zst[t % 2]
                P.op('act', lambda e, pz=pz, zt=zt: e.activation(out=zt.t[:, 0:512], in_=pz.t[:], func=AF.Silu), reads=[pz.b], writes=[zt.b])
                P.dma('sp', lambda e, t=t, zt=zt: e.dma_start(out=self.ZS[t * 128:(t + 1) * 128, :], in_=zt.t[:, 0:512]), reads=[zt.b], writes=[self.b_ZS[t]])
            P.barrier()
        P.stack = saved
        if stop == 2:
            for t in range(2, 18):
                P.op('pool', lambda e, t=t: e.memset(self.Osb[t].t[:, 0:512], 0.0), writes=[self.Osb[t].b])
                P.op('pool', lambda e, t=t: e.tensor_copy(out=self.Osb[t].t[:, 0:8], in_=la.t[:, t, :]), reads=[la.b], writes=[self.Osb[t].b])
                P.op('pool', lambda e, t=t: e.tensor_copy(out=self.Osb[t].t[:, 8:16], in_=beta.t[:, t, :]), reads=[beta.b], writes=[self.Osb[t].b])
            return
        f32t = lambda nm: tl([128, 128], F32, nm)
        LaT, labc, egB, decS, decT = f32t("LaT"), f32t("labc"), f32t("egB"), f32t("decS"), f32t("decT")
        Mm = [f32t("Mm") for _ in range(2)]
        Mt = [f32t("Mt") for _ in range(2)]
        IM = f32t("IM")
        Xt = [f32t("Xt") for _ in range(2)]
        W0, V0, usb = f32t("W0"), f32t("V0"), f32t("usb")
        attT = tl([128, 128], BF16, "attT")
        khat = tl([128, 128], BF16, "khat")
        wT = tl([128, 128], BF16, "wT")
        qcT = tl([128, 128], BF16, "qcT")
        vn = tl([128, 128], BF16, "vn")
        eg8 = tl([128, 8], F32, "eg8")
        beg = tl([128, 4], F32, "beg")
        egl2 = tl([128, 2], F32, "egl2")
        S = [tl([128, 128], F32, "Sg") for _ in range(4)]
        Sb = [tl([128, 128], BF16, "Sgb") for _ in range(4)]
        of = T(self.wf[0].t[:, 0:512])
        gw = T(self.wf[1].t[:, 0:512])
        sq = T(self.xin[0].t[:, 0:512])
        s4 = tl([128, 4], F32, "s4")
        for dr in range(2):
            for h in range(4):
                P.op('dve', lambda e, h=h: e.memset(S[h].t[:], 0.0), writes=[S[h].b])
                P.op('dve', lambda e, h=h: e.memset(Sb[h].t[:], 0.0), writes=[Sb[h].b])
            order = [0, 1] + list(range(2, 18)) if dr == 0 else [1, 0] + list(range(17, 1, -1))
            order = order[:self.cfg.get('gdn_tiles', 18)]
            if dr >= self.cfg.get('gdn_dirs', 2):
                for t in order:
                    if t >= 2:
                        P.dma('sp', lambda e, t=t: e.dma_start(out=of.t[:], in_=self.OF[t * 128:(t + 1) * 128, :]), reads=[self.b_OF[t]], writes=[of.b])
                        P.op('dve', lambda e, t=t: e.tensor_copy(out=self.Osb[t].t[:, 0:512], in_=of.t[:]), reads=[of.b], writes=[self.Osb[t].b])
                break
            chunks = [0, 1] if dr == 0 else [1, 0]
            T_ = gdc.t[:, dr, 0, :]
            CmT = gdc.t[:, dr, 1, :]
            MS = gdc.t[:, dr, 2, :]
            MIT = gdc.t[:, dr, 3, :]
            for t in order:
                tok = slice(t * 128, (t + 1) * 128)
                lat = t >= 2
                P.op('pe', lambda e, t=t, dr=dr, T_=T_: e.matmul(ps[0].t[:, 0:4], lhsT=T_, rhs=la.t[:, t, dr * 4:dr * 4 + 4], start=True, stop=True), reads=[gdc.b, la.b], writes=[ps[0].b])
                P.op('pe', lambda e, t=t, dr=dr, CmT=CmT: e.matmul(ps[0].t[:, 4:8], lhsT=CmT, rhs=la.t[:, t, dr * 4:dr * 4 + 4], start=True, stop=True), reads=[gdc.b, la.b], writes=[ps[0].b])
                P.op('act', lambda e: e.activation(out=eg8.t[:], in_=ps[0].t[:, 0:8], func=AF.Exp), reads=[ps[0].b], writes=[eg8.b])
                P.op('dve', lambda e, t=t, dr=dr: e.tensor_tensor(out=beg.t[:], in0=eg8.t[:, 0:4], in1=beta.t[:, t, dr * 4:dr * 4 + 4], op=ALU.mult), reads=[eg8.b, beta.b], writes=[beg.b])
                if dr == 1 and lat:
                    P.dma('sp', lambda e, t=t: e.dma_start(out=of.t[:], in_=self.OF[t * 128:(t + 1) * 128, :]), reads=[self.b_OF[t]], writes=[of.b])
                if dr == 0 and self.cfg.get('gdn_cut', 99) < 99:
                    P.op('dve', lambda e: e.memset(of.t[:], 0.0), writes=[of.b])
                for h in range(4):
                    col = dr * 4 + h
                    P.op('dve', lambda e, t=t, col=col, T_=T_: e.tensor_scalar(out=LaT.t[:], in0=T_, scalar1=la.t[:, t, col:col + 1], scalar2=None, op0=ALU.mult), reads=[gdc.b, la.b], writes=[LaT.b])
                    P.op('pool', lambda e, t=t, col=col: e.tensor_scalar(out=labc.t[:], in0=ones.t[:], scalar1=la.t[:, t, col:col + 1], scalar2=None, op0=ALU.mult), reads=[ones.b, la.b], writes=[labc.b])
                    P.op('pe', lambda e, T_=T_: e.matmul(ps[1].t[:, 0:128], lhsT=labc.t[:], rhs=T_, start=True, stop=True), reads=[labc.b, gdc.b], writes=[ps[1].b])
                    P.op('pe', lambda e: e.matmul(ps[1].t[:, 128:130], lhsT=labc.t[:], rhs=gcs.t[:], start=True, stop=True), reads=[labc.b, gcs.b], writes=[ps[1].b])
                    P.op('act', lambda e: e.activation(out=egB.t[:], in_=ps[1].t[:, 0:128], func=AF.Exp), reads=[ps[1].b], writes=[egB.b])
                    P.op('act', lambda e: e.activation(out=egl2.t[:], in_=ps[1].t[:, 128:130], func=AF.Exp), reads=[ps[1].b], writes=[egl2.b])
                    if self.cfg.get('gdn_cut', 99) <= 1:
                        continue

                    P.op('pe', lambda e: e.matmul(ps[2].t[:, 0:128], lhsT=LaT.t[:], rhs=ones.t[:], start=True, stop=False), reads=[LaT.b, ones.b], writes=[ps[2].b])
                    P.op('pe', lambda e: e.matmul(ps[2].t[:, 0:128], lhsT=negones.t[:], rhs=LaT.t[:], start=False, stop=False), reads=[LaT.b, negones.b], writes=[ps[2].b])
                    P.op('pe', lambda e, MS=MS: e.matmul(ps[2].t[:, 0:128], lhsT=ident.t[:], rhs=MS, start=False, stop=True), reads=[ident.b, gdc.b], writes=[ps[2].b])
                    P.op('act', lambda e: e.activation(out=decS.t[:], in_=ps[2].t[:, 0:128], func=AF.Exp), reads=[ps[2].b], writes=[decS.b])
                    P.op('pe', lambda e: e.matmul(ps[3].t[:, 0:128], lhsT=ones.t[:], rhs=LaT.t[:], start=True, stop=False), reads=[LaT.b, ones.b], writes=[ps[3].b])
                    P.op('pe', lambda e: e.matmul(ps[3].t[:, 0:128], lhsT=LaT.t[:], rhs=negones.t[:], start=False, stop=False), reads=[LaT.b, negones.b], writes=[ps[3].b])
                    P.op('pe', lambda e, MIT=MIT: e.matmul(ps[3].t[:, 0:128], lhsT=ident.t[:], rhs=MIT, start=False, stop=True), reads=[ident.b, gdc.b], writes=[ps[3].b])
                    P.op('act', lambda e: e.activation(out=decT.t[:], in_=ps[3].t[:, 0:128], func=AF.Exp), reads=[ps[3].b], writes=[decT.b])
                    if self.cfg.get('gdn_cut', 99) <= 2:
                        continue

                    P.op('pe', lambda e, h=h, tok=tok: e.matmul(ps[4].t[:, 0:128], lhsT=kT[h].t[:, tok], rhs=kT[h].t[:, tok], start=True, stop=True), reads=[kT[h].b], writes=[ps[4].b])
                    P.op('pe', lambda e, h=h, tok=tok: e.matmul(ps[4].t[:, 128:256], lhsT=kT[h].t[:, tok], rhs=qT[h].t[:, tok], start=True, stop=True), reads=[kT[h].b, qT[h].b], writes=[ps[4].b])
                    P.op('dve', lambda e, t=t, col=col: e.scalar_tensor_tensor(out=Mm[0].t[:], in0=ps[4].t[:, 0:128], scalar=nbeta.t[:, t, col:col + 1], in1=decS.t[:], op0=ALU.mult, op1=ALU.mult),
                         reads=[ps[4].b, nbeta.b, decS.b], writes=[Mm[0].b])
                    P.op('dve', lambda e: e.tensor_tensor(out=attT.t[:], in0=ps[4].t[:, 128:256], in1=decT.t[:], op=ALU.mult), reads=[ps[4].b, decT.b], writes=[attT.b])
                    if self.cfg.get('gdn_cut', 99) <= 3:
                        continue

                    P.op('pe', lambda e: e.transpose(out=ps[5].t[:, 0:128], in_=Mm[0].t[:], identity=ident.t[:]), reads=[Mm[0].b, ident.b], writes=[ps[5].b])
                    P.op('act', lambda e: e.activation(out=Mt[0].t[:], in_=ps[5].t[:, 0:128], func=AF.Copy), reads=[ps[5].b], writes=[Mt[0].b])
                    P.op('dve', lambda e: e.tensor_tensor(out=Xt[0].t[:], in0=ps[5].t[:, 0:128], in1=ident.t[:], op=ALU.add), reads=[ps[5].b, ident.b], writes=[Xt[0].b])
                    cur = 0
                    for p in range(1, 6):
                        nxt = 1 - cur
                        P.op('pe', lambda e, cur=cur: e.matmul(ps[5].t[:, 0:128], lhsT=Mt[cur].t[:], rhs=Mm[cur].t[:], start=True, stop=True), reads=[Mt[cur].b, Mm[cur].b], writes=[ps[5].b])
                        if p < 5:
                            P.op('pe', lambda e, cur=cur: e.matmul(ps[5].t[:, 128:256], lhsT=Mm[cur].t[:], rhs=Mt[cur].t[:], start=True, stop=True), reads=[Mt[cur].b, Mm[cur].b], writes=[ps[5].b])
                            P.op('act', lambda e, nxt=nxt: e.activation(out=Mm[nxt].t[:], in_=ps[5].t[:, 0:128], func=AF.Copy), reads=[ps[5].b], writes=[Mm[nxt].b])
                            P.op('act', lambda e, nxt=nxt: e.activation(out=Mt[nxt].t[:], in_=ps[5].t[:, 128:256], func=AF.Copy), reads=[ps[5].b], writes=[Mt[nxt].b])
                        P.op('dve', lambda e: e.tensor_tensor(out=IM.t[:], in0=ps[5].t[:, 0:128], in1=ident.t[:], op=ALU.add), reads=[ps[5].b, ident.b], writes=[IM.b])
                        P.op('pe', lambda e, cur=cur: e.matmul(ps[6].t[:, 0:128], lhsT=IM.t[:], rhs=Xt[cur].t[:], start=True, stop=True), reads=[IM.b, Xt[cur].b], writes=[ps[6].b])
                        P.op('act', lambda e, nxt=nxt: e.activation(out=Xt[nxt].t[:], in_=ps[6].t[:, 0:128], func=AF.Copy), reads=[ps[6].b], writes=[Xt[nxt].b])
                        cur = nxt
                    X = Xt[cur]
                    if self.cfg.get('gdn_cut', 99) <= 4:
                        continue

                    P.op('pe', lambda e, h=h, tok=tok: e.transpose(out=self.psb.t[:, 0:128], in_=kT[h].t[:, tok], identity=self.identb.t[:]), reads=[kT[h].b, self.identb.b], writes=[self.psb.b])
                    P.op('dve', lambda e, h=h: e.tensor_scalar(out=W0.t[:], in0=self.psb.t[:, 0:128], scalar1=beg.t[:, h:h + 1], scalar2=None, op0=ALU.mult), reads=[self.psb.b, beg.b], writes=[W0.b])
                    P.op('dve', lambda e, h=h: e.tensor_scalar(out=khat.t[:], in0=self.psb.t[:, 0:128], scalar1=eg8.t[:, 4 + h:5 + h], scalar2=None, op0=ALU.mult), reads=[self.psb.b, eg8.b], writes=[khat.b])
                    P.op('pool', lambda e, t=t, h=h, col=col: e.tensor_scalar(out=V0.t[:], in0=vtok.t[:, t, h, :], scalar1=beta.t[:, t, col:col + 1], scalar2=None, op0=ALU.mult), reads=[vtok.b, beta.b], writes=[V0.b])
                    P.op('pe', lambda e, X=X: e.matmul(ps[0].t[:, 128:256], lhsT=X.t[:], rhs=V0.t[:], start=True, stop=True), reads=[X.b, V0.b], writes=[ps[0].b])
                    P.op('act', lambda e: e.activation(out=usb.t[:], in_=ps[0].t[:, 128:256], func=AF.Copy), reads=[ps[0].b], writes=[usb.b])
                    P.op('pe', lambda e, X=X: e.matmul(ps[1].t[:, 256:384], lhsT=W0.t[:], rhs=X.t[:], start=True, stop=True), reads=[X.b, W0.b], writes=[ps[1].b])
                    P.op('act', lambda e: e.activation(out=wT.t[:], in_=ps[1].t[:, 256:384], func=AF.Copy), reads=[ps[1].b], writes=[wT.b])
                    P.op('dve', lambda e, h=h, tok=tok: e.tensor_tensor(out=qcT.t[:], in0=qT[h].t[:, tok], in1=egB.t[:], op=ALU.mult), reads=[qT[h].b, egB.b], writes=[qcT.b])
                    if self.cfg.get('gdn_cut', 99) <= 5:
                        continue

                    for c in chunks:
                        r = slice(c * 64, (c + 1) * 64)
                        P.op('pe', lambda e, r=r, h=h: e.matmul(ps[2].t[r, 128:256], lhsT=wT.t[:, r], rhs=Sb[h].t[:], start=True, stop=True), reads=[wT.b, Sb[h].b], writes=[ps[2].b])
                        P.op('dve', lambda e, r=r: e.tensor_tensor(out=vn.t[r, :], in0=usb.t[r, :], in1=ps[2].t[r, 128:256], op=ALU.subtract), reads=[usb.b, ps[2].b], writes=[vn.b])
                        if lat:
                            P.op('pe', lambda e, r=r, h=h: e.matmul(ps[3].t[r, 128:256], lhsT=qcT.t[:, r], rhs=Sb[h].t[:], start=True, stop=False), reads=[qcT.b, Sb[h].b], writes=[ps[3].b])
                            P.op('pe', lambda e, r=r: e.matmul(ps[3].t[r, 128:256], lhsT=attT.t[r, r], rhs=vn.t[r, :], start=False, stop=True), reads=[attT.b, vn.b], writes=[ps[3].b])
                        P.op('pe', lambda e, r=r: e.matmul(ps[4].t[:, 256:384], lhsT=khat.t[r, :], rhs=vn.t[r, :], start=True, stop=True), reads=[khat.b, vn.b], writes=[ps[4].b])
                        P.op('dve', lambda e, h=h, c=c: e.scalar_tensor_tensor(out=S[h].t[:], in0=S[h].t[:], scalar=egl2.t[:, c:c + 1], in1=ps[4].t[:, 256:384], op0=ALU.mult, op1=ALU.add),
                             reads=[S[h].b, egl2.b, ps[4].b], writes=[S[h].b])
                        P.op('act', lambda e, h=h: e.activation(out=Sb[h].t[:], in_=S[h].t[:], func=AF.Copy), reads=[S[h].b], writes=[Sb[h].b])
                        if lat:
                            if dr == 0:
                                P.op('act', lambda e, r=r, h=h: e.activation(out=of.t[r, h * 128:(h + 1) * 128], in_=ps[3].t[r, 128:256], func=AF.Copy), reads=[ps[3].b], writes=[of.b])
                            else:
                                P.op('dve', lambda e, r=r, h=h: e.tensor_tensor(out=of.t[r, h * 128:(h + 1) * 128], in0=of.t[r, h * 128:(h + 1) * 128], in1=ps[3].t[r, 128:256], op=ALU.add),
                                     reads=[ps[3].b, of.b], writes=[of.b])
                if lat and dr == 0:
                    P.dma('sp', lambda e, t=t: e.dma_start(out=self.OF[t * 128:(t + 1) * 128, :], in_=of.t[:]), reads=[of.b], writes=[self.b_OF[t]])
                if lat and dr == 1:
                    P.dma('sp', lambda e, t=t: e.dma_start(out=gw.t[:], in_=self.ZS[t * 128:(t + 1) * 128, :]), reads=[self.b_ZS[t]], writes=[gw.b])
                    P.op('pool', lambda e: e.tensor_tensor(out=gw.t[:].rearrange("p (h d) -> p h d", h=4), in0=gw.t[:].rearrange("p (h d) -> p h d", h=4),
                                                          in1=gnw.t[:].unsqueeze(1).to_broadcast([128, 4, 128]), op=ALU.mult), reads=[gw.b, gnw.b], writes=[gw.b])
                    P.op('pool', lambda e: e.tensor_tensor(out=sq.t[:], in0=of.t[:], in1=of.t[:], op=ALU.mult), reads=[of.b], writes=[sq.b])
                    P.op('dve', lambda e: e.tensor_reduce(out=s4.t[:], in_=sq.t[:].rearrange("p (h d) -> p h d", h=4), axis=AX.X, op=ALU.add), reads=[sq.b], writes=[s4.b])
                    P.op('dve', lambda e: e.tensor_scalar(out=s4.t[:], in0=s4.t[:], scalar1=1.0 / 128, scalar2=EPS, op0=ALU.mult, op1=ALU.add), reads=[s4.b], writes=[s4.b])
                    P.op('act', lambda e: e.activation(out=s4.t[:], in_=s4.t[:], func=AF.Sqrt), reads=[s4.b], writes=[s4.b])
                    P.op('dve', lambda e: e.reciprocal(out=s4.t[:], in_=s4.t[:]), reads=[s4.b], writes=[s4.b])
                    P.op('dve', lambda e: e.tensor_tensor(out=sq.t[:].rearrange("p (h d) -> p h d", h=4), in0=of.t[:].rearrange("p (h d) -> p h d", h=4),
                                                         in1=s4.t[:].unsqueeze(2).to_broadcast([128, 4, 128]), op=ALU.mult), reads=[of.b, s4.b], writes=[sq.b])
                    P.op('dve', lambda e, t=t: e.tensor_tensor(out=self.Osb[t].t[:, 0:512], in0=sq.t[:], in1=gw.t[:], op=ALU.mult), reads=[sq.b, gw.b], writes=[self.Osb[t].b])


class PeerCtx:
    def __init__(self, K):
        tl = K.tile
        self.qT = tl([128, 16, 128], BF16)
        self.s = tl([128, 16, 128], F32)
        self.wk = tl([128, 256], F32)
        self.m = tl([128, 16, 16], F32)
        self.iu = tl([128, 16, 16], U32)
        self.if_ = tl([128, 16, 16], F32)
        self.cand = tl([128, 8, 256], F32)
        self.cidx = tl([128, 8, 256], F32)
        self.best = tl([128, 8, 16], F32)
        self.nmax = tl([128, 8], F32)
        self.e = tl([128, 8, 16], F32)
        self.z = tl([128, 8], F32)
        self.gate = tl([128, 128], F32)
        self.idxf = tl([128, 128], F32)
        self.idxi = tl([128, 128], I32)
        self.junk = tl([128, 1024], F32)
        self.junk2 = tl([128, 256], F32)
        self.apre = tl([128, 128], F32)
        self.ga = tl([128, 128], F32)
        self.NG = 5
        self.gb = [tl([128, 1024], F32, "gb") for _ in range(self.NG)]
        self.gnext = 0


def peer_tile(K, C, l, h2f, h2T, wq, u_tab, v_tab, acc):
    P = K.P
    keysT = K.keysT
    ps = K.ps
    for g in range(4):
        pt = ps[g]
        for j in range(4):
            hp = g * 4 + j
            for k in range(8):
                P.op('pe', lambda e, pt=pt, j=j, hp=hp, k=k: e.matmul(
                    pt.t[:, j * 128:(j + 1) * 128], lhsT=wq.t[:, k, hp * 128:(hp + 1) * 128], rhs=h2T.t[:, k, :],
                    start=(k == 0), stop=(k == 7)), reads=[wq.b, h2T.b], writes=[pt.b])
        P.op('act', lambda e, pt=pt, g=g: e.activation(out=C.qT.t[:, g * 4:(g + 1) * 4, :], in_=pt.t[:].rearrange("p (a b) -> p a b", a=4), func=AF.Copy),
             reads=[pt.b], writes=[C.qT.b])
    for g in range(4):
        pt = ps[g]
        for j in range(4):
            hp = g * 4 + j
            P.op('pe', lambda e, pt=pt, j=j, hp=hp: e.matmul(
                pt.t[:, j * 128:(j + 1) * 128], lhsT=C.qT.t[:, hp, :], rhs=keysT.t[:, l, hp, :], start=True, stop=True),
                reads=[C.qT.b, keysT.b], writes=[pt.b])
        P.op('act', lambda e, pt=pt, g=g: e.activation(out=C.s.t[:, g * 4:(g + 1) * 4, :], in_=pt.t[:].rearrange("p (a b) -> p a b", a=4), func=AF.Copy),
             reads=[pt.b], writes=[C.s.b])
    for hp in range(16):
        P.op('dve', lambda e, hp=hp: e.max(out=C.m.t[:, hp, 0:8], in_=C.s.t[:, hp, :]), reads=[C.s.b], writes=[C.m.b])
        P.op('dve', lambda e, hp=hp: e.max_index(out=C.iu.t[:, hp, 0:8], in_max=C.m.t[:, hp, 0:8], in_values=C.s.t[:, hp, :]),
             reads=[C.s.b, C.m.b], writes=[C.iu.b])
        P.op('dve', lambda e, hp=hp: e.match_replace(out=C.wk.t[:, 0:128], in_to_replace=C.m.t[:, hp, 0:8], in_values=C.s.t[:, hp, :], imm_value=-1e30),
             reads=[C.s.b, C.m.b], writes=[C.wk.b])
        P.op('dve', lambda e, hp=hp: e.max(out=C.m.t[:, hp, 8:16], in_=C.wk.t[:, 0:128]), reads=[C.wk.b], writes=[C.m.b])
        P.op('dve', lambda e, hp=hp: e.max_index(out=C.iu.t[:, hp, 8:16], in_max=C.m.t[:, hp, 8:16], in_values=C.wk.t[:, 0:128]),
             reads=[C.wk.b, C.m.b], writes=[C.iu.b])
    P.op('dve', lambda e: e.tensor_copy(out=C.if_.t[:], in_=C.iu.t[:]), reads=[C.iu.b], writes=[C.if_.b])
    m4 = C.m.t[:].rearrange("p (h t) k -> p h t k", t=2)
    if4 = C.if_.t[:].rearrange("p (h t) k -> p h t k", t=2)
    P.op('dve', lambda e: e.tensor_scalar(out=if4[:, :, 0, :], in0=if4[:, :, 0, :], scalar1=128.0, scalar2=None, op0=ALU.mult),
         reads=[C.if_.b], writes=[C.if_.b])
    cand4 = C.cand.t[:].rearrange("p h (a b) -> p h a b", a=16)
    cidx4 = C.cidx.t[:].rearrange("p h (a b) -> p h a b", a=16)
    for h in range(8):
        P.op('dve', lambda e, h=h: e.tensor_tensor(out=cand4[:, h], in0=m4[:, h, 0, :].unsqueeze(2).to_broadcast([128, 16, 16]),
                                                  in1=m4[:, h, 1, :].unsqueeze(1).to_broadcast([128, 16, 16]), op=ALU.add),
             reads=[C.m.b], writes=[C.cand.b])
        P.op('dve', lambda e, h=h: e.tensor_tensor(out=cidx4[:, h], in0=if4[:, h, 0, :].unsqueeze(2).to_broadcast([128, 16, 16]),
                                                  in1=if4[:, h, 1, :].unsqueeze(1).to_broadcast([128, 16, 16]), op=ALU.add),
             reads=[C.if_.b], writes=[C.cidx.b])
    for h in range(8):
        P.op('dve', lambda e, h=h: e.max(out=C.best.t[:, h, 0:8], in_=C.cand.t[:, h, :]), reads=[C.cand.b], writes=[C.best.b])
        P.op('dve', lambda e, h=h: e.match_replace(out=C.wk.t[:], in_to_replace=C.best.t[:, h, 0:8], in_values=C.cand.t[:, h, :], imm_value=-1e30),
             reads=[C.cand.b, C.best.b], writes=[C.wk.b])
        P.op('dve', lambda e, h=h: e.max(out=C.best.t[:, h, 8:16], in_=C.wk.t[:]), reads=[C.wk.b], writes=[C.best.b])
    for h in range(8):
        for k in range(16):
            sl = h * 16 + k
            P.op('dve', lambda e, h=h, k=k, sl=sl: e.scalar_tensor_tensor(
                out=C.junk2.t[:], in0=C.cand.t[:, h, :], scalar=C.best.t[:, h, k:k + 1], in1=C.cidx.t[:, h, :],
                op0=ALU.is_equal, op1=ALU.mult, accum_out=C.idxf.t[:, sl:sl + 1]),
                reads=[C.cand.b, C.best.b, C.cidx.b], writes=[C.junk2.b, C.idxf.b])
    P.op('dve', lambda e: e.tensor_scalar(out=C.idxf.t[:], in0=C.idxf.t[:], scalar1=float(NEXP - 1), scalar2=0.0, op0=ALU.min, op1=ALU.max),
         reads=[C.idxf.b], writes=[C.idxf.b])
    if l > 0:
        P.op('dve', lambda e: e.tensor_scalar(out=C.idxf.t[:], in0=C.idxf.t[:], scalar1=float(l * NEXP), scalar2=None, op0=ALU.add),
             reads=[C.idxf.b], writes=[C.idxf.b])
    P.op('dve', lambda e: e.tensor_copy(out=C.idxi.t[:], in_=C.idxf.t[:]), reads=[C.idxf.b], writes=[C.idxi.b])
    P.op('dve', lambda e: e.tensor_scalar(out=C.nmax.t[:], in0=C.best.t[:, :, 0], scalar1=-1.0, scalar2=None, op0=ALU.mult),
         reads=[C.best.b], writes=[C.nmax.b])
    for h in range(8):
        P.op('act', lambda e, h=h: e.activation(out=C.e.t[:, h, :], in_=C.best.t[:, h, :], func=AF.Exp, bias=C.nmax.t[:, h:h + 1], scale=1.0,
                                                accum_out=C.z.t[:, h:h + 1]),
             reads=[C.best.b, C.nmax.b], writes=[C.e.b, C.z.b])
    P.op('dve', lambda e: e.reciprocal(out=C.z.t[:], in_=C.z.t[:]), reads=[C.z.b], writes=[C.z.b])
    P.op('dve', lambda e: e.tensor_tensor(out=C.gate.t[:].rearrange("p (h k) -> p h k", h=8), in0=C.e.t[:],
                                          in1=C.z.t[:].unsqueeze(2).to_broadcast([128, 8, 16]), op=ALU.mult),
         reads=[C.e.b, C.z.b], writes=[C.gate.b])
    for sl in range(128):
        gb = C.gb[C.gnext]
        C.gnext = (C.gnext + 1) % C.NG
        P.dma('pool', lambda e, gb=gb, sl=sl: e.indirect_dma_start(
            out=gb.t[:], out_offset=None, in_=u_tab,
            in_offset=bass.IndirectOffsetOnAxis(ap=C.idxi.t[:, sl:sl + 1], axis=0)),
            reads=[C.idxi.b], writes=[gb.b])
        P.op('dve', lambda e, gb=gb, sl=sl: e.scalar_tensor_tensor(
            out=C.junk.t[:], in0=h2f.t[:], scalar=1.0, in1=gb.t[:], op0=ALU.mult, op1=ALU.mult, accum_out=C.apre.t[:, sl:sl + 1]),
            reads=[h2f.b, gb.b], writes=[C.junk.b, C.apre.b])
    P.op('act', lambda e: e.activation(out=C.ga.t[:], in_=C.apre.t[:], func=AF.Gelu), reads=[C.apre.b], writes=[C.ga.b])
    P.op('dve', lambda e: e.tensor_tensor(out=C.ga.t[:], in0=C.ga.t[:], in1=C.gate.t[:], op=ALU.mult), reads=[C.ga.b, C.gate.b], writes=[C.ga.b])
    for sl in range(128):
        gb = C.gb[C.gnext]
        C.gnext = (C.gnext + 1) % C.NG
        P.dma('pool', lambda e, gb=gb, sl=sl: e.indirect_dma_start(
            out=gb.t[:], out_offset=None, in_=v_tab,
            in_offset=bass.IndirectOffsetOnAxis(ap=C.idxi.t[:, sl:sl + 1], axis=0)),
            reads=[C.idxi.b], writes=[gb.b])
        if sl == 0:
            P.op('dve', lambda e, gb=gb, sl=sl: e.tensor_scalar(out=acc.t[:], in0=gb.t[:], scalar1=C.ga.t[:, sl:sl + 1], scalar2=None, op0=ALU.mult),
                 reads=[gb.b, C.ga.b], writes=[acc.b])
        else:
            P.op('dve', lambda e, gb=gb, sl=sl: e.scalar_tensor_tensor(
                out=acc.t[:], in0=gb.t[:], scalar=C.ga.t[:, sl:sl + 1], in1=acc.t[:], op0=ALU.mult, op1=ALU.add),
                reads=[gb.b, C.ga.b, acc.b], writes=[acc.b])


_CACHE = {}


def make_in_maps(inputs, n_cores=8):
    consts = make_consts()
    maps = []
    for i in range(n_cores):
        m = {}
        sl = slice(i * NB, (i + 1) * NB)
        m['x'] = np.ascontiguousarray(inputs['x'][sl])
        m['ctx'] = np.ascontiguousarray(inputs['ctx'][sl])
        c5 = np.concatenate([inputs['c'][sl], inputs['c_ctx'][None, :]], axis=0)
        m['c5T'] = np.ascontiguousarray(c5.T.reshape(8, 128, 5).transpose(1, 0, 2))
        for k in IN_SHAPES:
            if k in ('x', 'ctx', 'c5T'):
                continue
            m[k] = np.ascontiguousarray(inputs[k], dtype=np.float32)
        m.update(consts)
        maps.append(m)
    return maps


def kernel(**inputs):
    inputs = {k: np.asarray(v) for k, v in inputs.items()}
    if 'nc' not in _CACHE:
        _CACHE['nc'] = Kern({}).build()
    nc = _CACHE['nc']
    maps = make_in_maps(inputs)
    res = run_bass_kernel_spmd(nc, maps, core_ids=list(range(8)))
    return np.concatenate([r['out'] for r in res.results], axis=0).astype(np.float32)
```

```python
import math
import numpy as np
import concourse.bass as bass
import concourse.mybir as mybir
from concourse.bass_utils import run_bass_kernel_spmd
from contextlib import ExitStack

F32 = mybir.dt.float32
BF16 = mybir.dt.bfloat16
I32 = mybir.dt.int32
U32 = mybir.dt.uint32
AF = mybir.ActivationFunctionType
ALU = mybir.AluOpType
AX = mybir.AxisListType

ENG = ['pe', 'dve', 'act', 'pool', 'sp']
CH = 30000
NEXP = 16384
D = 1024
NT = 2304
LCTX = 256
TLAT = 2048
EPS = 1e-6
NEG = -30000.0
NB = 4


class Buf:
    __slots__ = ('w', 'r', 'x')

    def __init__(self):
        self.w = None
        self.r = []
        self.x = False


class _Rec:
    def __init__(self):
        self.call = None

    def __getattr__(self, name):
        def f(*a, **k):
            self.call = (name, a, k)
            return self
        return f


def _record(fn):
    r = _Rec()
    fn(r)
    assert r.call is not None
    return r.call


class Prog:
    def __init__(self, nc, stack, n_dma_slots=32, n_chunks=8):
        self.nc = nc
        self.stack = stack
        self.ops = {e: [] for e in ENG}
        self.count = {e: 0 for e in ENG}
        self.sems = {e: [stack.enter_context(nc.semaphore(f"s_{e}_{i}")) for i in range(n_chunks)] for e in ENG}
        self.dsem = [stack.enter_context(nc.semaphore(f"d{i}")) for i in range(n_dma_slots)]
        self.dval = [0] * n_dma_slots
        self.dnext = 0
        self.waited = {e: {} for e in ENG}
        self.same_sync = {'pe': False, 'dve': True, 'act': True, 'pool': True, 'sp': False}
        self.uid = 0

    def sb(self, shape, dt, name=None):
        self.uid += 1
        return self.stack.enter_context(self.nc.sbuf_tensor(f"{name or 't'}_{self.uid}", shape, dt))

    def ps(self, shape, dt, name=None):
        self.uid += 1
        return self.stack.enter_context(self.nc.psum_tensor(f"{name or 'p'}_{self.uid}", shape, dt))

    def _deps(self, reads, writes, e=None):
        deps = []
        for b in reads:
            if b.w is not None:
                deps.append(b.w)
            if b.x:
                deps.extend(r for r in b.r if r[1] != e)
        for b in writes:
            if b.w is not None:
                deps.append(b.w)
            deps.extend(b.r)
        return deps

    def _emit_waits(self, e, deps):
        for d in deps:
            if d[0] == 'e':
                if d[1] == e and not self.same_sync[e]:
                    continue
                key = ('e', d[1])
                val = d[2]
                if self.waited[e].get(key, 0) >= val:
                    continue
                self.waited[e][key] = val
                k, v = (val - 1) // CH, (val - 1) % CH + 1
                self.ops[e].append(('wait', self.sems[d[1]][k], v))
            else:
                key = ('d', d[1])
                val = d[2]
                if self.waited[e].get(key, 0) >= val:
                    continue
                self.waited[e][key] = val
                self.ops[e].append(('wait', self.dsem[d[1]], val))

    def op(self, e, fn, reads=(), writes=()):
        self._emit_waits(e, self._deps(reads, writes, e))
        self.count[e] += 1
        n = self.count[e]
        k = (n - 1) // CH
        self.ops[e].append(('op', _record(fn), self.sems[e][k], 1))
        tok = ('e', e, n)
        for b in reads:
            b.r.append(tok)
        for b in writes:
            b.w = tok
            b.r = []

    def dma(self, q, fn, reads=(), writes=()):
        self._emit_waits(q, self._deps(reads, writes, q))
        s = self.dnext
        self.dnext = (self.dnext + 1) % len(self.dsem)
        prev = self.dval[s]
        if prev > 0:
            self._emit_waits(q, [('d', s, prev)])
        self.dval[s] = prev + 16
        self.ops[q].append(('op', _record(fn), self.dsem[s], 16))
        tok = ('d', s, prev + 16)
        for b in reads:
            b.r.append(tok)
        for b in writes:
            b.w = tok
            b.r = []

    def barrier(self):
        deps = [('e', e2, self.count[e2]) for e2 in ENG if self.count[e2] > 0]
        deps += [('d', s, v) for s, v in enumerate(self.dval) if v > 0]
        for e in ENG:
            saved = self.same_sync[e]
            self.same_sync[e] = True
            self._emit_waits(e, deps)
            self.same_sync[e] = saved

    def emit(self):
        nc = self.nc
        with nc.Block() as block:
            def run(e):
                def body(eng):
                    for item in self.ops[e]:
                        if item[0] == 'wait':
                            eng.wait_ge(item[1], item[2])
                        else:
                            name, a, k = item[1]
                            getattr(eng, name)(*a, **k).then_inc(item[2], item[3])
                return body
            block.tensor(run('pe'))
            block.vector(run('dve'))
            block.scalar(run('act'))
            block.gpsimd(run('pool'))
            block.sync(run('sp'))


class T:
    def __init__(self, t):
        self.t = t
        self.b = Buf()

    def __getitem__(self, k):
        return self.t[k]


def make_consts():
    c = {}
    c['ident'] = np.eye(128, dtype=np.float32)
    t = np.arange(TLAT)
    row = (t // 64).astype(np.float32)
    col = (t % 64).astype(np.float32)
    inv = (np.float32(10000.0) ** (-np.arange(16, dtype=np.float32) / np.float32(16))).astype(np.float32)
    ar = row[:, None] * inv[None, :]
    ac = col[:, None] * inv[None, :]
    ang = np.concatenate([ar, ar, ac, ac], axis=-1).astype(np.float32)
    cos = np.cos(ang).astype(np.float32).T
    sin = np.sin(ang).astype(np.float32).T
    sign = np.ones((64, 1), np.float32)
    sign[0:16] = -1.0
    sign[32:48] = -1.0
    ssin = sin * sign
    c['rope'] = np.ascontiguousarray(np.stack([np.concatenate([cos, cos], 0), np.concatenate([ssin, ssin], 0)], axis=1))
    s = np.arange(128)[:, None]
    u = np.arange(128)[None, :]
    cs, cu = s // 32, u // 32
    same = (cs == cu)
    hg = np.zeros((128, 2, 4, 128), np.float32)
    tri_f = same & (s <= u)
    tri_b = same & (s >= u)
    mid_f = same & (s <= 32 * cu + 16)
    mid_b = same & (s >= 32 * cu + 15)
    hg[:, 0, 0] = tri_f.astype(np.float32) - mid_f
    hg[:, 1, 0] = tri_b.astype(np.float32) - mid_b
    hg[:, 0, 1] = tri_f
    hg[:, 1, 1] = tri_b
    hg[:, 0, 2] = same.astype(np.float32) - tri_f
    hg[:, 1, 2] = same.astype(np.float32) - tri_b
    hg[:, 0, 3] = tri_f
    hg[:, 1, 3] = tri_b
    c['hgc'] = hg
    c['hcsel'] = (np.arange(128)[:, None] // 32 == np.arange(4)[None, :]).astype(np.float32)
    cs, cu = s // 64, u // 64
    same = (cs == cu)
    gd = np.zeros((128, 2, 4, 128), np.float32)
    t_f = same & (s <= u)
    t_b = same & (s >= u)
    gd[:, 0, 0] = t_f
    gd[:, 1, 0] = t_b
    gd[:, 0, 1] = same.astype(np.float32) - t_f
    gd[:, 1, 1] = same.astype(np.float32) - t_b
    gd[:, 0, 2] = np.where(same & (s > u), 0.0, NEG)
    gd[:, 1, 2] = np.where(same & (s < u), 0.0, NEG)
    gd[:, 0, 3] = np.where(same & (s <= u), 0.0, NEG)
    gd[:, 1, 3] = np.where(same & (s >= u), 0.0, NEG)
    c['gdc'] = gd
    c['gcsel'] = (np.arange(128)[:, None] // 64 == np.arange(2)[None, :]).astype(np.float32)
    sw = np.zeros((128, 2, 128), np.float32)
    sw[:, 0] = (u <= s)
    sw[:, 1] = (s <= u)
    c['swm'] = sw
    return c


CONST_SHAPES = {'ident': [128, 128], 'rope': [128, 2, TLAT], 'hgc': [128, 2, 4, 128], 'hcsel': [128, 4],
                'gdc': [128, 2, 4, 128], 'gcsel': [128, 2], 'swm': [128, 2, 128]}

IN_SHAPES = {
    'x': [NB, TLAT, D], 'ctx': [NB, LCTX, D], 'c5T': [128, 8, 5],
    'ada_w': [2, D, 6 * D], 'ada_b': [2, 6 * D], 'norm_w': [2, 2, D], 'final_norm_w': [D],
    'even_w_in': [1, D, 4096], 'even_w_out': [1, D, D], 'diff_lambda': [1, 4, 64], 'diff_subln_w': [1, 128],
    'hgrn_lb': [3, 512], 'hgrn_norm_w': [1, 64], 'odd_w_in': [1, D, 2832], 'odd_w_out': [1, D, D],
    'gdn_conv_w': [1, 5, 1536], 'gdn_a_log': [1, 2, 4], 'gdn_dt_bias': [1, 2, 4], 'gdn_norm_w': [1, 128],
    'swa_sink': [1, 8], 'peer_w_q': [2, D, 2048], 'peer_keys': [2, 8, 2, 128, 128],
    'peer_u': [2, NEXP, D], 'peer_v': [2, NEXP, D],
}


class Kern:
    def __init__(self, cfg):
        self.cfg = cfg
        nc = bass.Bass("TRN2", target_bir_lowering=False)
        self.nc = nc
        self.I = {}
        for k, shp in list(IN_SHAPES.items()) + list(CONST_SHAPES.items()):
            self.I[k] = nc.dram_tensor(k, shp, F32, kind="ExternalInput").ap()
        self.out = nc.dram_tensor("out", [NB, TLAT, D], F32, kind="ExternalOutput").ap()
        dbg = cfg.get('dbg', False)
        kind = "ExternalOutput" if dbg else "Internal"
        self.XS = nc.dram_tensor("XS", [NB, NT, D], F32, kind=kind).ap()
        self.MOD = nc.dram_tensor("MOD", [2, 5, 6 * D], F32, kind=kind).ap()
        self.OF = nc.dram_tensor("OF", [NT, 512], F32, kind="Internal").ap()
        self.b_XS = [[Buf() for _ in range(18)] for _ in range(NB)]
        self.b_MOD = Buf()
        self.b_OF = [Buf() for _ in range(18)]
        self.ZS = nc.dram_tensor("ZS", [NT, 512], F32, kind="Internal").ap()
        self.b_ZS = [Buf() for _ in range(18)]
        self.UB = nc.dram_tensor("UB", [2 * NEXP, D], BF16, kind="Internal").ap()
        self.VB = nc.dram_tensor("VB", [2 * NEXP, D], BF16, kind="Internal").ap()
        self.b_tab = Buf()
        self.b_out = Buf()
        if dbg:
            self.DBG = nc.dram_tensor("DBG", [NT, D], F32, kind="ExternalOutput").ap()
            self.b_DBG = Buf()

    def tile(self, shape, dt, name=None):
        return T(self.P.sb(shape, dt, name))

    def load_bcast(self, dst, src1d, q='sp', reads=()):
        self.P.dma(q, lambda e: e.dma_start(out=dst.t[:], in_=src1d.partition_broadcast(128)), reads=list(reads), writes=[dst.b])

    def load_w(self, dst, src2d, c0, n, d0=0):
        self.P.dma('pool', lambda e: e.dma_start(out=dst.t[:, :, d0:d0 + n], in_=src2d[:, c0:c0 + n].rearrange("(k p) c -> p k c", p=128)),
                   writes=[dst.b])

    def rstd_from_ss(self, ss, n, out):
        P = self.P
        P.op('dve', lambda e: e.tensor_scalar(out=out.t[:], in0=ss.t[:], scalar1=1.0 / n, scalar2=EPS, op0=ALU.mult, op1=ALU.add),
             reads=[ss.b], writes=[out.b])
        P.op('act', lambda e: e.activation(out=out.t[:], in_=out.t[:], func=AF.Sqrt), reads=[out.b], writes=[out.b])
        P.op('dve', lambda e: e.reciprocal(out=out.t[:], in_=out.t[:]), reads=[out.b], writes=[out.b])

    def x_src(self, l, b, t):
        if l == 0:
            if t < 2:
                return self.I['ctx'][b, t * 128:(t + 1) * 128, :], None
            return self.I['x'][b, (t - 2) * 128:(t - 1) * 128, :], None
        return self.XS[b, t * 128:(t + 1) * 128, :], self.b_XS[b][t]

    def transpose_to(self, src_bf, dst_ap_fn, dst_buf, nblk=8):
        P = self.P
        for k in range(nblk):
            P.op('pe', lambda e, k=k: e.transpose(out=self.psb.t[:, k * 128:(k + 1) * 128], in_=src_bf.t[:, k * 128:(k + 1) * 128], identity=self.identb.t[:]),
                 reads=[src_bf.b, self.identb.b], writes=[self.psb.b])
        P.op('act', lambda e: e.activation(out=dst_ap_fn(), in_=self.psb.t[:, 0:nblk * 128].rearrange("p (a b) -> p a b", a=nblk), func=AF.Copy),
             reads=[self.psb.b], writes=[dst_buf])

    def build(self):
        nc = self.nc
        cfg = self.cfg
        with ExitStack() as st:
            P = Prog(nc, st)
            self.P = P
            self.ps = [T(P.ps([128, 512], F32)) for _ in range(7)]
            self.psb = T(P.ps([128, 1024], BF16))
            for p_ in self.ps + [self.psb]:
                p_.b.x = True
            self.setup()
            for l in cfg.get('layers', [0, 1]):
                for b in cfg.get('batches', list(range(NB))):
                    self.layer_batch(l, b)
            if cfg.get('final', True):
                self.final_norm()
            if cfg.get('dbg', False):
                pass
            P.barrier()
            P.emit()
            self.counts = {e: (P.count[e], len(P.ops[e])) for e in ENG}
        return nc

    def setup(self):
        P = self.P
        I = self.I
        self.ident = self.tile([128, 128], F32)
        self.identb = self.tile([128, 128], BF16)
        self.ones = self.tile([128, 128], F32)
        self.negones = self.tile([128, 128], F32)
        self.keysT = self.tile([128, 2, 16, 128], BF16)
        self.modbig = self.tile([128, 6 * D], F32, "modbig")
        self.mod = [T(self.modbig.t[:, i * D:(i + 1) * D]) for i in range(6)]
        self.xin = [self.tile([128, D], F32, "xin") for _ in range(2)]
        self.wf = [self.tile([128, D], F32, "wf") for _ in range(2)]
        self.wb = [self.tile([128, D], BF16, "wb") for _ in range(2)]
        self.ss = self.tile([128, 8], F32)
        self.rs = self.tile([128, 8], F32)
        self.xi = 0
        P.dma('sp', lambda e: e.dma_start(out=self.ident.t[:], in_=I['ident']), writes=[self.ident.b])
        P.op('dve', lambda e: e.tensor_copy(out=self.identb.t[:], in_=self.ident.t[:]), reads=[self.ident.b], writes=[self.identb.b])
        P.op('dve', lambda e: e.memset(self.ones.t[:], 1.0), writes=[self.ones.b])
        P.op('dve', lambda e: e.memset(self.negones.t[:], -1.0), writes=[self.negones.b])
        with ExitStack() as ph:
            P.stack, saved = ph, P.stack
            kf = self.tile([128, 16, 128], F32)
            for l in range(2):
                P.dma('sp', lambda e, l=l: e.dma_start(out=kf.t[:], in_=I['peer_keys'][l].rearrange("h t k d -> k (h t) d")), writes=[kf.b])
                for g in range(4):
                    pt = self.ps[g]
                    for j in range(4):
                        hp = g * 4 + j
                        P.op('pe', lambda e, pt=pt, j=j, hp=hp: e.transpose(out=pt.t[:, j * 128:(j + 1) * 128], in_=kf.t[:, hp, :], identity=self.ident.t[:]),
                             reads=[kf.b, self.ident.b], writes=[pt.b])
                    P.op('act', lambda e, pt=pt, g=g, l=l: e.activation(out=self.keysT.t[:, l, g * 4:(g + 1) * 4, :], in_=pt.t[:].rearrange("p (a b) -> p a b", a=4), func=AF.Copy),
                         reads=[pt.b], writes=[self.keysT.b])
            stg = [self.tile([128, 8, D], BF16, "stg") for _ in range(2)]
            si = 0
            for tab, dst in (('peer_u', self.UB), ('peer_v', self.VB)):
                src = I[tab].rearrange("l e d -> (l e) d")
                for ch in range(2 * NEXP // 1024):
                    stt = stg[si % 2]
                    si += 1
                    P.dma('pool', lambda e, stt=stt, src=src, ch=ch: e.dma_start(out=stt.t[:], in_=src[ch * 1024:(ch + 1) * 1024, :].rearrange("(p j) d -> p j d", j=8)), writes=[stt.b])
                    P.dma('sp', lambda e, stt=stt, dst=dst, ch=ch: e.dma_start(out=dst[ch * 1024:(ch + 1) * 1024, :].rearrange("(p j) d -> p j d", j=8), in_=stt.t[:]),
                          reads=[stt.b], writes=[self.b_tab])
            c5 = self.tile([128, 8, 5], F32)
            P.dma('sp', lambda e: e.dma_start(out=c5.t[:], in_=I['c5T']), writes=[c5.b])
            P.op('act', lambda e: e.activation(out=c5.t[:], in_=c5.t[:], func=AF.Silu), reads=[c5.b], writes=[c5.b])
            wst = [self.tile([128, 3072], F32, "adaw") for _ in range(2)]
            bias = self.tile([5, 3072], F32)
            msb = self.tile([5, 3072], F32)
            wi = 0
            for l in range(2):
                for half in range(2):
                    c0 = half * 3072
                    P.dma('sp', lambda e, l=l, c0=c0: e.dma_start(out=bias.t[:], in_=I['ada_b'][l, c0:c0 + 3072].partition_broadcast(5)), writes=[bias.b])
                    for k in range(8):
                        w = wst[wi % 2]
                        wi += 1
                        P.dma('sp', lambda e, w=w, l=l, k=k, c0=c0: e.dma_start(out=w.t[:], in_=I['ada_w'][l, k * 128:(k + 1) * 128, c0:c0 + 3072]), writes=[w.b])
                        for cg in range(6):
                            pt = self.ps[cg]
                            P.op('pe', lambda e, pt=pt, w=w, k=k, cg=cg: e.matmul(pt.t[0:5, :], lhsT=c5.t[:, k, :], rhs=w.t[:, cg * 512:(cg + 1) * 512],
                                                                                 start=(k == 0), stop=(k == 7)), reads=[c5.b, w.b], writes=[pt.b])
                    for cg in range(6):
                        pt = self.ps[cg]
                        P.op('dve', lambda e, pt=pt, cg=cg: e.tensor_tensor(out=msb.t[:, cg * 512:(cg + 1) * 512], in0=pt.t[0:5, :], in1=bias.t[:, cg * 512:(cg + 1) * 512], op=ALU.add),
                             reads=[pt.b, bias.b], writes=[msb.b])
                    P.dma('sp', lambda e, l=l, c0=c0: e.dma_start(out=self.MOD[l, :, c0:c0 + 3072], in_=msb.t[:]), reads=[msb.b], writes=[self.b_MOD])
            P.barrier()
            P.stack = saved

    def load_mod(self, l, r, which, dstA, dstS, tmp_sc, tmp_nw):
        P = self.P
        j_sh, j_sc = (0, 1) if which == 0 else (3, 4)
        self.load_bcast(dstS, self.MOD[l, r, j_sh * D:(j_sh + 1) * D], reads=[self.b_MOD])
        self.P.dma('sp', lambda e: e.dma_start(out=tmp_sc.t[:], in_=self.MOD[l, r, j_sc * D:(j_sc + 1) * D].partition_broadcast(128)),
                   reads=[self.b_MOD], writes=[tmp_sc.b])
        self.load_bcast(tmp_nw, self.I['norm_w'][l, which, :])
        P.op('dve', lambda e: e.scalar_tensor_tensor(out=dstA.t[:], in0=tmp_sc.t[:], scalar=1.0, in1=tmp_nw.t[:], op0=ALU.add, op1=ALU.mult),
             reads=[tmp_sc.b, tmp_nw.b], writes=[dstA.b])

    def load_gate(self, l, r, which, dst):
        j = 2 if which == 0 else 5
        self.P.dma('sp', lambda e: e.dma_start(out=dst.t[:], in_=self.MOD[l, r, j * D:(j + 1) * D].partition_broadcast(128)),
                   reads=[self.b_MOD], writes=[dst.b])

    def norm_mod(self, xt, A, S, out_f=None, out_b=None):
        P = self.P
        junk = self.wb[0]
        P.op('act', lambda e: e.activation(out=junk.t[:], in_=xt.t[:], func=AF.Square, accum_out=self.ss.t[:, 0:1]),
             reads=[xt.b], writes=[junk.b, self.ss.b])
        P.op('dve', lambda e: e.tensor_scalar(out=self.rs.t[:, 0:1], in0=self.ss.t[:, 0:1], scalar1=1.0 / D, scalar2=EPS, op0=ALU.mult, op1=ALU.add),
             reads=[self.ss.b], writes=[self.rs.b])
        P.op('act', lambda e: e.activation(out=self.rs.t[:, 0:1], in_=self.rs.t[:, 0:1], func=AF.Sqrt), reads=[self.rs.b], writes=[self.rs.b])
        P.op('dve', lambda e: e.reciprocal(out=self.rs.t[:, 0:1], in_=self.rs.t[:, 0:1]), reads=[self.rs.b], writes=[self.rs.b])
        tmp = self.wf[0]
        P.op('dve', lambda e: e.scalar_tensor_tensor(out=tmp.t[:], in0=xt.t[:], scalar=self.rs.t[:, 0:1], in1=A.t[:], op0=ALU.mult, op1=ALU.mult),
             reads=[xt.b, self.rs.b, A.b], writes=[tmp.b])
        if out_f is not None:
            P.op('pool', lambda e: e.tensor_tensor(out=out_f.t[:], in0=tmp.t[:], in1=S.t[:], op=ALU.add), reads=[tmp.b, S.b], writes=[out_f.b])
            if out_b is not None:
                P.op('act', lambda e: e.activation(out=out_b.t[:], in_=out_f.t[:], func=AF.Copy), reads=[out_f.b], writes=[out_b.b])
        else:
            P.op('pool', lambda e: e.tensor_tensor(out=out_b.t[:], in0=tmp.t[:], in1=S.t[:], op=ALU.add), reads=[tmp.b, S.b], writes=[out_b.b])

    def layer_batch(self, l, b):
        P = self.P
        cfg = self.cfg
        tiles = list(range(18)) if l == 0 else list(range(2, 18))
        with ExitStack() as ph:
            P.stack, saved = ph, P.stack
            self.hT = self.tile([128, 8, NT], BF16, "hT")
            self.Osb = [self.tile([128, D], BF16, "Osb") if (l == 0 or t_ >= 2) else None for t_ in range(18)]
            Al, Sl, Ac, Sc = self.mod[0], self.mod[1], self.mod[2], self.mod[3]
            self.load_mod(l, b, 0, Al, Sl, self.mod[4], self.mod[5])
            self.load_mod(l, 4, 0, Ac, Sc, self.mod[4], self.mod[5])
            for t in range(18):
                src, sbuf = self.x_src(l, b, t)
                xt = self.xin[self.xi % 2]
                self.xi += 1
                P.dma('sp', lambda e, xt=xt, src=src: e.dma_start(out=xt.t[:], in_=src), reads=[sbuf] if sbuf else [], writes=[xt.b])
                hb = self.wb[1]
                self.norm_mod(xt, Ac if t < 2 else Al, Sc if t < 2 else Sl, out_b=hb)
                self.transpose_to(hb, lambda t=t: self.hT.t[:, :, t * 128:(t + 1) * 128], self.hT.b)
            if cfg.get('mixer', True):
                with ExitStack() as mx:
                    P.stack = mx
                    if l == 0:
                        self.mixer_even(b)
                    else:
                        self.mixer_odd(b)
                    P.barrier()
                P.stack = ph
            else:
                for t in tiles:
                    P.op('pool', lambda e, t=t: e.memset(self.Osb[t].t[:], 0.0), writes=[self.Osb[t].b])
            wout = self.tile([128, 8, D], BF16, "wout")
            self.load_w(wout, self.I['even_w_out' if l == 0 else 'odd_w_out'][0], 0, D)
            G1l, G1c = self.mod[0], self.mod[1]
            self.load_gate(l, b, 0, G1l)
            self.load_gate(l, 4, 0, G1c)
            oT = self.tile([128, 8, 128], BF16, "oT")
            for t in tiles:
                G = G1c if t < 2 else G1l
                self.transpose_to(self.Osb[t], lambda: oT.t[:], oT.b)
                src, sbuf = self.x_src(l, b, t)
                xt = self.xin[self.xi % 2]
                self.xi += 1
                P.dma('sp', lambda e, xt=xt, src=src: e.dma_start(out=xt.t[:], in_=src), reads=[sbuf] if sbuf else [], writes=[xt.b])
                for half in range(2):
                    pt = self.ps[half]
                    for k in range(8):
                        P.op('pe', lambda e, pt=pt, k=k, half=half: e.matmul(pt.t[:], lhsT=oT.t[:, k, :], rhs=wout.t[:, k, half * 512:(half + 1) * 512],
                                                                           start=(k == 0), stop=(k == 7)), reads=[oT.b, wout.b], writes=[pt.b])
                    tmp = self.wf[0]
                    P.op('dve', lambda e, pt=pt, half=half, G=G, tmp=tmp: e.tensor_tensor(out=tmp.t[:, half * 512:(half + 1) * 512], in0=pt.t[:], in1=G.t[:, half * 512:(half + 1) * 512], op=ALU.mult),
                         reads=[pt.b, G.b], writes=[tmp.b])
                P.op('pool', lambda e, xt=xt: e.tensor_tensor(out=xt.t[:], in0=xt.t[:], in1=self.wf[0].t[:], op=ALU.add), reads=[xt.b, self.wf[0].b], writes=[xt.b])
                P.dma('sp', lambda e, xt=xt, t=t: e.dma_start(out=self.XS[b, t * 128:(t + 1) * 128, :], in_=xt.t[:]), reads=[xt.b], writes=[self.b_XS[b][t]])
            P.barrier()
            P.stack = saved
        if cfg.get('peer', True):
            with ExitStack() as ph:
                P.stack, saved = ph, P.stack
                self.peer_phase(l, b, tiles)
                P.barrier()
                P.stack = saved

    def final_norm(self):
        P = self.P
        fw_ = self.mod[0]
        self.load_bcast(fw_, self.I['final_norm_w'])
        for b in self.cfg.get('batches', list(range(NB))):
            for t in range(2, 18):
                xt = self.xin[self.xi % 2]
                self.xi += 1
                P.dma('sp', lambda e, xt=xt, b=b, t=t: e.dma_start(out=xt.t[:], in_=self.XS[b, t * 128:(t + 1) * 128, :]), reads=[self.b_XS[b][t]], writes=[xt.b])
                junk = self.wb[0]
                P.op('act', lambda e, xt=xt: e.activation(out=junk.t[:], in_=xt.t[:], func=AF.Square, accum_out=self.ss.t[:, 0:1]),
                     reads=[xt.b], writes=[junk.b, self.ss.b])
                P.op('dve', lambda e: e.tensor_scalar(out=self.rs.t[:, 0:1], in0=self.ss.t[:, 0:1], scalar1=1.0 / D, scalar2=EPS, op0=ALU.mult, op1=ALU.add),
                     reads=[self.ss.b], writes=[self.rs.b])
                P.op('act', lambda e: e.activation(out=self.rs.t[:, 0:1], in_=self.rs.t[:, 0:1], func=AF.Sqrt), reads=[self.rs.b], writes=[self.rs.b])
                P.op('dve', lambda e: e.reciprocal(out=self.rs.t[:, 0:1], in_=self.rs.t[:, 0:1]), reads=[self.rs.b], writes=[self.rs.b])
                o = self.wf[1]
                P.op('dve', lambda e, xt=xt: e.scalar_tensor_tensor(out=o.t[:], in0=xt.t[:], scalar=self.rs.t[:, 0:1], in1=fw_.t[:], op0=ALU.mult, op1=ALU.mult),
                     reads=[xt.b, self.rs.b, fw_.b], writes=[o.b])
                P.dma('sp', lambda e, b=b, t=t: e.dma_start(out=self.out[b, (t - 2) * 128:(t - 1) * 128, :], in_=o.t[:]), reads=[o.b], writes=[self.b_out])

    def peer_phase(self, l, b, tiles):
        P = self.P
        I = self.I
        wq = self.tile([128, 8, 2048], BF16, "wq")
        self.load_w(wq, I['peer_w_q'][l], 0, 2048)
        A2l, S2l, G2l, A2c, S2c, G2c = self.mod
        tmp_sc, tmp_nw = self.wf[0], self.wf[1]
        self.load_mod(l, b, 1, A2l, S2l, tmp_sc, tmp_nw)
        self.load_gate(l, b, 1, G2l)
        if l == 0:
            self.load_mod(l, 4, 1, A2c, S2c, tmp_sc, tmp_nw)
            self.load_gate(l, 4, 1, G2c)
        C = PeerCtx(self)
        h2f = self.tile([128, D], F32, "h2f")
        h2b = self.wb[1]
        h2T = self.tile([128, 8, 128], BF16, "h2T")
        acc = self.tile([128, D], F32, "acc")
        u_tab, v_tab = self.UB, self.VB
        for t in tiles:
            A, S, G = (A2c, S2c, G2c) if t < 2 else (A2l, S2l, G2l)
            xt = self.xin[self.xi % 2]
            self.xi += 1
            P.dma('sp', lambda e, xt=xt, t=t: e.dma_start(out=xt.t[:], in_=self.XS[b, t * 128:(t + 1) * 128, :]), reads=[self.b_XS[b][t]], writes=[xt.b])
            self.norm_mod(xt, A, S, out_f=h2f, out_b=h2b)
            self.transpose_to(h2b, lambda: h2T.t[:], h2T.b)
            peer_tile(self, C, l, h2f, h2T, wq, u_tab, v_tab, acc)
            P.op('dve', lambda e, G=G: e.tensor_tensor(out=acc.t[:], in0=acc.t[:], in1=G.t[:], op=ALU.mult), reads=[acc.b, G.b], writes=[acc.b])
            P.op('pool', lambda e, xt=xt: e.tensor_tensor(out=xt.t[:], in0=xt.t[:], in1=acc.t[:], op=ALU.add), reads=[xt.b, acc.b], writes=[xt.b])
            P.dma('sp', lambda e, xt=xt, t=t: e.dma_start(out=self.XS[b, t * 128:(t + 1) * 128, :], in_=xt.t[:]), reads=[xt.b], writes=[self.b_XS[b][t]])

    def proj_tok(self, t, w, c0, n, pt):
        for k in range(8):
            self.P.op('pe', lambda e, k=k: e.matmul(pt.t[:, 0:n], lhsT=self.hT.t[:, k, t * 128:(t + 1) * 128], rhs=w.t[:, k, c0:c0 + n],
                                                  start=(k == 0), stop=(k == 7)), reads=[self.hT.b, w.b], writes=[pt.b])

    def proj_feat(self, w, c0, t0, n, pt):
        for k in range(8):
            self.P.op('pe', lambda e, k=k: e.matmul(pt.t[:, 0:n], lhsT=w.t[:, k, c0:c0 + 128], rhs=self.hT.t[:, k, t0:t0 + n],
                                                  start=(k == 0), stop=(k == 7)), reads=[self.hT.b, w.b], writes=[pt.b])

    def perm_rope_w(self, w, wp, ncols):
        src = w.t[:, :, 0:ncols].rearrange("p k (g h s) -> p k g h s", h=2, s=16)
        dst = wp.t[:, :, 0:ncols].rearrange("p k (g h s) -> p k g h s", h=2, s=16)
        for hf in range(2):
            self.P.op('pool', lambda e, hf=hf: e.tensor_copy(out=dst[:, :, :, hf, :], in_=src[:, :, :, 1 - hf, :]), reads=[w.b], writes=[wp.b])

    def rope_evac(self, pa, pb, rope, r0, n, dst_ap, dst_buf):
        P = self.P
        t1, t2 = self.wf[0], self.wf[1]
        P.op('dve', lambda e: e.tensor_tensor(out=t1.t[:, 0:n], in0=pa.t[:, 0:n], in1=rope.t[:, 0, r0:r0 + n], op=ALU.mult), reads=[pa.b, rope.b], writes=[t1.b])
        P.op('dve', lambda e: e.tensor_tensor(out=t2.t[:, 0:n], in0=pb.t[:, 0:n], in1=rope.t[:, 1, r0:r0 + n], op=ALU.mult), reads=[pb.b, rope.b], writes=[t2.b])
        P.op('pool', lambda e: e.tensor_tensor(out=dst_ap, in0=t1.t[:, 0:n], in1=t2.t[:, 0:n], op=ALU.add), reads=[t1.b, t2.b], writes=[dst_buf])

    def dump_osb(self):
        for t in range(18):
            if self.Osb[t] is None:
                continue
            self.P.dma('pool', lambda e, t=t: e.dma_start(out=self.DBG[t * 128:(t + 1) * 128, :], in_=self.Osb[t].t[:]), reads=[self.Osb[t].b], writes=[self.b_DBG])

    def mixer_even(self, b):
        P = self.P
        win = self.I['even_w_in'][0]
        parts = self.cfg.get('parts', 'ab')
        if 'a' in parts:
            with ExitStack() as sc:
                P.stack, saved = sc, P.stack
                self.attn_A(b, win)
                P.barrier()
            P.stack = saved
        else:
            for t in range(18):
                P.op('pool', lambda e, t=t: e.memset(self.Osb[t].t[:, 0:512], 0.0), writes=[self.Osb[t].b])
        if 'b' in parts:
            with ExitStack() as sc:
                P.stack, saved = sc, P.stack
                self.hgrn(b, win)
                P.barrier()
            P.stack = saved
        else:
            for t in range(18):
                P.op('pool', lambda e, t=t: e.memset(self.Osb[t].t[:, 512:1024], 0.0), writes=[self.Osb[t].b])
        if self.cfg.get('dump_osb', False):
            self.dump_osb()

    def attn_A(self, b, win):
        P = self.P
        I = self.I
        tl = self.tile
        lambda_init = 0.8 - 0.6 * math.exp(-0.3 * 0)
        rope = tl([128, 2, TLAT], F32, "rope")
        P.dma('sp', lambda e: e.dma_start(out=rope.t[:], in_=I['rope']), writes=[rope.b])
        sm = tl([128, 8], F32, "sm")
        dl = tl([128, 256], F32, "dl")
        self.load_bcast(dl, I['diff_lambda'][0].rearrange("a d -> (a d)"))
        junk = tl([128, 128], F32, "junkA")
        P.op('dve', lambda e: e.scalar_tensor_tensor(out=junk.t[:, 0:64], in0=dl.t[:, 0:64], scalar=1.0, in1=dl.t[:, 64:128], op0=ALU.mult, op1=ALU.mult, accum_out=sm.t[:, 0:1]),
             reads=[dl.b], writes=[junk.b, sm.b])
        P.op('dve', lambda e: e.scalar_tensor_tensor(out=junk.t[:, 0:64], in0=dl.t[:, 128:192], scalar=1.0, in1=dl.t[:, 192:256], op0=ALU.mult, op1=ALU.mult, accum_out=sm.t[:, 1:2]),
             reads=[dl.b], writes=[junk.b, sm.b])
        P.op('act', lambda e: e.activation(out=sm.t[:, 0:2], in_=sm.t[:, 0:2], func=AF.Exp), reads=[sm.b], writes=[sm.b])
        P.op('dve', lambda e: e.tensor_tensor(out=sm.t[:, 2:3], in0=sm.t[:, 1:2], in1=sm.t[:, 0:1], op=ALU.subtract), reads=[sm.b], writes=[sm.b])
        P.op('dve', lambda e: e.tensor_scalar(out=sm.t[:, 2:3], in0=sm.t[:, 2:3], scalar1=-lambda_init, scalar2=None, op0=ALU.add), reads=[sm.b], writes=[sm.b])
        subw = tl([128, 128], F32, "subw")
        self.load_bcast(subw, I['diff_subln_w'][0])
        P.op('dve', lambda e: e.tensor_scalar(out=subw.t[:], in0=subw.t[:], scalar1=1.0 - lambda_init, scalar2=None, op0=ALU.mult), reads=[subw.b], writes=[subw.b])
        V1 = tl([128, 18, 4, 132], BF16, "V1")
        P.op('pool', lambda e: e.memset(V1.t[:], 1.0), writes=[V1.b])
        with ExitStack() as sc:
            P.stack, saved = sc, P.stack
            wv = tl([128, 8, 512], BF16, "wv")
            self.load_w(wv, win, 1024, 512)
            for t in range(18):
                pt = self.ps[t % 2]
                self.proj_tok(t, wv, 0, 512, pt)
                P.op('act', lambda e, t=t, pt=pt: e.activation(out=V1.t[:, t, :, 0:128], in_=pt.t[:].rearrange("p (h d) -> p h d", h=4), func=AF.Copy),
                     reads=[pt.b], writes=[V1.b])
            P.barrier()
        P.stack = saved
        wqk = tl([128, 8, 256], BF16, "wqk")
        wqkp = tl([128, 8, 256], BF16, "wqkp")
        qT = tl([128, NT], BF16, "qTa")
        kT = tl([128, NT], BF16, "kTa")
        PT = [tl([128, 512], BF16, "PT") for _ in range(3)]
        A0 = tl([128, 4, 132], F32, "A0")
        o1 = tl([128, 128], F32, "o1")
        oo = tl([128, 128], F32, "oo")
        blocks = [(0, 256)] + [(256 + i * 512, 512) for i in range(4)]
        for hd in range(4):
            self.load_w(wqk, win, hd * 128, 128, d0=0)
            self.load_w(wqk, win, 512 + hd * 128, 128, d0=128)
            self.perm_rope_w(wqk, wqkp, 256)
            for dstT, c0 in ((qT, 0), (kT, 128)):
                for (t0, n) in blocks:
                    pa, pb = self.ps[0], self.ps[1]
                    self.proj_feat(wqk, c0, t0, n, pa)
                    if t0 == 0:
                        P.op('act', lambda e, dstT=dstT, pa=pa: e.activation(out=dstT.t[:, 0:256], in_=pa.t[:, 0:256], func=AF.Copy), reads=[pa.b], writes=[dstT.b])
                    else:
                        self.proj_feat(wqkp, c0, t0, n, pb)
                        self.rope_evac(pa, pb, rope, t0 - 256, n, dstT.t[:, t0:t0 + n], dstT.b)
            pti = 0
            for (q0, nq) in blocks:
                nqs = nq // 128
                kts = list(range(2)) if q0 == 0 else list(range(18))
                for sub in range(2):
                    p0 = sub * 64
                    for ki, kt in enumerate(kts):
                        pS = self.ps[ki % 2]
                        P.op('pe', lambda e, pS=pS, kt=kt, p0=p0, q0=q0, nq=nq: e.matmul(pS.t[:, 0:nq], lhsT=kT.t[p0:p0 + 64, kt * 128:(kt + 1) * 128],
                                                                                       rhs=qT.t[p0:p0 + 64, q0:q0 + nq], start=True, stop=True),
                             reads=[kT.b, qT.b], writes=[pS.b])
                        pt_ = PT[pti % 3]
                        pti += 1
                        P.op('act', lambda e, pS=pS, pt_=pt_, nq=nq: e.activation(out=pt_.t[:, 0:nq], in_=pS.t[:, 0:nq], func=AF.Exp, scale=0.125),
                             reads=[pS.b], writes=[pt_.b])
                        for qs in range(nqs):
                            po = self.ps[2 + qs]
                            P.op('pe', lambda e, po=po, pt_=pt_, qs=qs, kt=kt, hd=hd, ki=ki, nk=len(kts): e.matmul(
                                po.t[:, 0:129], lhsT=pt_.t[:, qs * 128:(qs + 1) * 128], rhs=V1.t[:, kt, hd, 0:129], start=(ki == 0), stop=(ki == nk - 1)),
                                reads=[pt_.b, V1.b], writes=[po.b])
                    for qs in range(nqs):
                        po = self.ps[2 + qs]
                        if sub == 0:
                            P.op('act', lambda e, po=po, qs=qs: e.activation(out=A0.t[:, qs, 0:129], in_=po.t[:, 0:129], func=AF.Copy), reads=[po.b], writes=[A0.b])
                        else:
                            tt = (q0 + qs * 128) // 128
                            P.op('dve', lambda e, qs=qs: e.reciprocal(out=sm.t[:, 3:4], in_=A0.t[:, qs, 128:129]), reads=[A0.b], writes=[sm.b])
                            P.op('dve', lambda e, po=po: e.reciprocal(out=sm.t[:, 4:5], in_=po.t[:, 128:129]), reads=[po.b], writes=[sm.b])
                            P.op('dve', lambda e: e.tensor_tensor(out=sm.t[:, 4:5], in0=sm.t[:, 4:5], in1=sm.t[:, 2:3], op=ALU.mult), reads=[sm.b], writes=[sm.b])
                            P.op('dve', lambda e, qs=qs: e.tensor_scalar(out=o1.t[:], in0=A0.t[:, qs, 0:128], scalar1=sm.t[:, 3:4], scalar2=None, op0=ALU.mult),
                                 reads=[A0.b, sm.b], writes=[o1.b])
                            P.op('dve', lambda e, po=po: e.scalar_tensor_tensor(out=oo.t[:], in0=po.t[:, 0:128], scalar=sm.t[:, 4:5], in1=o1.t[:], op0=ALU.mult, op1=ALU.add),
                                 reads=[po.b, sm.b, o1.b], writes=[oo.b])
                            P.op('act', lambda e: e.activation(out=junk.t[:], in_=oo.t[:], func=AF.Square, accum_out=sm.t[:, 5:6]), reads=[oo.b], writes=[junk.b, sm.b])
                            P.op('dve', lambda e: e.tensor_scalar(out=sm.t[:, 6:7], in0=sm.t[:, 5:6], scalar1=1.0 / 128, scalar2=EPS, op0=ALU.mult, op1=ALU.add), reads=[sm.b], writes=[sm.b])
                            P.op('act', lambda e: e.activation(out=sm.t[:, 6:7], in_=sm.t[:, 6:7], func=AF.Sqrt), reads=[sm.b], writes=[sm.b])
                            P.op('dve', lambda e: e.reciprocal(out=sm.t[:, 6:7], in_=sm.t[:, 6:7]), reads=[sm.b], writes=[sm.b])
                            P.op('dve', lambda e, tt=tt, hd=hd: e.scalar_tensor_tensor(out=self.Osb[tt].t[:, hd * 128:(hd + 1) * 128], in0=oo.t[:], scalar=sm.t[:, 6:7], in1=subw.t[:],
                                                                                    op0=ALU.mult, op1=ALU.mult), reads=[oo.b, sm.b, subw.b], writes=[self.Osb[tt].b])

    def mixer_odd(self, b):
        P = self.P
        win = self.I['odd_w_in'][0]
        parts = self.cfg.get('parts', 'cd')
        if 'd' in parts:
            with ExitStack() as sc:
                P.stack, saved = sc, P.stack
                self.swa(b, win)
                P.barrier()
            P.stack = saved
        else:
            for t in range(2, 18):
                P.op('pool', lambda e, t=t: e.memset(self.Osb[t].t[:, 512:1024], 0.0), writes=[self.Osb[t].b])
        if 'c' in parts:
            with ExitStack() as sc:
                P.stack, saved = sc, P.stack
                self.gdn(b, win)
                P.barrier()
            P.stack = saved
        else:
            for t in range(2, 18):
                P.op('pool', lambda e, t=t: e.memset(self.Osb[t].t[:, 0:512], 0.0), writes=[self.Osb[t].b])
        if self.cfg.get('dump_osb', False):
            self.dump_osb()

    def swa(self, b, win):
        P = self.P
        I = self.I
        tl = self.tile
        rope = tl([128, 2, TLAT], F32, "rope")
        P.dma('sp', lambda e: e.dma_start(out=rope.t[:], in_=I['rope']), writes=[rope.b])
        swm = tl([128, 2, 128], F32, "swm")
        P.dma('sp', lambda e: e.dma_start(out=swm.t[:], in_=I['swm']), writes=[swm.b])
        esink = tl([128, 8], F32, "esink")
        self.load_bcast(esink, I['swa_sink'][0])
        P.op('act', lambda e: e.activation(out=esink.t[:], in_=esink.t[:], func=AF.Exp), reads=[esink.b], writes=[esink.b])
        sm = tl([128, 8], F32, "smD")
        wq = tl([128, 8, 512], BF16, "wqd")
        wqp = tl([128, 8, 512], BF16, "wqdp")
        for g in range(4):
            for kv in range(2):
                self.load_w(wq, win, 2064 + (kv * 4 + g) * 64, 64, d0=g * 128 + kv * 64)
        self.perm_rope_w(wq, wqp, 512)
        wkv = tl([128, 8, 256], BF16, "wkvd")
        wkp = tl([128, 8, 256], BF16, "wkdp")
        self.load_w(wkv, win, 2576, 256)
        self.perm_rope_w(wkv, wkp, 128)
        V1 = tl([128, 18, 2, 66], BF16, "V1d")
        P.op('pool', lambda e: e.memset(V1.t[:], 1.0), writes=[V1.b])
        for t in range(18):
            pt = self.ps[t % 2]
            self.proj_tok(t, wkv, 128, 128, pt)
            P.op('act', lambda e, t=t, pt=pt: e.activation(out=V1.t[:, t, :, 0:64], in_=pt.t[:, 0:128].rearrange("p (h d) -> p h d", h=2), func=AF.Copy),
                 reads=[pt.b], writes=[V1.b])
        kT = tl([128, NT], BF16, "kTd")
        qT = tl([128, 4, TLAT], BF16, "qTd")
        blocks = [(0, 256)] + [(256 + i * 512, 512) for i in range(4)]
        for (t0, n) in blocks:
            pa, pb = self.ps[0], self.ps[1]
            self.proj_feat(wkv, 0, t0, n, pa)
            if t0 == 0:
                P.op('act', lambda e, pa=pa: e.activation(out=kT.t[:, 0:256], in_=pa.t[:, 0:256], func=AF.Copy), reads=[pa.b], writes=[kT.b])
            else:
                self.proj_feat(wkp, 0, t0, n, pb)
                self.rope_evac(pa, pb, rope, t0 - 256, n, kT.t[:, t0:t0 + n], kT.b)
        for g in range(4):
            for (t0, n) in blocks[1:]:
                pa, pb = self.ps[0], self.ps[1]
                self.proj_feat(wq, g * 128, t0, n, pa)
                self.proj_feat(wqp, g * 128, t0, n, pb)
                self.rope_evac(pa, pb, rope, t0 - 256, n, qT.t[:, g, t0 - 256:t0 - 256 + n], qT.b)
        PT = [tl([128, 512], BF16, "PTd") for _ in range(5)]
        zz = tl([128, 4], F32, "zz")
        for kv in range(2):
            p0 = kv * 64
            for i in range(16):
                keys = [(0, None), (1, None)]
                if i > 0:
                    keys.append((2 + i - 1, 0))
                keys.append((2 + i, None))
                if i < 15:
                    keys.append((2 + i + 1, 1))
                for n_, (kt, msk) in enumerate(keys):
                    pS = self.ps[n_ % 2]
                    P.op('pe', lambda e, pS=pS, kt=kt, p0=p0, i=i: e.matmul(pS.t[:, 0:512], lhsT=kT.t[p0:p0 + 64, kt * 128:(kt + 1) * 128],
                                                                         rhs=qT.t[p0:p0 + 64, :, i * 128:(i + 1) * 128], start=True, stop=True),
                         reads=[kT.b, qT.b], writes=[pS.b])
                    pt_ = PT[n_]
                    P.op('act', lambda e, pS=pS, pt_=pt_: e.activation(out=pt_.t[:], in_=pS.t[:], func=AF.Exp, scale=0.125), reads=[pS.b], writes=[pt_.b])
                    if msk is not None:
                        P.op('pool', lambda e, pt_=pt_, msk=msk: e.tensor_tensor(out=pt_.t[:].rearrange("p (g q) -> p g q", g=4), in0=pt_.t[:].rearrange("p (g q) -> p g q", g=4),
                                                                             in1=swm.t[:, msk, :].unsqueeze(1).to_broadcast([128, 4, 128]), op=ALU.mult),
                             reads=[pt_.b, swm.b], writes=[pt_.b])
                po = self.ps[2]
                for g in range(4):
                    for n_, (kt, msk) in enumerate(keys):
                        P.op('pe', lambda e, g=g, n_=n_, kt=kt, kv=kv, nk=len(keys): e.matmul(po.t[:, g * 66:g * 66 + 65], lhsT=PT[n_].t[:, g * 128:(g + 1) * 128],
                                                                                          rhs=V1.t[:, kt, kv, 0:65], start=(n_ == 0), stop=(n_ == nk - 1)),
                             reads=[PT[n_].b, V1.b], writes=[po.b])
                po3 = po.t[:, 0:264].rearrange("p (g c) -> p g c", c=66)
                P.op('dve', lambda e, kv=kv, po3=po3: e.tensor_tensor(out=zz.t[:], in0=po3[:, :, 64], in1=esink.t[:, kv * 4:(kv + 1) * 4], op=ALU.add),
                     reads=[po.b, esink.b], writes=[zz.b])
                P.op('dve', lambda e: e.reciprocal(out=zz.t[:], in_=zz.t[:]), reads=[zz.b], writes=[zz.b])
                ot = self.Osb[2 + i]
                P.op('dve', lambda e, ot=ot, kv=kv, po3=po3: e.tensor_tensor(out=ot.t[:, 512 + kv * 256:512 + (kv + 1) * 256].rearrange("p (g d) -> p g d", g=4), in0=po3[:, :, 0:64],
                                                                          in1=zz.t[:].unsqueeze(2).to_broadcast([128, 4, 64]), op=ALU.mult),
                     reads=[po.b, zz.b], writes=[ot.b])

    def hgrn(self, b, win):
        P = self.P
        I = self.I
        tl = self.tile
        wB = tl([128, 8, 2048], BF16, "wB")
        self.load_w(wB, win, 1536, 1024, d0=0)
        self.load_w(wB, win, 3584, 512, d0=1536)
        hgc = tl([128, 2, 4, 128], F32, "hgc")
        P.dma('sp', lambda e: e.dma_start(out=hgc.t[:], in_=I['hgc']), writes=[hgc.b])
        hcs = tl([128, 4], F32, "hcs")
        P.dma('sp', lambda e: e.dma_start(out=hcs.t[:], in_=I['hcsel']), writes=[hcs.b])
        lb = tl([128, 512], F32, "lb")
        oml = tl([128, 512], F32, "oml")
        e1_, e2_ = self.wf[0], self.wf[1]
        self.load_bcast(lb, I['hgrn_lb'][0])
        self.load_bcast(e1_, I['hgrn_lb'][1:3].rearrange("a d -> (a d)"))
        P.op('act', lambda e: e.activation(out=lb.t[:], in_=lb.t[:], func=AF.Exp), reads=[lb.b], writes=[lb.b])
        P.op('act', lambda e: e.activation(out=e1_.t[:], in_=e1_.t[:], func=AF.Exp), reads=[e1_.b], writes=[e1_.b])
        P.op('dve', lambda e: e.tensor_tensor(out=oml.t[:], in0=e1_.t[:, 0:512], in1=e1_.t[:, 512:1024], op=ALU.add), reads=[e1_.b], writes=[oml.b])
        P.op('dve', lambda e: e.tensor_tensor(out=oml.t[:], in0=oml.t[:], in1=lb.t[:], op=ALU.add), reads=[oml.b, lb.b], writes=[oml.b])
        P.op('dve', lambda e: e.reciprocal(out=oml.t[:], in_=oml.t[:]), reads=[oml.b], writes=[oml.b])
        P.op('dve', lambda e: e.tensor_tensor(out=lb.t[:], in0=lb.t[:], in1=oml.t[:], op=ALU.mult), reads=[oml.b, lb.b], writes=[lb.b])
        P.op('dve', lambda e: e.tensor_scalar(out=oml.t[:], in0=lb.t[:], scalar1=-1.0, scalar2=1.0, op0=ALU.mult, op1=ALU.add), reads=[lb.b], writes=[oml.b])
        hnw = tl([128, 64], F32, "hnw")
        self.load_bcast(hnw, I['hgrn_norm_w'][0])
        S = tl([128, 4, 64], F32, "S")
        Sb = tl([128, 4, 64], BF16, "Sb")
        ebl = tl([128, 4, 4], F32, "ebl")
        qf = tl([128, 512], F32, "qf")
        vb = tl([128, 512], BF16, "vb")
        ff = tl([128, 512], F32, "ff")
        lf = tl([128, 512], F32, "lf")
        kk = tl([128, 512], F32, "kk")
        ee = [tl([128, 512], F32, "ee") for _ in range(2)]
        qt = tl([128, 512], BF16, "qt")
        kt_ = tl([128, 512], BF16, "kt")
        qc = tl([128, 512], BF16, "qc")
        kh = tl([128, 512], BF16, "kh")
        qtT = tl([128, 4, 128], BF16, "qtT")
        ktT = tl([128, 4, 128], BF16, "ktT")
        qcT = tl([128, 4, 128], BF16, "qcT")
        att = [tl([128, 128], BF16, "att") for _ in range(2)]
        qcTz = tl([128, 4, 64], BF16, "qcTz")
        P.op('pool', lambda e: e.memset(qcTz.t[:], 0.0), writes=[qcTz.b])
        khz = tl([128, 512], BF16, "khz")
        rm3 = tl([128, 1], F32, "rm3")
        P.op('dve', lambda e: e.tensor_scalar(out=rm3.t[:], in0=hcs.t[:, 2:3], scalar1=-1.0, scalar2=1.0, op0=ALU.mult, op1=ALU.add), reads=[hcs.b], writes=[rm3.b])
        of = tl([128, 512], F32, "of")
        gw = tl([128, 512], F32, "gw")
        sq = tl([128, 512], F32, "sq")
        s8 = tl([128, 8], F32, "s8")
        ps = self.ps
        for dr in range(2):
            self.load_w(wB, win, 2560 + dr * 512, 512, d0=1024)
            P.op('dve', lambda e: e.memset(S.t[:], 0.0), writes=[S.b])
            P.op('dve', lambda e: e.memset(Sb.t[:], 0.0), writes=[Sb.b])
            order = [0, 1] + list(range(2, 18)) if dr == 0 else [1, 0] + list(range(17, 1, -1))
            chunks = [0, 1, 2, 3] if dr == 0 else [3, 2, 1, 0]
            for t in order:
                self.proj_tok(t, wB, 0, 512, ps[0])
                P.op('act', lambda e: e.activation(out=qf.t[:], in_=ps[0].t[:], func=AF.Silu), reads=[ps[0].b], writes=[qf.b])
                self.proj_tok(t, wB, 512, 512, ps[1])
                P.op('act', lambda e: e.activation(out=vb.t[:], in_=ps[1].t[:], func=AF.Copy), reads=[ps[1].b], writes=[vb.b])
                self.proj_tok(t, wB, 1024, 512, ps[2])
                P.op('act', lambda e: e.activation(out=ff.t[:], in_=ps[2].t[:], func=AF.Sigmoid), reads=[ps[2].b], writes=[ff.b])
                P.op('dve', lambda e: e.tensor_tensor(out=ff.t[:], in0=ff.t[:], in1=oml.t[:], op=ALU.mult), reads=[ff.b, oml.b], writes=[ff.b])
                P.op('dve', lambda e: e.tensor_tensor(out=ff.t[:], in0=ff.t[:], in1=lb.t[:], op=ALU.add), reads=[ff.b, lb.b], writes=[ff.b])
                P.op('act', lambda e: e.activation(out=lf.t[:], in_=ff.t[:], func=AF.Ln), reads=[ff.b], writes=[lf.b])
                P.op('pool', lambda e: e.tensor_scalar(out=kk.t[:], in0=ff.t[:], scalar1=-1.0, scalar2=1.0, op0=ALU.mult, op1=ALU.add), reads=[ff.b], writes=[kk.b])
                for j in range(3):
                    P.op('pe', lambda e, j=j, dr=dr: e.matmul(ps[3 + j].t[:], lhsT=hgc.t[:, dr, j, :], rhs=lf.t[:], start=True, stop=True),
                         reads=[hgc.b, lf.b], writes=[ps[3 + j].b])
                P.op('act', lambda e: e.activation(out=ee[0].t[:], in_=ps[3].t[:], func=AF.Exp), reads=[ps[3].b], writes=[ee[0].b])
                P.op('dve', lambda e: e.scalar_tensor_tensor(out=qt.t[:], in0=qf.t[:], scalar=0.125, in1=ee[0].t[:], op0=ALU.mult, op1=ALU.mult),
                     reads=[qf.b, ee[0].b], writes=[qt.b])
                P.op('act', lambda e: e.activation(out=ee[1].t[:], in_=ps[3].t[:], func=AF.Exp, scale=-1.0), reads=[ps[3].b], writes=[ee[1].b])
                P.op('pool', lambda e: e.tensor_tensor(out=kt_.t[:], in0=kk.t[:], in1=ee[1].t[:], op=ALU.mult), reads=[kk.b, ee[1].b], writes=[kt_.b])
                P.op('act', lambda e: e.activation(out=ee[0].t[:], in_=ps[4].t[:], func=AF.Exp), reads=[ps[4].b], writes=[ee[0].b])
                P.op('dve', lambda e: e.scalar_tensor_tensor(out=qc.t[:], in0=qf.t[:], scalar=0.125, in1=ee[0].t[:], op0=ALU.mult, op1=ALU.mult),
                     reads=[qf.b, ee[0].b], writes=[qc.b])
                P.op('act', lambda e: e.activation(out=ee[1].t[:], in_=ps[5].t[:], func=AF.Exp), reads=[ps[5].b], writes=[ee[1].b])
                P.op('pool', lambda e: e.tensor_tensor(out=kh.t[:], in0=kk.t[:], in1=ee[1].t[:], op=ALU.mult), reads=[kk.b, ee[1].b], writes=[kh.b])
                self.transpose_to(qt, lambda: qtT.t[:], qtT.b, nblk=4)
                self.transpose_to(kt_, lambda: ktT.t[:], ktT.b, nblk=4)
                self.transpose_to(qc, lambda: qcT.t[:], qcT.b, nblk=4)
                P.op('pool', lambda e: e.tensor_copy(out=qcTz.t[:, :, 32:64], in_=qcT.t[:, :, 96:128]), reads=[qcT.b], writes=[qcTz.b])
                P.op('pool', lambda e: e.tensor_scalar(out=khz.t[:], in0=kh.t[:], scalar1=rm3.t[:, 0:1], scalar2=None, op0=ALU.mult), reads=[kh.b, rm3.b], writes=[khz.b])
                for hp in range(4):
                    P.op('pe', lambda e, hp=hp: e.matmul(ps[6].t[:, 64 + hp * 4:64 + (hp + 1) * 4], lhsT=lf.t[:, hp * 128:(hp + 1) * 128], rhs=hcs.t[:], start=True, stop=True),
                         reads=[lf.b, hcs.b], writes=[ps[6].b])
                P.op('act', lambda e: e.activation(out=ebl.t[:], in_=ps[6].t[:, 64:80].rearrange("p (a c) -> p a c", a=4), func=AF.Exp), reads=[ps[6].b], writes=[ebl.b])
                for hp in range(4):
                    for half in range(2):
                        h = hp * 2 + half
                        p0 = half * 64
                        pa = ps[half]
                        P.op('pe', lambda e, pa=pa, p0=p0, hp=hp: e.matmul(pa.t[:, 0:128], lhsT=ktT.t[p0:p0 + 64, hp, :], rhs=qtT.t[p0:p0 + 64, hp, :], start=True, stop=True),
                             reads=[ktT.b, qtT.b], writes=[pa.b])
                        P.op('dve', lambda e, pa=pa, half=half, dr=dr: e.tensor_tensor(out=att[half].t[:], in0=pa.t[:, 0:128], in1=hgc.t[:, dr, 3, :], op=ALU.mult),
                             reads=[pa.b, hgc.b], writes=[att[half].b])
                        po = ps[2 + half]
                        P.op('pe', lambda e, po=po, half=half, hp=hp, h=h: e.matmul(po.t[:, hp * 64:(hp + 1) * 64], lhsT=att[half].t[:], rhs=vb.t[:, h * 64:(h + 1) * 64], start=True, stop=False),
                             reads=[att[half].b, vb.b], writes=[po.b])
                    for ci, c in enumerate(chunks):
                        for half in range(2):
                            p0 = half * 64
                            po = ps[2 + half]
                            if c < 3:
                                P.op('pe', lambda e, po=po, p0=p0, hp=hp, c=c, ci=ci: e.matmul(po.t[c * 32:(c + 1) * 32, hp * 64:(hp + 1) * 64], lhsT=qcT.t[p0:p0 + 64, hp, c * 32:(c + 1) * 32],
                                                                                         rhs=Sb.t[p0:p0 + 64, hp, :], start=False, stop=(ci == 3)),
                                     reads=[qcT.b, Sb.b], writes=[po.b])
                            else:
                                P.op('pe', lambda e, po=po, p0=p0, hp=hp, c=c, ci=ci: e.matmul(po.t[64:128, hp * 64:(hp + 1) * 64], lhsT=qcTz.t[p0:p0 + 64, hp, :],
                                                                                         rhs=Sb.t[p0:p0 + 64, hp, :], start=False, stop=(ci == 3)),
                                     reads=[qcTz.b, Sb.b], writes=[po.b])
                        for half in range(2):
                            h = hp * 2 + half
                            p0 = half * 64
                            if c < 3:
                                P.op('pe', lambda e, p0=p0, c=c, h=h: e.matmul(ps[6].t[p0:p0 + 64, 0:64], lhsT=kh.t[c * 32:(c + 1) * 32, h * 64:(h + 1) * 64],
                                                                            rhs=vb.t[c * 32:(c + 1) * 32, h * 64:(h + 1) * 64], start=True, stop=True),
                                     reads=[kh.b, vb.b], writes=[ps[6].b])
                            else:
                                P.op('pe', lambda e, p0=p0, c=c, h=h: e.matmul(ps[6].t[p0:p0 + 64, 0:64], lhsT=khz.t[64:128, h * 64:(h + 1) * 64],
                                                                            rhs=vb.t[64:128, h * 64:(h + 1) * 64], start=True, stop=True),
                                     reads=[khz.b, vb.b], writes=[ps[6].b])
                        P.op('dve', lambda e, hp=hp, c=c: e.scalar_tensor_tensor(out=S.t[:, hp, :], in0=S.t[:, hp, :], scalar=ebl.t[:, hp, c:c + 1], in1=ps[6].t[:, 0:64],
                                                                              op0=ALU.mult, op1=ALU.add), reads=[S.b, ebl.b, ps[6].b], writes=[S.b])
                        P.op('act', lambda e, hp=hp: e.activation(out=Sb.t[:, hp, :], in_=S.t[:, hp, :], func=AF.Copy), reads=[S.b], writes=[Sb.b])
                of4 = of.t[:].rearrange("p (a h d) -> p a h d", a=4, h=2)
                if dr == 0:
                    for half in range(2):
                        P.op('act', lambda e, half=half: e.activation(out=of4[:, :, half, :], in_=ps[2 + half].t[:, 0:256].rearrange("p (a d) -> p a d", a=4), func=AF.Copy),
                             reads=[ps[2 + half].b], writes=[of.b])
                    P.dma('sp', lambda e, t=t: e.dma_start(out=self.OF[t * 128:(t + 1) * 128, :], in_=of.t[:]), reads=[of.b], writes=[self.b_OF[t]])
                else:
                    P.dma('sp', lambda e, t=t: e.dma_start(out=of.t[:], in_=self.OF[t * 128:(t + 1) * 128, :]), reads=[self.b_OF[t]], writes=[of.b])
                    for half in range(2):
                        P.op('dve', lambda e, half=half: e.tensor_tensor(out=of4[:, :, half, :], in0=of4[:, :, half, :], in1=ps[2 + half].t[:, 0:256].rearrange("p (a d) -> p a d", a=4), op=ALU.add),
                             reads=[ps[2 + half].b, of.b], writes=[of.b])
                    self.proj_tok(t, wB, 1536, 512, ps[4])
                    P.op('act', lambda e: e.activation(out=gw.t[:], in_=ps[4].t[:], func=AF.Silu), reads=[ps[4].b], writes=[gw.b])
                    P.op('pool', lambda e: e.tensor_tensor(out=gw.t[:].rearrange("p (h d) -> p h d", h=8), in0=gw.t[:].rearrange("p (h d) -> p h d", h=8),
                                                          in1=hnw.t[:].unsqueeze(1).to_broadcast([128, 8, 64]), op=ALU.mult), reads=[gw.b, hnw.b], writes=[gw.b])
                    P.op('pool', lambda e: e.tensor_tensor(out=sq.t[:], in0=of.t[:], in1=of.t[:], op=ALU.mult), reads=[of.b], writes=[sq.b])
                    P.op('dve', lambda e: e.tensor_reduce(out=s8.t[:], in_=sq.t[:].rearrange("p (h d) -> p h d", h=8), axis=AX.X, op=ALU.add), reads=[sq.b], writes=[s8.b])
                    P.op('dve', lambda e: e.tensor_scalar(out=s8.t[:], in0=s8.t[:], scalar1=1.0 / 64, scalar2=EPS, op0=ALU.mult, op1=ALU.add), reads=[s8.b], writes=[s8.b])
                    P.op('act', lambda e: e.activation(out=s8.t[:], in_=s8.t[:], func=AF.Sqrt), reads=[s8.b], writes=[s8.b])
                    P.op('dve', lambda e: e.reciprocal(out=s8.t[:], in_=s8.t[:]), reads=[s8.b], writes=[s8.b])
                    P.op('dve', lambda e: e.tensor_tensor(out=sq.t[:].rearrange("p (h d) -> p h d", h=8), in0=of.t[:].rearrange("p (h d) -> p h d", h=8),
                                                         in1=s8.t[:].unsqueeze(2).to_broadcast([128, 8, 64]), op=ALU.mult), reads=[of.b, s8.b], writes=[sq.b])
                    P.op('dve', lambda e, t=t: e.tensor_tensor(out=self.Osb[t].t[:, 512:1024], in0=sq.t[:], in1=gw.t[:], op=ALU.mult), reads=[sq.b, gw.b], writes=[self.Osb[t].b])

    def gdn(self, b, win):
        P = self.P
        I = self.I
        tl = self.tile
        ps = self.ps
        ident, ones, negones = self.ident, self.ones, self.negones
        gdc = tl([128, 2, 4, 128], F32, "gdc")
        P.dma('sp', lambda e: e.dma_start(out=gdc.t[:], in_=I['gdc']), writes=[gdc.b])
        gcs = tl([128, 2], F32, "gcs")
        P.dma('sp', lambda e: e.dma_start(out=gcs.t[:], in_=I['gcsel']), writes=[gcs.b])
        nexpA = tl([128, 8], F32, "nexpA")
        self.load_bcast(nexpA, I['gdn_a_log'][0].rearrange("a h -> (a h)"))
        P.op('act', lambda e: e.activation(out=nexpA.t[:], in_=nexpA.t[:], func=AF.Exp), reads=[nexpA.b], writes=[nexpA.b])
        P.op('dve', lambda e: e.tensor_scalar(out=nexpA.t[:], in0=nexpA.t[:], scalar1=-1.0, scalar2=None, op0=ALU.mult), reads=[nexpA.b], writes=[nexpA.b])
        dtb = tl([128, 8], F32, "dtb")
        self.load_bcast(dtb, I['gdn_dt_bias'][0].rearrange("a h -> (a h)"))
        gnw = tl([128, 128], F32, "gnw")
        self.load_bcast(gnw, I['gdn_norm_w'][0])
        cw = tl([128, 5, 12], F32, "cw")
        for j in range(5):
            P.dma('sp', lambda e, j=j: e.dma_start(out=cw.t[:, j, :], in_=I['gdn_conv_w'][0, j, :].rearrange("(cb p) -> p cb", p=128), allow_slow_non_contiguous=True), writes=[cw.b])
        qT = [tl([128, NT], BF16, "qTc") for _ in range(4)]
        kT = [tl([128, NT], BF16, "kTc") for _ in range(4)]
        vtok = tl([128, 18, 4, 128], BF16, "vtok")
        blocks = [(0, 256)] + [(256 + i * 512, 512) for i in range(4)]
        segs = [(0, 256), (256, NT)]
        with ExitStack() as sc:
            P.stack, saved = sc, P.stack
            P.barrier()
            raw = T(self.modbig.t[:, 0:NT])
            acc = T(self.modbig.t[:, NT:2 * NT])
            vfm = T(self.modbig.t[:, 2 * NT:6 * D].bitcast(BF16)[:, 0:NT])
            wblk = [tl([128, 8, 128], BF16, "wblk") for _ in range(2)]
            for cb in range(12):
                w = wblk[cb % 2]
                self.load_w(w, win, cb * 128, 128)
                for bi, (t0, n) in enumerate(blocks):
                    pa = ps[bi % 2]
                    self.proj_feat(w, 0, t0, n, pa)
                    P.op('act', lambda e, pa=pa, t0=t0, n=n: e.activation(out=raw.t[:, t0:t0 + n], in_=pa.t[:, 0:n], func=AF.Copy), reads=[pa.b], writes=[raw.b])
                P.op('dve', lambda e, cb=cb: e.tensor_scalar(out=acc.t[:], in0=raw.t[:], scalar1=cw.t[:, 2, cb:cb + 1], scalar2=None, op0=ALU.mult), reads=[raw.b, cw.b], writes=[acc.b])
                for j in (0, 1, 3, 4):
                    sh = j - 2
                    for (s0, s1) in segs:
                        lo, hi = max(s0, s0 - sh), min(s1, s1 - sh)
                        P.op('dve', lambda e, cb=cb, j=j, lo=lo, hi=hi, sh=sh: e.scalar_tensor_tensor(out=acc.t[:, lo:hi], in0=raw.t[:, lo + sh:hi + sh], scalar=cw.t[:, j, cb:cb + 1],
                                                                                                   in1=acc.t[:, lo:hi], op0=ALU.mult, op1=ALU.add), reads=[raw.b, cw.b, acc.b], writes=[acc.b])
                P.op('act', lambda e: e.activation(out=raw.t[:], in_=acc.t[:], func=AF.Silu), reads=[acc.b], writes=[raw.b])
                if cb < 8:
                    dest = qT[cb] if cb < 4 else kT[cb - 4]
                    scale = 128.0 ** -0.5 if cb < 4 else 1.0
                    for (t0, n) in blocks:
                        sq, rn = self.wf[0], self.wf[1]
                        P.op('pool', lambda e, t0=t0, n=n: e.tensor_tensor(out=sq.t[:, 0:n], in0=raw.t[:, t0:t0 + n], in1=raw.t[:, t0:t0 + n], op=ALU.mult), reads=[raw.b], writes=[sq.b])
                        P.op('pe', lambda e, n=n: e.matmul(ps[2].t[:, 0:n], lhsT=ones.t[:], rhs=sq.t[:, 0:n], start=True, stop=True), reads=[ones.b, sq.b], writes=[ps[2].b])
                        P.op('dve', lambda e, n=n: e.tensor_scalar(out=rn.t[:, 0:n], in0=ps[2].t[:, 0:n], scalar1=EPS, scalar2=None, op0=ALU.add), reads=[ps[2].b], writes=[rn.b])
                        P.op('act', lambda e, n=n: e.activation(out=rn.t[:, 0:n], in_=rn.t[:, 0:n], func=AF.Sqrt), reads=[rn.b], writes=[rn.b])
                        P.op('dve', lambda e, n=n: e.reciprocal(out=rn.t[:, 0:n], in_=rn.t[:, 0:n]), reads=[rn.b], writes=[rn.b])
                        P.op('dve', lambda e, t0=t0, n=n, dest=dest, scale=scale: e.scalar_tensor_tensor(out=dest.t[:, t0:t0 + n], in0=raw.t[:, t0:t0 + n], scalar=scale, in1=rn.t[:, 0:n],
                                                                                                     op0=ALU.mult, op1=ALU.mult), reads=[raw.b, rn.b], writes=[dest.b])
                else:
                    h = cb - 8
                    P.op('act', lambda e: e.activation(out=vfm.t[:], in_=raw.t[:], func=AF.Copy), reads=[raw.b], writes=[vfm.b])
                    for tg in range(0, 18, 8):
                        nt_ = min(8, 18 - tg)
                        for k in range(nt_):
                            P.op('pe', lambda e, k=k, tg=tg: e.transpose(out=self.psb.t[:, k * 128:(k + 1) * 128], in_=vfm.t[:, (tg + k) * 128:(tg + k + 1) * 128], identity=self.identb.t[:]),
                                 reads=[vfm.b, self.identb.b], writes=[self.psb.b])
                        P.op('act', lambda e, tg=tg, nt_=nt_, h=h: e.activation(out=vtok.t[:, tg:tg + nt_, h, :], in_=self.psb.t[:, 0:nt_ * 128].rearrange("p (a b) -> p a b", a=nt_), func=AF.Copy),
                             reads=[self.psb.b], writes=[vtok.b])
            P.barrier()
        P.stack = saved
        stop = self.cfg.get('gdn_stop', 99)
        if stop == 1:
            for t in range(2, 18):
                P.op('pool', lambda e, t=t: e.memset(self.Osb[t].t[:, 0:512], 0.0), writes=[self.Osb[t].b])
                P.op('pool', lambda e, t=t: e.tensor_copy(out=self.Osb[t].t[:, 0:128], in_=vtok.t[:, t, 0, :]), reads=[vtok.b], writes=[self.Osb[t].b])
            return
        wg = tl([128, 8, 16], BF16, "wg")
        self.load_w(wg, win, 2048, 16)
        graw = tl([128, 18, 16], F32, "graw")
        for t in range(18):
            pg = ps[t % 2]
            self.proj_tok(t, wg, 0, 16, pg)
            P.op('act', lambda e, t=t, pg=pg: e.activation(out=graw.t[:, t, :], in_=pg.t[:, 0:16], func=AF.Copy), reads=[pg.b], writes=[graw.b])
        la = tl([128, 18, 8], F32, "la")
        beta = tl([128, 18, 8], F32, "beta")
        nbeta = tl([128, 18, 8], F32, "nbeta")
        P.op('dve', lambda e: e.tensor_tensor(out=la.t[:], in0=graw.t[:, :, 0:8], in1=dtb.t[:].unsqueeze(1).to_broadcast([128, 18, 8]), op=ALU.add), reads=[graw.b, dtb.b], writes=[la.b])
        P.op('act', lambda e: e.activation(out=la.t[:], in_=la.t[:], func=AF.Exp), reads=[la.b], writes=[la.b])
        P.op('act', lambda e: e.activation(out=la.t[:], in_=la.t[:], func=AF.Ln, bias=1.0, scale=1.0), reads=[la.b], writes=[la.b])
        P.op('dve', lambda e: e.tensor_tensor(out=la.t[:], in0=la.t[:], in1=nexpA.t[:].unsqueeze(1).to_broadcast([128, 18, 8]), op=ALU.mult), reads=[la.b, nexpA.b], writes=[la.b])
        P.op('act', lambda e: e.activation(out=beta.t[:], in_=graw.t[:, :, 8:16], func=AF.Sigmoid), reads=[graw.b], writes=[beta.b])
        P.op('dve', lambda e: e.tensor_scalar(out=nbeta.t[:], in0=beta.t[:], scalar1=-1.0, scalar2=None, op0=ALU.mult), reads=[beta.b], writes=[nbeta.b])
        with ExitStack() as sc:
            P.stack, saved = sc, P.stack
            wz = tl([128, 8, 512], BF16, "wz")
            self.load_w(wz, win, 1536, 512)
            zst = [self.wf[0], self.wf[1]]
            for t in range(2, 18):
                pz = ps[2 + t % 2]
                self.proj_tok(t, wz, 0, 512, pz)
                zt = zst[t % 2]
                P.op('act', lambda e, pz=pz, zt=zt: e.activation(out=zt.t[:, 0:512], in_=pz.t[:], func=AF.Silu), reads=[pz.b], writes=[zt.b])
                P.dma('sp', lambda e, t=t, zt=zt: e.dma_start(out=self.ZS[t * 128:(t + 1) * 128, :], in_=zt.t[:, 0:512]), reads=[zt.b], writes=[self.b_ZS[t]])
            P.barrier()
        P.stack = saved
        if stop == 2:
            for t in range(2, 18):
                P.op('pool', lambda e, t=t: e.memset(self.Osb[t].t[:, 0:512], 0.0), writes=[self.Osb[t].b])
                P.op('pool', lambda e, t=t: e.tensor_copy(out=self.Osb[t].t[:, 0:8], in_=la.t[:, t, :]), reads=[la.b], writes=[self.Osb[t].b])
                P.op('pool', lambda e, t=t: e.tensor_copy(out=self.Osb[t].t[:, 8:16], in_=beta.t[:, t, :]), reads=[beta.b], writes=[self.Osb[t].b])
            return
        f32t = lambda nm: tl([128, 128], F32, nm)
        LaT, labc, egB, decS, decT = f32t("LaT"), f32t("labc"), f32t("egB"), f32t("decS"), f32t("decT")
        Mm = [f32t("Mm") for _ in range(2)]
        Mt = [f32t("Mt") for _ in range(2)]
        IM = f32t("IM")
        Xt = [f32t("Xt") for _ in range(2)]
        W0, V0, usb = f32t("W0"), f32t("V0"), f32t("usb")
        attT = tl([128, 128], BF16, "attT")
        khat = tl([128, 128], BF16, "khat")
        wT = tl([128, 128], BF16, "wT")
        qcT = tl([128, 128], BF16, "qcT")
        vn = tl([128, 128], BF16, "vn")
        eg8 = tl([128, 8], F32, "eg8")
        beg = tl([128, 4], F32, "beg")
        egl2 = tl([128, 2], F32, "egl2")
        S = [tl([128, 128], F32, "Sg") for _ in range(4)]
        Sb = [tl([128, 128], BF16, "Sgb") for _ in range(4)]
        of = T(self.wf[0].t[:, 0:512])
        gw = T(self.wf[1].t[:, 0:512])
        sq = T(self.xin[0].t[:, 0:512])
        s4 = tl([128, 4], F32, "s4")
        for dr in range(2):
            for h in range(4):
                P.op('dve', lambda e, h=h: e.memset(S[h].t[:], 0.0), writes=[S[h].b])
                P.op('dve', lambda e, h=h: e.memset(Sb[h].t[:], 0.0), writes=[Sb[h].b])
            order = [0, 1] + list(range(2, 18)) if dr == 0 else [1, 0] + list(range(17, 1, -1))
            order = order[:self.cfg.get('gdn_tiles', 18)]
            if dr >= self.cfg.get('gdn_dirs', 2):
                for t in order:
                    if t >= 2:
                        P.dma('sp', lambda e, t=t: e.dma_start(out=of.t[:], in_=self.OF[t * 128:(t + 1) * 128, :]), reads=[self.b_OF[t]], writes=[of.b])
                        P.op('dve', lambda e, t=t: e.tensor_copy(out=self.Osb[t].t[:, 0:512], in_=of.t[:]), reads=[of.b], writes=[self.Osb[t].b])
                break
            chunks = [0, 1] if dr == 0 else [1, 0]
            T_ = gdc.t[:, dr, 0, :]
            CmT = gdc.t[:, dr, 1, :]
            MS = gdc.t[:, dr, 2, :]
            MIT = gdc.t[:, dr, 3, :]
            for t in order:
                tok = slice(t * 128, (t + 1) * 128)
                lat = t >= 2
                P.op('pe', lambda e, t=t, dr=dr, T_=T_: e.matmul(ps[0].t[:, 0:4], lhsT=T_, rhs=la.t[:, t, dr * 4:dr * 4 + 4], start=True, stop=True), reads=[gdc.b, la.b], writes=[ps[0].b])
                P.op('pe', lambda e, t=t, dr=dr, CmT=CmT: e.matmul(ps[0].t[:, 4:8], lhsT=CmT, rhs=la.t[:, t, dr * 4:dr * 4 + 4], start=True, stop=True), reads=[gdc.b, la.b], writes=[ps[0].b])
                P.op('act', lambda e: e.activation(out=eg8.t[:], in_=ps[0].t[:, 0:8], func=AF.Exp), reads=[ps[0].b], writes=[eg8.b])
                P.op('dve', lambda e, t=t, dr=dr: e.tensor_tensor(out=beg.t[:], in0=eg8.t[:, 0:4], in1=beta.t[:, t, dr * 4:dr * 4 + 4], op=ALU.mult), reads=[eg8.b, beta.b], writes=[beg.b])
                if dr == 1 and lat:
                    P.dma('sp', lambda e, t=t: e.dma_start(out=of.t[:], in_=self.OF[t * 128:(t + 1) * 128, :]), reads=[self.b_OF[t]], writes=[of.b])
                if dr == 0 and self.cfg.get('gdn_cut', 99) < 99:
                    P.op('dve', lambda e: e.memset(of.t[:], 0.0), writes=[of.b])
                for h in range(4):
                    col = dr * 4 + h
                    P.op('dve', lambda e, t=t, col=col, T_=T_: e.tensor_scalar(out=LaT.t[:], in0=T_, scalar1=la.t[:, t, col:col + 1], scalar2=None, op0=ALU.mult), reads=[gdc.b, la.b], writes=[LaT.b])
                    P.op('pool', lambda e, t=t, col=col: e.tensor_scalar(out=labc.t[:], in0=ones.t[:], scalar1=la.t[:, t, col:col + 1], scalar2=None, op0=ALU.mult), reads=[ones.b, la.b], writes=[labc.b])
                    P.op('pe', lambda e, T_=T_: e.matmul(ps[1].t[:, 0:128], lhsT=labc.t[:], rhs=T_, start=True, stop=True), reads=[labc.b, gdc.b], writes=[ps[1].b])
                    P.op('pe', lambda e: e.matmul(ps[1].t[:, 128:130], lhsT=labc.t[:], rhs=gcs.t[:], start=True, stop=True), reads=[labc.b, gcs.b], writes=[ps[1].b])
                    P.op('act', lambda e: e.activation(out=egB.t[:], in_=ps[1].t[:, 0:128], func=AF.Exp), reads=[ps[1].b], writes=[egB.b])
                    P.op('act', lambda e: e.activation(out=egl2.t[:], in_=ps[1].t[:, 128:130], func=AF.Exp), reads=[ps[1].b], writes=[egl2.b])
                    if self.cfg.get('gdn_cut', 99) <= 1:
                        continue

                    P.op('pe', lambda e: e.matmul(ps[2].t[:, 0:128], lhsT=LaT.t[:], rhs=ones.t[:], start=True, stop=False), reads=[LaT.b, ones.b], writes=[ps[2].b])
                    P.op('pe', lambda e: e.matmul(ps[2].t[:, 0:128], lhsT=negones.t[:], rhs=LaT.t[:], start=False, stop=False), reads=[LaT.b, negones.b], writes=[ps[2].b])
                    P.op('pe', lambda e, MS=MS: e.matmul(ps[2].t[:, 0:128], lhsT=ident.t[:], rhs=MS, start=False, stop=True), reads=[ident.b, gdc.b], writes=[ps[2].b])
                    P.op('act', lambda e: e.activation(out=decS.t[:], in_=ps[2].t[:, 0:128], func=AF.Exp), reads=[ps[2].b], writes=[decS.b])
                    P.op('pe', lambda e: e.matmul(ps[3].t[:, 0:128], lhsT=ones.t[:], rhs=LaT.t[:], start=True, stop=False), reads=[LaT.b, ones.b], writes=[ps[3].b])
                    P.op('pe', lambda e: e.matmul(ps[3].t[:, 0:128], lhsT=LaT.t[:], rhs=negones.t[:], start=False, stop=False), reads=[LaT.b, negones.b], writes=[ps[3].b])
                    P.op('pe', lambda e, MIT=MIT: e.matmul(ps[3].t[:, 0:128], lhsT=ident.t[:], rhs=MIT, start=False, stop=True), reads=[ident.b, gdc.b], writes=[ps[3].b])
                    P.op('act', lambda e: e.activation(out=decT.t[:], in_=ps[3].t[:, 0:128], func=AF.Exp), reads=[ps[3].b], writes=[decT.b])
                    if self.cfg.get('gdn_cut', 99) <= 2:
                        continue

                    P.op('pe', lambda e, h=h, tok=tok: e.matmul(ps[4].t[:, 0:128], lhsT=kT[h].t[:, tok], rhs=kT[h].t[:, tok], start=True, stop=True), reads=[kT[h].b], writes=[ps[4].b])
                    P.op('pe', lambda e, h=h, tok=tok: e.matmul(ps[4].t[:, 128:256], lhsT=kT[h].t[:, tok], rhs=qT[h].t[:, tok], start=True, stop=True), reads=[kT[h].b, qT[h].b], writes=[ps[4].b])
                    P.op('dve', lambda e, t=t, col=col: e.scalar_tensor_tensor(out=Mm[0].t[:], in0=ps[4].t[:, 0:128], scalar=nbeta.t[:, t, col:col + 1], in1=decS.t[:], op0=ALU.mult, op1=ALU.mult),
                         reads=[ps[4].b, nbeta.b, decS.b], writes=[Mm[0].b])
                    P.op('dve', lambda e: e.tensor_tensor(out=attT.t[:], in0=ps[4].t[:, 128:256], in1=decT.t[:], op=ALU.mult), reads=[ps[4].b, decT.b], writes=[attT.b])
                    if self.cfg.get('gdn_cut', 99) <= 3:
                        continue

                    P.op('pe', lambda e: e.transpose(out=ps[5].t[:, 0:128], in_=Mm[0].t[:], identity=ident.t[:]), reads=[Mm[0].b, ident.b], writes=[ps[5].b])
                    P.op('act', lambda e: e.activation(out=Mt[0].t[:], in_=ps[5].t[:, 0:128], func=AF.Copy), reads=[ps[5].b], writes=[Mt[0].b])
                    P.op('dve', lambda e: e.tensor_tensor(out=Xt[0].t[:], in0=ps[5].t[:, 0:128], in1=ident.t[:], op=ALU.add), reads=[ps[5].b, ident.b], writes=[Xt[0].b])
                    cur = 0
                    for p in range(1, 6):
                        nxt = 1 - cur
                        P.op('pe', lambda e, cur=cur: e.matmul(ps[5].t[:, 0:128], lhsT=Mt[cur].t[:], rhs=Mm[cur].t[:], start=True, stop=True), reads=[Mt[cur].b, Mm[cur].b], writes=[ps[5].b])
                        if p < 5:
                            P.op('pe', lambda e, cur=cur: e.matmul(ps[5].t[:, 128:256], lhsT=Mm[cur].t[:], rhs=Mt[cur].t[:], start=True, stop=True), reads=[Mt[cur].b, Mm[cur].b], writes=[ps[5].b])
                            P.op('act', lambda e, nxt=nxt: e.activation(out=Mm[nxt].t[:], in_=ps[5].t[:, 0:128], func=AF.Copy), reads=[ps[5].b], writes=[Mm[nxt].b])
                            P.op('act', lambda e, nxt=nxt: e.activation(out=Mt[nxt].t[:], in_=ps[5].t[:, 128:256], func=AF.Copy), reads=[ps[5].b], writes=[Mt[nxt].b])
                        P.op('dve', lambda e: e.tensor_tensor(out=IM.t[:], in0=ps[5].t[:, 0:128], in1=ident.t[:], op=ALU.add), reads=[ps[5].b, ident.b], writes=[IM.b])
                        P.op('pe', lambda e, cur=cur: e.matmul(ps[6].t[:, 0:128], lhsT=IM.t[:], rhs=Xt[cur].t[:], start=True, stop=True), reads=[IM.b, Xt[cur].b], writes=[ps[6].b])
                        P.op('act', lambda e, nxt=nxt: e.activation(out=Xt[nxt].t[:], in_=ps[6].t[:, 0:128], func=AF.Copy), reads=[ps[6].b], writes=[Xt[nxt].b])
                        cur = nxt
                    X = Xt[cur]
                    if self.cfg.get('gdn_cut', 99) <= 4:
                        continue

                    P.op('pe', lambda e, h=h, tok=tok: e.transpose(out=self.psb.t[:, 0:128], in_=kT[h].t[:, tok], identity=self.identb.t[:]), reads=[kT[h].b, self.identb.b], writes=[self.psb.b])
                    P.op('dve', lambda e, h=h: e.tensor_scalar(out=W0.t[:], in0=self.psb.t[:, 0:128], scalar1=beg.t[:, h:h + 1], scalar2=None, op0=ALU.mult), reads=[self.psb.b, beg.b], writes=[W0.b])
                    P.op('dve', lambda e, h=h: e.tensor_scalar(out=khat.t[:], in0=self.psb.t[:, 0:128], scalar1=eg8.t[:, 4 + h:5 + h], scalar2=None, op0=ALU.mult), reads=[self.psb.b, eg8.b], writes=[khat.b])
                    P.op('pool', lambda e, t=t, h=h, col=col: e.tensor_scalar(out=V0.t[:], in0=vtok.t[:, t, h, :], scalar1=beta.t[:, t, col:col + 1], scalar2=None, op0=ALU.mult), reads=[vtok.b, beta.b], writes=[V0.b])
                    P.op('pe', lambda e, X=X: e.matmul(ps[0].t[:, 128:256], lhsT=X.t[:], rhs=V0.t[:], start=True, stop=True), reads=[X.b, V0.b], writes=[ps[0].b])
                    P.op('act', lambda e: e.activation(out=usb.t[:], in_=ps[0].t[:, 128:256], func=AF.Copy), reads=[ps[0].b], writes=[usb.b])
                    P.op('pe', lambda e, X=X: e.matmul(ps[1].t[:, 256:384], lhsT=W0.t[:], rhs=X.t[:], start=True, stop=True), reads=[X.b, W0.b], writes=[ps[1].b])
                    P.op('act', lambda e: e.activation(out=wT.t[:], in_=ps[1].t[:, 256:384], func=AF.Copy), reads=[ps[1].b], writes=[wT.b])
                    P.op('dve', lambda e, h=h, tok=tok: e.tensor_tensor(out=qcT.t[:], in0=qT[h].t[:, tok], in1=egB.t[:], op=ALU.mult), reads=[qT[h].b, egB.b], writes=[qcT.b])
                    if self.cfg.get('gdn_cut', 99) <= 5:
                        continue

                    for c in chunks:
                        r = slice(c * 64, (c + 1) * 64)
                        P.op('pe', lambda e, r=r, h=h: e.matmul(ps[2].t[r, 128:256], lhsT=wT.t[:, r], rhs=Sb[h].t[:], start=True, stop=True), reads=[wT.b, Sb[h].b], writes=[ps[2].b])
                        P.op('dve', lambda e, r=r: e.tensor_tensor(out=vn.t[r, :], in0=usb.t[r, :], in1=ps[2].t[r, 128:256], op=ALU.subtract), reads=[usb.b, ps[2].b], writes=[vn.b])
                        if lat:
                            P.op('pe', lambda e, r=r, h=h: e.matmul(ps[3].t[r, 128:256], lhsT=qcT.t[:, r], rhs=Sb[h].t[:], start=True, stop=False), reads=[qcT.b, Sb[h].b], writes=[ps[3].b])
                            P.op('pe', lambda e, r=r: e.matmul(ps[3].t[r, 128:256], lhsT=attT.t[r, r], rhs=vn.t[r, :], start=False, stop=True), reads=[attT.b, vn.b], writes=[ps[3].b])
                        P.op('pe', lambda e, r=r: e.matmul(ps[4].t[:, 256:384], lhsT=khat.t[r, :], rhs=vn.t[r, :], start=True, stop=True), reads=[khat.b, vn.b], writes=[ps[4].b])
                        P.op('dve', lambda e, h=h, c=c: e.scalar_tensor_tensor(out=S[h].t[:], in0=S[h].t[:], scalar=egl2.t[:, c:c + 1], in1=ps[4].t[:, 256:384], op0=ALU.mult, op1=ALU.add),
                             reads=[S[h].b, egl2.b, ps[4].b], writes=[S[h].b])
                        P.op('act', lambda e, h=h: e.activation(out=Sb[h].t[:], in_=S[h].t[:], func=AF.Copy), reads=[S[h].b], writes=[Sb[h].b])
                        if lat:
                            if dr == 0:
                                P.op('act', lambda e, r=r, h=h: e.activation(out=of.t[r, h * 128:(h + 1) * 128], in_=ps[3].t[r, 128:256], func=AF.Copy), reads=[ps[3].b], writes=[of.b])
                            else:
                                P.op('dve', lambda e, r=r, h=h: e.tensor_tensor(out=of.t[r, h * 128:(h + 1) * 128], in0=of.t[r, h * 128:(h + 1) * 128], in1=ps[3].t[r, 128:256], op=ALU.add),
                                     reads=[ps[3].b, of.b], writes=[of.b])
                if lat and dr == 0:
                    P.dma('sp', lambda e, t=t: e.dma_start(out=self.OF[t * 128:(t + 1) * 128, :], in_=of.t[:]), reads=[of.b], writes=[self.b_OF[t]])
                if lat and dr == 1:
                    P.dma('sp', lambda e, t=t: e.dma_start(out=gw.t[:], in_=self.ZS[t * 128:(t + 1) * 128, :]), reads=[self.b_ZS[t]], writes=[gw.b])
                    P.op('pool', lambda e: e.tensor_tensor(out=gw.t[:].rearrange("p (h d) -> p h d", h=4), in0=gw.t[:].rearrange("p (h d) -> p h d", h=4),
                                                          in1=gnw.t[:].unsqueeze(1).to_broadcast([128, 4, 128]), op=ALU.mult), reads=[gw.b, gnw.b], writes=[gw.b])
                    P.op('pool', lambda e: e.tensor_tensor(out=sq.t[:], in0=of.t[:], in1=of.t[:], op=ALU.mult), reads=[of.b], writes=[sq.b])
                    P.op('dve', lambda e: e.tensor_reduce(out=s4.t[:], in_=sq.t[:].rearrange("p (h d) -> p h d", h=4), axis=AX.X, op=ALU.add), reads=[sq.b], writes=[s4.b])
                    P.op('dve', lambda e: e.tensor_scalar(out=s4.t[:], in0=s4.t[:], scalar1=1.0 / 128, scalar2=EPS, op0=ALU.mult, op1=ALU.add), reads=[s4.b], writes=[s4.b])
                    P.op('act', lambda e: e.activation(out=s4.t[:], in_=s4.t[:], func=AF.Sqrt), reads=[s4.b], writes=[s4.b])
                    P.op('dve', lambda e: e.reciprocal(out=s4.t[:], in_=s4.t[:]), reads=[s4.b], writes=[s4.b])
                    P.op('dve', lambda e: e.tensor_tensor(out=sq.t[:].rearrange("p (h d) -> p h d", h=4), in0=of.t[:].rearrange("p (h d) -> p h d", h=4),
                                                         in1=s4.t[:].unsqueeze(2).to_broadcast([128, 4, 128]), op=ALU.mult), reads=[of.b, s4.b], writes=[sq.b])
                    P.op('dve', lambda e, t=t: e.tensor_tensor(out=self.Osb[t].t[:, 0:512], in0=sq.t[:], in1=gw.t[:], op=ALU.mult), reads=[sq.b, gw.b], writes=[self.Osb[t].b])


class PeerCtx:
    def __init__(self, K):
        tl = K.tile
        self.qT = tl([128, 16, 128], BF16)
        self.s = tl([128, 16, 128], F32)
        self.wk = tl([128, 256], F32)
        self.m = tl([128, 16, 16], F32)
        self.iu = tl([128, 16, 16], U32)
        self.if_ = tl([128, 16, 16], F32)
        self.cand = tl([128, 8, 256], F32)
        self.cidx = tl([128, 8, 256], F32)
        self.best = tl([128, 8, 16], F32)
        self.nmax = tl([128, 8], F32)
        self.e = tl([128, 8, 16], F32)
        self.z = tl([128, 8], F32)
        self.gate = tl([128, 128], F32)
        self.idxf = tl([128, 128], F32)
        self.idxi = tl([128, 128], I32)
        self.junk = tl([128, 1024], F32)
        self.junk2 = tl([128, 256], F32)
        self.apre = tl([128, 128], F32)
        self.ga = tl([128, 128], F32)
        self.NG = 8
        self.gb = [tl([128, 1024], BF16, "gb") for _ in range(self.NG)]
        self.gnext = 0
        self.dg = [tl([128, 128], BF16, "dg") for _ in range(4)]


def peer_tile(K, C, l, h2f, h2T, wq, u_tab, v_tab, acc):
    P = K.P
    keysT = K.keysT
    ps = K.ps
    for g in range(4):
        pt = ps[g]
        for j in range(4):
            hp = g * 4 + j
            for k in range(8):
                P.op('pe', lambda e, pt=pt, j=j, hp=hp, k=k: e.matmul(
                    pt.t[:, j * 128:(j + 1) * 128], lhsT=wq.t[:, k, hp * 128:(hp + 1) * 128], rhs=h2T.t[:, k, :],
                    start=(k == 0), stop=(k == 7)), reads=[wq.b, h2T.b], writes=[pt.b])
        P.op('act', lambda e, pt=pt, g=g: e.activation(out=C.qT.t[:, g * 4:(g + 1) * 4, :], in_=pt.t[:].rearrange("p (a b) -> p a b", a=4), func=AF.Copy),
             reads=[pt.b], writes=[C.qT.b])
    for g in range(4):
        pt = ps[g]
        for j in range(4):
            hp = g * 4 + j
            P.op('pe', lambda e, pt=pt, j=j, hp=hp: e.matmul(
                pt.t[:, j * 128:(j + 1) * 128], lhsT=C.qT.t[:, hp, :], rhs=keysT.t[:, l, hp, :], start=True, stop=True),
                reads=[C.qT.b, keysT.b], writes=[pt.b])
        P.op('act', lambda e, pt=pt, g=g: e.activation(out=C.s.t[:, g * 4:(g + 1) * 4, :], in_=pt.t[:].rearrange("p (a b) -> p a b", a=4), func=AF.Copy),
             reads=[pt.b], writes=[C.s.b])
    for hp in range(16):
        P.op('dve', lambda e, hp=hp: e.max(out=C.m.t[:, hp, 0:8], in_=C.s.t[:, hp, :]), reads=[C.s.b], writes=[C.m.b])
        P.op('dve', lambda e, hp=hp: e.max_index(out=C.iu.t[:, hp, 0:8], in_max=C.m.t[:, hp, 0:8], in_values=C.s.t[:, hp, :]),
             reads=[C.s.b, C.m.b], writes=[C.iu.b])
        P.op('dve', lambda e, hp=hp: e.match_replace(out=C.wk.t[:, 0:128], in_to_replace=C.m.t[:, hp, 0:8], in_values=C.s.t[:, hp, :], imm_value=-1e30),
             reads=[C.s.b, C.m.b], writes=[C.wk.b])
        P.op('dve', lambda e, hp=hp: e.max(out=C.m.t[:, hp, 8:16], in_=C.wk.t[:, 0:128]), reads=[C.wk.b], writes=[C.m.b])
        P.op('dve', lambda e, hp=hp: e.max_index(out=C.iu.t[:, hp, 8:16], in_max=C.m.t[:, hp, 8:16], in_values=C.wk.t[:, 0:128]),
             reads=[C.wk.b, C.m.b], writes=[C.iu.b])
    P.op('dve', lambda e: e.tensor_copy(out=C.if_.t[:], in_=C.iu.t[:]), reads=[C.iu.b], writes=[C.if_.b])
    m4 = C.m.t[:].rearrange("p (h t) k -> p h t k", t=2)
    if4 = C.if_.t[:].rearrange("p (h t) k -> p h t k", t=2)
    P.op('dve', lambda e: e.tensor_scalar(out=if4[:, :, 0, :], in0=if4[:, :, 0, :], scalar1=128.0, scalar2=None, op0=ALU.mult),
         reads=[C.if_.b], writes=[C.if_.b])
    cand4 = C.cand.t[:].rearrange("p h (a b) -> p h a b", a=16)
    cidx4 = C.cidx.t[:].rearrange("p h (a b) -> p h a b", a=16)
    for h in range(8):
        P.op('dve', lambda e, h=h: e.tensor_tensor(out=cand4[:, h], in0=m4[:, h, 0, :].unsqueeze(2).to_broadcast([128, 16, 16]),
                                                  in1=m4[:, h, 1, :].unsqueeze(1).to_broadcast([128, 16, 16]), op=ALU.add),
             reads=[C.m.b], writes=[C.cand.b])
        P.op('dve', lambda e, h=h: e.tensor_tensor(out=cidx4[:, h], in0=if4[:, h, 0, :].unsqueeze(2).to_broadcast([128, 16, 16]),
                                                  in1=if4[:, h, 1, :].unsqueeze(1).to_broadcast([128, 16, 16]), op=ALU.add),
             reads=[C.if_.b], writes=[C.cidx.b])
    for h in range(8):
        P.op('dve', lambda e, h=h: e.max(out=C.best.t[:, h, 0:8], in_=C.cand.t[:, h, :]), reads=[C.cand.b], writes=[C.best.b])
        P.op('dve', lambda e, h=h: e.match_replace(out=C.wk.t[:], in_to_replace=C.best.t[:, h, 0:8], in_values=C.cand.t[:, h, :], imm_value=-1e30),
             reads=[C.cand.b, C.best.b], writes=[C.wk.b])
        P.op('dve', lambda e, h=h: e.max(out=C.best.t[:, h, 8:16], in_=C.wk.t[:]), reads=[C.wk.b], writes=[C.best.b])
    for h in range(8):
        for k in range(16):
            sl = h * 16 + k
            P.op('dve', lambda e, h=h, k=k, sl=sl: e.scalar_tensor_tensor(
                out=C.junk2.t[:], in0=C.cand.t[:, h, :], scalar=C.best.t[:, h, k:k + 1], in1=C.cidx.t[:, h, :],
                op0=ALU.is_equal, op1=ALU.mult, accum_out=C.idxf.t[:, sl:sl + 1]),
                reads=[C.cand.b, C.best.b, C.cidx.b], writes=[C.junk2.b, C.idxf.b])
    P.op('dve', lambda e: e.tensor_scalar(out=C.idxf.t[:], in0=C.idxf.t[:], scalar1=float(NEXP - 1), scalar2=0.0, op0=ALU.min, op1=ALU.max),
         reads=[C.idxf.b], writes=[C.idxf.b])
    if l > 0:
        P.op('dve', lambda e: e.tensor_scalar(out=C.idxf.t[:], in0=C.idxf.t[:], scalar1=float(l * NEXP), scalar2=None, op0=ALU.add),
             reads=[C.idxf.b], writes=[C.idxf.b])
    P.op('dve', lambda e: e.tensor_copy(out=C.idxi.t[:], in_=C.idxf.t[:]), reads=[C.idxf.b], writes=[C.idxi.b])
    P.op('dve', lambda e: e.tensor_scalar(out=C.nmax.t[:], in0=C.best.t[:, :, 0], scalar1=-1.0, scalar2=None, op0=ALU.mult),
         reads=[C.best.b], writes=[C.nmax.b])
    for h in range(8):
        P.op('act', lambda e, h=h: e.activation(out=C.e.t[:, h, :], in_=C.best.t[:, h, :], func=AF.Exp, bias=C.nmax.t[:, h:h + 1], scale=1.0,
                                                accum_out=C.z.t[:, h:h + 1]),
             reads=[C.best.b, C.nmax.b], writes=[C.e.b, C.z.b])
    P.op('dve', lambda e: e.reciprocal(out=C.z.t[:], in_=C.z.t[:]), reads=[C.z.b], writes=[C.z.b])
    P.op('dve', lambda e: e.tensor_tensor(out=C.gate.t[:].rearrange("p (h k) -> p h k", h=8), in0=C.e.t[:],
                                          in1=C.z.t[:].unsqueeze(2).to_broadcast([128, 8, 16]), op=ALU.mult),
         reads=[C.e.b, C.z.b], writes=[C.gate.b])
    for sl in range(128):
        gb = C.gb[C.gnext]
        C.gnext = (C.gnext + 1) % C.NG
        P.dma('pool', lambda e, gb=gb, sl=sl: e.indirect_dma_start(
            out=gb.t[:], out_offset=None, in_=u_tab,
            in_offset=bass.IndirectOffsetOnAxis(ap=C.idxi.t[:, sl:sl + 1], axis=0)),
            reads=[C.idxi.b], writes=[gb.b])
        P.op('dve', lambda e, gb=gb, sl=sl: e.scalar_tensor_tensor(
            out=C.junk.t[:], in0=h2f.t[:], scalar=1.0, in1=gb.t[:], op0=ALU.mult, op1=ALU.mult, accum_out=C.apre.t[:, sl:sl + 1]),
            reads=[h2f.b, gb.b], writes=[C.junk.b, C.apre.b])
    P.op('act', lambda e: e.activation(out=C.ga.t[:], in_=C.apre.t[:], func=AF.Gelu), reads=[C.apre.b], writes=[C.ga.b])
    P.op('dve', lambda e: e.tensor_tensor(out=C.ga.t[:], in0=C.ga.t[:], in1=C.gate.t[:], op=ALU.mult), reads=[C.ga.b, C.gate.b], writes=[C.ga.b])
    for sl in range(128):
        gb = C.gb[C.gnext]
        C.gnext = (C.gnext + 1) % C.NG
        P.dma('pool', lambda e, gb=gb, sl=sl: e.indirect_dma_start(
            out=gb.t[:], out_offset=None, in_=v_tab,
            in_offset=bass.IndirectOffsetOnAxis(ap=C.idxi.t[:, sl:sl + 1], axis=0)),
            reads=[C.idxi.b], writes=[gb.b])
        dg = C.dg[sl % 4]
        P.op('act', lambda e, dg=dg, sl=sl: e.activation(out=dg.t[:], in_=K.identb.t[:], func=AF.Copy, scale=C.ga.t[:, sl:sl + 1]),
             reads=[K.identb.b, C.ga.b], writes=[dg.b])
        for half in range(2):
            pa = ps[4 + half]
            P.op('pe', lambda e, pa=pa, dg=dg, gb=gb, half=half, sl=sl: e.matmul(pa.t[:], lhsT=dg.t[:], rhs=gb.t[:, half * 512:(half + 1) * 512],
                                                                                 start=(sl == 0), stop=(sl == 127)), reads=[dg.b, gb.b], writes=[pa.b])
    for half in range(2):
        pa = ps[4 + half]
        P.op('act', lambda e, pa=pa, half=half: e.activation(out=acc.t[:, half * 512:(half + 1) * 512], in_=pa.t[:], func=AF.Copy), reads=[pa.b], writes=[acc.b])


_CACHE = {}


def make_in_maps(inputs, n_cores=8):
    consts = make_consts()
    maps = []
    for i in range(n_cores):
        m = {}
        sl = slice(i * NB, (i + 1) * NB)
        m['x'] = np.ascontiguousarray(inputs['x'][sl])
        m['ctx'] = np.ascontiguousarray(inputs['ctx'][sl])
        c5 = np.concatenate([inputs['c'][sl], inputs['c_ctx'][None, :]], axis=0)
        m['c5T'] = np.ascontiguousarray(c5.T.reshape(8, 128, 5).transpose(1, 0, 2))
        for k in IN_SHAPES:
            if k in ('x', 'ctx', 'c5T'):
                continue
            m[k] = np.ascontiguousarray(inputs[k], dtype=np.float32)
        m.update(consts)
        maps.append(m)
    return maps


def kernel(**inputs):
    inputs = {k: np.asarray(v) for k, v in inputs.items()}
    if 'nc' not in _CACHE:
        _CACHE['nc'] = Kern({}).build()
    nc = _CACHE['nc']
    maps = make_in_maps(inputs)
    res = run_bass_kernel_spmd(nc, maps, core_ids=list(range(8)))
    return np.concatenate([r['out'] for r in res.results], axis=0).astype(np.float32)
```
